# Optimizing a Trainium2 kernel written in Bass

```python
import math
import jax, jax.numpy as jnp
from jax import lax
import numpy as np

D_MODEL = 1024
BATCH = 1
SEQ = 16384
DEPTH = 2
DEC_BATCH = 32
DEC_SEQ = 8
PAST_LEN = 16384
PAGE_SIZE = 128

D_BR = D_MODEL // 4
HEAD_DIM = 64
POOL_WINDOWS = (2, 4, 8, 16)
POOL_GW = D_BR // len(POOL_WINDOWS)
POOL_BUF = max(POOL_WINDOWS) - 1
RWKV_HEADS = D_BR // HEAD_DIM
RWKV_W_LORA = 64
RWKV_A_LORA = 64
RWKV_G_LORA = 128
RWKV_COLS = 3 * D_BR + RWKV_W_LORA + RWKV_A_LORA + RWKV_G_LORA
RWKV_LN_EPS = 64e-5
SB_HEADS = D_BR // HEAD_DIM
SB_BLOCK = 128
S5_GW = 16
S5_GROUPS = D_BR // S5_GW
S5_STATE = 64
N_BRANCH = 4
OFF_POOL = 0
OFF_RWKV = OFF_POOL + D_BR
OFF_SB = OFF_RWKV + RWKV_COLS
OFF_S5 = OFF_SB + 3 * D_BR
OFF_GATE = OFF_S5 + D_BR
N_IN = OFF_GATE + N_BRANCH * D_MODEL
XA_HEADS = 4
N_MEM = 256
D_FF = -(-8 * D_MODEL // (3 * 256)) * 256
RMS_EPS = 1e-6

kernel_name = 'hybrid_pool_rwkv7_stickbreak_s5_decode_step'


def rms_norm(x, g):
    xf = x.astype(jnp.float32)
    y = xf * lax.rsqrt(jnp.mean(xf * xf, axis=-1, keepdims=True) + RMS_EPS)
    return (y * g.astype(jnp.float32)).astype(x.dtype)


def pool_mixer(u, buf, pos0, w_pool, scale):
    b, t, _ = u.shape
    ext = jnp.concatenate([buf.astype(u.dtype), u], axis=1)
    cs = jnp.cumsum(ext.astype(jnp.float32), axis=1)
    cs = jnp.concatenate([jnp.zeros_like(cs[:, :1]), cs], axis=1)
    pos = pos0 + jnp.arange(t)
    uf = u.astype(jnp.float32)
    groups = []
    for gi, w in enumerate(POOL_WINDOWS):
        c0, c1 = gi * POOL_GW, (gi + 1) * POOL_GW
        win_sum = (cs[:, POOL_BUF + 1:POOL_BUF + 1 + t, c0:c1]
                   - cs[:, POOL_BUF + 1 - w:POOL_BUF + 1 - w + t, c0:c1])
        count = jnp.minimum(pos + 1, w).astype(jnp.float32)[None, :, None]
        groups.append(win_sum / count - uf[..., c0:c1])
    pooled = jnp.stack(groups, axis=2)
    mixed = jnp.einsum('btgc,gcd->btgd', pooled, w_pool.astype(jnp.float32)).reshape(b, t, D_BR)
    return mixed * scale.astype(jnp.float32), ext[:, -POOL_BUF:]


def rwkv_mixer(p, shift, wkv, lw):
    f32 = jnp.float32
    b, t, _ = p.shape
    pf = p.astype(f32)
    prev = jnp.concatenate([shift[:, None, :].astype(f32), pf[:, :-1]], axis=1)
    pm = pf + (prev - pf) * lw['rwkv_mu'].astype(f32)
    o1, o2, o3 = D_BR, 2 * D_BR, 3 * D_BR
    o4 = o3 + RWKV_W_LORA
    o5 = o4 + RWKV_A_LORA
    r, k, v = pm[..., :o1], pm[..., o1:o2], pm[..., o2:o3]
    w_lo, a_lo, g_lo = pm[..., o3:o4], pm[..., o4:o5], pm[..., o5:]
    w_log = -jax.nn.softplus(-(lw['rwkv_w0'] + jnp.tanh(w_lo) @ lw['rwkv_w_up'])) - 0.5
    decay = jnp.exp(-jnp.exp(w_log))
    a = jax.nn.sigmoid(lw['rwkv_a0'] + a_lo @ lw['rwkv_a_up'])
    g = jax.nn.sigmoid(g_lo) @ lw['rwkv_g_up']
    heads = lambda z: z.reshape(b, t, RWKV_HEADS, HEAD_DIM)
    kk = heads(k * lw['rwkv_k_k'])
    kk = kk * lax.rsqrt(jnp.maximum(jnp.sum(kk * kk, -1, keepdims=True), 1e-24))
    k = k * (1.0 + (a - 1.0) * lw['rwkv_k_a'])
    r, k, v, decay, a = heads(r), heads(k), heads(v), heads(decay), heads(a)

    def step(state, inp):
        r_t, k_t, v_t, d_t, kk_t, a_t = inp
        sa = jnp.einsum('bhvk,bhk->bhv', state, -kk_t)
        state = (state * d_t[:, :, None, :]
                 + sa[..., None] * (kk_t * a_t)[:, :, None, :]
                 + v_t[..., None] * k_t[:, :, None, :])
        return state, jnp.einsum('bhvk,bhk->bhv', state, r_t)

    xs = tuple(jnp.swapaxes(z, 0, 1) for z in (r, k, v, decay, kk, a))
    wkv_new, o = lax.scan(step, wkv.astype(f32), xs)
    o = jnp.swapaxes(o, 0, 1)
    o_mean = jnp.mean(o, -1, keepdims=True)
    o_var = jnp.mean(jnp.square(o - o_mean), -1, keepdims=True)
    o = ((o - o_mean) * lax.rsqrt(o_var + RWKV_LN_EPS)).reshape(b, t, D_BR) * lw['rwkv_ln_w'] + lw['rwkv_ln_b']
    bonus = jnp.sum(r * k * lw['rwkv_r_k'], -1, keepdims=True) * v
    o = (o + bonus.reshape(b, t, D_BR)) * g
    return o, p[:, -1], wkv_new


def stick_breaking(q, k, v, q_pos0, bias):
    f32 = jnp.float32
    b, tq, h, hd = q.shape
    blk = SB_BLOCK if tq % SB_BLOCK == 0 else tq
    nb = tq // blk
    kf, vf = k.astype(f32), v.astype(f32)
    key_pos = jnp.arange(k.shape[1])
    scale = hd ** -0.5
    bias_f = bias.astype(f32)[None, :, None, None]

    def one_block(args):
        qb, b_idx = args
        q_pos = q_pos0 + b_idx * blk + jnp.arange(blk)
        z = jnp.einsum('bqhd,bkhd->bhqk', qb.astype(f32), kf) * scale + bias_f
        mask = key_pos[None, :] < q_pos[:, None]
        log_1m = jnp.where(mask, jax.nn.log_sigmoid(-z), 0.0)
        after = lax.cumsum(log_1m, axis=3, reverse=True) - log_1m
        att = jnp.where(mask, jnp.exp(jax.nn.log_sigmoid(z) + after), 0.0)
        return jnp.einsum('bhqk,bkhd->bqhd', att, vf)

    qblocks = jnp.moveaxis(q.reshape(b, nb, blk, h, hd), 1, 0)
    out = lax.map(one_block, (qblocks, jnp.arange(nb)))
    return jnp.moveaxis(out, 0, 1).reshape(b, tq, h, hd).astype(q.dtype)


def complex_affine_combine(e1, e2):
    a1r, a1i, b1r, b1i = e1
    a2r, a2i, b2r, b2i = e2
    return (a1r * a2r - a1i * a2i, a1r * a2i + a1i * a2r,
            a2r * b1r - a2i * b1i + b2r, a2r * b1i + a2i * b1r + b2i)


def s5_mixer(u, h_re, h_im, lw):
    f32 = jnp.float32
    b, t, _ = u.shape
    uf = u.astype(f32)
    ug = uf.reshape(b, t, S5_GROUPS, S5_GW)
    a_re = lw['s5_a_re'].astype(f32)
    a_im = lw['s5_a_im'].astype(f32)
    dt_g = jnp.exp(lw['s5_log_dt'].astype(f32))[:, None]
    mag = jnp.exp(dt_g * a_re)
    ab_re, ab_im = mag * jnp.cos(dt_g * a_im), mag * jnp.sin(dt_g * a_im)
    den = a_re * a_re + a_im * a_im
    n_re = ab_re - 1.0
    f_re = (n_re * a_re + ab_im * a_im) / den
    f_im = (ab_im * a_re - n_re * a_im) / den
    b_re, b_im = lw['s5_b_re'], lw['s5_b_im']
    bb_re = f_re[..., None] * b_re - f_im[..., None] * b_im
    bb_im = f_re[..., None] * b_im + f_im[..., None] * b_re
    bu_re = jnp.einsum('gpc,btgc->btgp', bb_re, ug)
    bu_im = jnp.einsum('gpc,btgc->btgp', bb_im, ug)
    h_re, h_im = h_re.astype(f32), h_im.astype(f32)
    bu_re = bu_re.at[:, 0].add(ab_re * h_re - ab_im * h_im)
    bu_im = bu_im.at[:, 0].add(ab_re * h_im + ab_im * h_re)
    a_seq_re = jnp.broadcast_to(ab_re, bu_re.shape)
    a_seq_im = jnp.broadcast_to(ab_im, bu_im.shape)
    _, _, s_re, s_im = lax.associative_scan(complex_affine_combine, (a_seq_re, a_seq_im, bu_re, bu_im), axis=1)
    y = (jnp.einsum('gcp,btgp->btgc', lw['s5_c_re'], s_re)
         - jnp.einsum('gcp,btgp->btgc', lw['s5_c_im'], s_im))
    y = y.reshape(b, t, D_BR) + lw['s5_d'] * uf
    z = jax.nn.gelu(y) @ lw['s5_w_glu']
    return z[..., :D_BR] * jax.nn.sigmoid(z[..., D_BR:]), s_re[:, -1], s_im[:, -1]


def memory_kv(mem, g_mem, w_k, w_v, k_gain):
    b, m, _ = mem.shape
    mn = rms_norm(mem, g_mem)
    k = rms_norm((mn @ w_k).reshape(b, m, XA_HEADS, HEAD_DIM), k_gain)
    v = (mn @ w_v).reshape(b, m, XA_HEADS, HEAD_DIM)
    return k, v


def cross_attention(h, mem_k, mem_v, w_q, q_gain, w_o):
    b, t, _ = h.shape
    q = rms_norm((h @ w_q).reshape(b, t, XA_HEADS, HEAD_DIM), q_gain)
    s = jnp.einsum('bthd,bmhd->bhtm', q.astype(jnp.float32), mem_k.astype(jnp.float32)) * HEAD_DIM ** -0.5
    pr = jax.nn.softmax(s, axis=-1)
    o = jnp.einsum('bhtm,bmhd->bthd', pr, mem_v.astype(jnp.float32)).reshape(b, t, XA_HEADS * HEAD_DIM)
    return o.astype(h.dtype) @ w_o


def swiglu(h, w_gate, w_up, w_down):
    return (jax.nn.silu(h @ w_gate) * (h @ w_up)) @ w_down


def trunk_layer(x, pos0, mem_k, mem_v, past_k, past_v, pool_buf, shift, wkv, s5_re, s5_im, lw):
    b, t, _ = x.shape
    h = rms_norm(x, lw['norm_mix'])
    proj = h @ lw['w_in']
    o_pool, new_pool = pool_mixer(proj[..., OFF_POOL:OFF_RWKV], pool_buf, pos0, lw['pool_w'], lw['pool_scale'])
    o_rwkv, new_shift, new_wkv = rwkv_mixer(proj[..., OFF_RWKV:OFF_SB], shift, wkv, lw)
    q, k, v = jnp.split(proj[..., OFF_SB:OFF_S5], 3, axis=-1)
    heads = lambda z: z.reshape(b, t, SB_HEADS, HEAD_DIM)
    q = rms_norm(heads(q), lw['sb_q_norm'])
    k = rms_norm(heads(k), lw['sb_k_norm'])
    v = heads(v)
    k_all = jnp.concatenate([past_k.astype(k.dtype), k], axis=1)
    v_all = jnp.concatenate([past_v.astype(v.dtype), v], axis=1)
    o_sb = stick_breaking(q, k_all, v_all, pos0, lw['sb_bias']).reshape(b, t, D_BR)
    o_s5, new_s5_re, new_s5_im = s5_mixer(proj[..., OFF_S5:OFF_GATE], s5_re, s5_im, lw)
    gates = jax.nn.sigmoid(proj[..., OFF_GATE:].astype(jnp.float32)).reshape(b, t, N_BRANCH, D_MODEL)
    branches = jnp.stack([o_pool, o_rwkv, o_sb, o_s5], axis=2).astype(x.dtype)
    lifted = jnp.einsum('btnc,ncd->btnd', branches, lw['w_branch'])
    merged = jnp.sum(gates * lifted.astype(jnp.float32), axis=2).astype(x.dtype)
    x = x + merged @ lw['w_out']
    x = x + cross_attention(rms_norm(x, lw['norm_cross']), mem_k, mem_v, lw['xa_w_q'], lw['xa_q_norm'], lw['xa_w_o'])
    x = x + swiglu(rms_norm(x, lw['norm_ffn']), lw['ffn_w_gate'], lw['ffn_w_up'], lw['ffn_w_down'])
    return x, (k, v, new_pool, new_shift, new_wkv, new_s5_re, new_s5_im)


def setup_inputs(seed: int = 0) -> dict:
    key = jax.random.key(seed)
    ks = iter(jax.random.split(key, 80))
    f32 = jnp.float32
    L = DEPTH

    def nrm(shape, scale=1.0):
        return jax.random.normal(next(ks), shape, f32) * scale

    def gain(shape):
        return 1.0 + 0.02 * jax.random.normal(next(ks), shape, f32)

    def unif(shape, lo, hi):
        return jax.random.uniform(next(ks), shape, f32, lo, hi)

    n_pages = PAST_LEN // PAGE_SIZE
    n_used = DEC_BATCH * n_pages
    n_pool = n_used + max(1, n_used // 4)
    page_table = jax.random.permutation(next(ks), n_pool)[:n_used].reshape(DEC_BATCH, n_pages).astype(jnp.int32)
    a_im_init = jnp.broadcast_to(jnp.pi * jnp.arange(S5_STATE, dtype=f32), (L, S5_GROUPS, S5_STATE))
    return {
        'x_prompt': nrm((BATCH, SEQ, D_MODEL)),
        'x_sample': nrm((DEC_BATCH, DEC_SEQ, D_MODEL)),
        'cache_sb_k': nrm((L, n_pool, PAGE_SIZE, SB_HEADS, HEAD_DIM)),
        'cache_sb_v': nrm((L, n_pool, PAGE_SIZE, SB_HEADS, HEAD_DIM)),
        'cache_mem_k': nrm((L, DEC_BATCH, N_MEM, XA_HEADS, HEAD_DIM)),
        'cache_mem_v': nrm((L, DEC_BATCH, N_MEM, XA_HEADS, HEAD_DIM)),
        'state_pool': nrm((L, DEC_BATCH, POOL_BUF, D_BR)),
        'state_rwkv_shift': nrm((L, DEC_BATCH, RWKV_COLS)),
        'state_rwkv_wkv': nrm((L, DEC_BATCH, RWKV_HEADS, HEAD_DIM, HEAD_DIM), 0.3),
        'state_s5_re': nrm((L, DEC_BATCH, S5_GROUPS, S5_STATE), 0.1),
        'state_s5_im': nrm((L, DEC_BATCH, S5_GROUPS, S5_STATE), 0.1),
        'page_table': page_table,
        'mem_prompt': nrm((BATCH, N_MEM, D_MODEL)),
        'norm_mix': gain((L, D_MODEL)),
        'norm_cross': gain((L, D_MODEL)),
        'norm_mem': gain((L, D_MODEL)),
        'norm_ffn': gain((L, D_MODEL)),
        'w_in': nrm((L, D_MODEL, N_IN), D_MODEL ** -0.5),
        'pool_w': nrm((L, len(POOL_WINDOWS), POOL_GW, POOL_GW), POOL_GW ** -0.5),
        'pool_scale': gain((L, D_BR)),
        'rwkv_mu': unif((L, RWKV_COLS), 0.0, 1.0),
        'rwkv_w0': unif((L, D_BR), -6.0, 0.0),
        'rwkv_w_up': nrm((L, RWKV_W_LORA, D_BR), 0.1 * RWKV_W_LORA ** -0.5),
        'rwkv_a0': nrm((L, D_BR), 0.1),
        'rwkv_a_up': nrm((L, RWKV_A_LORA, D_BR), RWKV_A_LORA ** -0.5),
        'rwkv_g_up': nrm((L, RWKV_G_LORA, D_BR), RWKV_G_LORA ** -0.5),
        'rwkv_k_k': 0.85 + nrm((L, D_BR), 0.02),
        'rwkv_k_a': gain((L, D_BR)),
        'rwkv_r_k': nrm((L, RWKV_HEADS, HEAD_DIM), 0.1),
        'rwkv_ln_w': gain((L, D_BR)),
        'rwkv_ln_b': nrm((L, D_BR), 0.02),
        'sb_q_norm': gain((L, HEAD_DIM)),
        'sb_k_norm': gain((L, HEAD_DIM)),
        'sb_bias': unif((L, SB_HEADS), -9.0, -7.0),
        's5_a_re': -0.5 + nrm((L, S5_GROUPS, S5_STATE), 0.01),
        's5_a_im': a_im_init + nrm((L, S5_GROUPS, S5_STATE), 0.01),
        's5_log_dt': unif((L, S5_GROUPS), math.log(1e-3), math.log(1e-1)),
        's5_b_re': nrm((L, S5_GROUPS, S5_STATE, S5_GW), (2 * S5_GW) ** -0.5),
        's5_b_im': nrm((L, S5_GROUPS, S5_STATE, S5_GW), (2 * S5_GW) ** -0.5),
        's5_c_re': nrm((L, S5_GROUPS, S5_GW, S5_STATE), S5_STATE ** -0.5),
        's5_c_im': nrm((L, S5_GROUPS, S5_GW, S5_STATE), S5_STATE ** -0.5),
        's5_d': nrm((L, D_BR)),
        's5_w_glu': nrm((L, D_BR, 2 * D_BR), D_BR ** -0.5),
        'w_branch': nrm((L, N_BRANCH, D_BR, D_MODEL), D_BR ** -0.5),
        'w_out': nrm((L, D_MODEL, D_MODEL), D_MODEL ** -0.5),
        'xa_w_q': nrm((L, D_MODEL, XA_HEADS * HEAD_DIM), D_MODEL ** -0.5),
        'xa_w_k': nrm((L, D_MODEL, XA_HEADS * HEAD_DIM), D_MODEL ** -0.5),
        'xa_w_v': nrm((L, D_MODEL, XA_HEADS * HEAD_DIM), D_MODEL ** -0.5),
        'xa_q_norm': gain((L, HEAD_DIM)),
        'xa_k_norm': gain((L, HEAD_DIM)),
        'xa_w_o': nrm((L, XA_HEADS * HEAD_DIM, D_MODEL), (XA_HEADS * HEAD_DIM) ** -0.5),
        'ffn_w_gate': nrm((L, D_MODEL, D_FF), D_MODEL ** -0.5),
        'ffn_w_up': nrm((L, D_MODEL, D_FF), D_MODEL ** -0.5),
        'ffn_w_down': nrm((L, D_FF, D_MODEL), D_FF ** -0.5),
    }


def reference(x_prompt, x_sample, cache_sb_k, cache_sb_v, cache_mem_k, cache_mem_v,
              state_pool, state_rwkv_shift, state_rwkv_wkv, state_s5_re, state_s5_im,
              page_table, mem_prompt,
              norm_mix, norm_cross, norm_mem, norm_ffn, w_in, pool_w, pool_scale,
              rwkv_mu, rwkv_w0, rwkv_w_up, rwkv_a0, rwkv_a_up, rwkv_g_up, rwkv_k_k, rwkv_k_a,
              rwkv_r_k, rwkv_ln_w, rwkv_ln_b, sb_q_norm, sb_k_norm, sb_bias,
              s5_a_re, s5_a_im, s5_log_dt, s5_b_re, s5_b_im, s5_c_re, s5_c_im, s5_d, s5_w_glu,
              w_branch, w_out, xa_w_q, xa_w_k, xa_w_v, xa_q_norm, xa_k_norm, xa_w_o,
              ffn_w_gate, ffn_w_up, ffn_w_down):
    bp = x_prompt.shape[0]
    bs = x_sample.shape[0]
    past_len = page_table.shape[1] * PAGE_SIZE
    dt = x_prompt.dtype
    xp, xs = x_prompt, x_sample
    names_p = ('k', 'v', 'mk', 'mv', 'pool', 'shift', 'wkv', 's5r', 's5i')
    names_s = ('k', 'v', 'pool', 'shift', 'wkv', 's5r', 's5i')
    new_p = {n: [] for n in names_p}
    new_s = {n: [] for n in names_s}
    for l in range(DEPTH):
        lw = {
            'norm_mix': norm_mix[l], 'norm_cross': norm_cross[l], 'norm_ffn': norm_ffn[l],
            'w_in': w_in[l], 'pool_w': pool_w[l], 'pool_scale': pool_scale[l],
            'rwkv_mu': rwkv_mu[l], 'rwkv_w0': rwkv_w0[l], 'rwkv_w_up': rwkv_w_up[l],
            'rwkv_a0': rwkv_a0[l], 'rwkv_a_up': rwkv_a_up[l], 'rwkv_g_up': rwkv_g_up[l],
            'rwkv_k_k': rwkv_k_k[l], 'rwkv_k_a': rwkv_k_a[l], 'rwkv_r_k': rwkv_r_k[l],
            'rwkv_ln_w': rwkv_ln_w[l], 'rwkv_ln_b': rwkv_ln_b[l],
            'sb_q_norm': sb_q_norm[l], 'sb_k_norm': sb_k_norm[l], 'sb_bias': sb_bias[l],
            's5_a_re': s5_a_re[l], 's5_a_im': s5_a_im[l], 's5_log_dt': s5_log_dt[l],
            's5_b_re': s5_b_re[l], 's5_b_im': s5_b_im[l], 's5_c_re': s5_c_re[l], 's5_c_im': s5_c_im[l],
            's5_d': s5_d[l], 's5_w_glu': s5_w_glu[l],
            'w_branch': w_branch[l], 'w_out': w_out[l],
            'xa_w_q': xa_w_q[l], 'xa_q_norm': xa_q_norm[l], 'xa_w_o': xa_w_o[l],
            'ffn_w_gate': ffn_w_gate[l], 'ffn_w_up': ffn_w_up[l], 'ffn_w_down': ffn_w_down[l],
        }
        mk_p, mv_p = memory_kv(mem_prompt, norm_mem[l], xa_w_k[l], xa_w_v[l], xa_k_norm[l])
        empty = jnp.zeros((bp, 0, SB_HEADS, HEAD_DIM), dt)
        xp, st_p = trunk_layer(
            xp, 0, mk_p, mv_p, empty, empty,
            jnp.zeros((bp, POOL_BUF, D_BR), dt),
            jnp.zeros((bp, RWKV_COLS), dt),
            jnp.zeros((bp, RWKV_HEADS, HEAD_DIM, HEAD_DIM), jnp.float32),
            jnp.zeros((bp, S5_GROUPS, S5_STATE), jnp.float32),
            jnp.zeros((bp, S5_GROUPS, S5_STATE), jnp.float32), lw)
        for n, val in zip(('k', 'v', 'pool', 'shift', 'wkv', 's5r', 's5i'), st_p):
            new_p[n].append(val)
        new_p['mk'].append(mk_p)
        new_p['mv'].append(mv_p)
        past_k = cache_sb_k[l][page_table].reshape(bs, past_len, SB_HEADS, HEAD_DIM)
        past_v = cache_sb_v[l][page_table].reshape(bs, past_len, SB_HEADS, HEAD_DIM)
        xs, st_s = trunk_layer(
            xs, past_len, cache_mem_k[l], cache_mem_v[l], past_k, past_v,
            state_pool[l], state_rwkv_shift[l], state_rwkv_wkv[l],
            state_s5_re[l], state_s5_im[l], lw)
        for n, val in zip(names_s, st_s):
            new_s[n].append(val)
    sb_k_prompt = jnp.stack(new_p['k'], 0)
    sb_v_prompt = jnp.stack(new_p['v'], 0)
    mem_k_prompt = jnp.stack(new_p['mk'], 0)
    mem_v_prompt = jnp.stack(new_p['mv'], 0)
    pool_prompt = jnp.stack(new_p['pool'], 0)
    shift_prompt = jnp.stack(new_p['shift'], 0)
    wkv_prompt = jnp.stack(new_p['wkv'], 0)
    s5_re_prompt = jnp.stack(new_p['s5r'], 0)
    s5_im_prompt = jnp.stack(new_p['s5i'], 0)
    sb_k_sample = jnp.stack(new_s['k'], 0)
    sb_v_sample = jnp.stack(new_s['v'], 0)
    pool_sample = jnp.stack(new_s['pool'], 0)
    shift_sample = jnp.stack(new_s['shift'], 0)
    wkv_sample = jnp.stack(new_s['wkv'], 0)
    s5_re_sample = jnp.stack(new_s['s5r'], 0)
    s5_im_sample = jnp.stack(new_s['s5i'], 0)
    return (xp, xs,
            sb_k_prompt, sb_v_prompt, mem_k_prompt, mem_v_prompt,
            pool_prompt, shift_prompt, wkv_prompt, s5_re_prompt, s5_im_prompt,
            sb_k_sample, sb_v_sample, pool_sample, shift_sample, wkv_sample,
            s5_re_sample, s5_im_sample)
```

```python
import os
import numpy as np
from contextlib import ExitStack
import concourse.bass as bass
import concourse.mybir as mybir
from concourse.bass_utils import run_bass_kernel_spmd

F32 = mybir.dt.float32
BF16 = mybir.dt.bfloat16
I32 = mybir.dt.int32
AF = mybir.ActivationFunctionType
ALU = mybir.AluOpType
AX = mybir.AxisListType


class View:
    __slots__ = ("buf", "ap")

    def __init__(self, buf, ap):
        self.buf = buf
        self.ap = ap

    def __getitem__(self, k):
        return View(self.buf, self.ap[k])

    def rearrange(self, s, **kw):
        return View(self.buf, self.ap.rearrange(s, **kw))

    def pbc(self, n):
        return View(self.buf, self.ap.partition_broadcast(n))

    def bc(self, shape):
        return View(self.buf, self.ap.to_broadcast(shape))


class Buf:
    __slots__ = ("name", "t", "lw", "rd")

    def __init__(self, name, t=None):
        self.name = name
        self.t = t
        self.lw = None
        self.rd = []

    def __getitem__(self, k):
        return View(self, self.t[k])


class Op:
    __slots__ = ("eng", "fn", "deps", "dma", "sem", "val", "need", "waits")


def _ap(x):
    return x.ap if isinstance(x, View) else x


class Prog:
    ENG = ("pe", "act", "dve", "pool", "sp")
    R = 8

    def __init__(self, nc, es):
        self.nc = nc
        self.es = es
        self.ges = es
        self.ops = {e: [] for e in self.ENG}
        self.ndma = {e: 0 for e in self.ENG}
        self.dma_last = {}
        self.csem = {e: es.enter_context(nc.semaphore("c_" + e)) for e in ("pe", "act", "dve", "pool")}
        self.dsem = {e: [es.enter_context(nc.semaphore("d_%s%d" % (e, i))) for i in range(self.R)]
                     for e in ("sp", "act", "pool")}
        self.nbuf = 0
        self.last = {e: None for e in self.ENG}

    def sb(self, shape, dt=F32, name=None):
        self.nbuf += 1
        name = (name or "sb") + "_%d" % self.nbuf
        t = self.es.enter_context(self.nc.sbuf_tensor(name, list(shape), dt))
        return Buf(name, t)

    def ps(self, shape, dt=F32, name=None):
        self.nbuf += 1
        name = (name or "ps") + "_%d" % self.nbuf
        t = self.es.enter_context(self.nc.psum_tensor(name, list(shape), dt))
        return Buf(name, t)

    def dram(self, name, shape, dt=F32, kind="Internal"):
        t = self.nc.dram_tensor(name, list(shape), dt, kind=kind)
        return Buf(name, t.ap())

    def op(self, eng, fn, reads=(), writes=(), dma=False, extra=()):
        o = Op()
        o.eng, o.fn, o.dma = eng, fn, dma
        o.need = dma
        o.sem = None
        o.val = 0
        deps = [(d, True) for d in extra]
        for b in reads:
            if b.lw is not None:
                deps.append((b.lw, True))
        for b in writes:
            if b.lw is not None:
                deps.append((b.lw, False))
            for r in b.rd:
                deps.append((r, False))
        if dma:
            n = self.ndma[eng]
            self.ndma[eng] = n + 1
            key = (eng, n % self.R)
            prev = self.dma_last.get(key)
            if prev is not None:
                deps.append((prev, True))
            self.dma_last[key] = o
            o.sem = self.dsem[eng][n % self.R]
            o.val = 16 * (n // self.R + 1)
        o.deps = []
        for (d, raw) in deps:
            if d is o:
                continue
            if (not d.dma) and d.eng == eng:
                if eng == "pe" or not raw:
                    continue
            d.need = True
            o.deps.append(d)
        for b in reads:
            b.rd.append(o)
        for b in writes:
            b.lw = o
            b.rd = []
        self.ops[eng].append(o)
        if not dma:
            self.last[eng] = o
        return o

    def _rw(self, outs, ins):
        return [x.buf for x in ins if isinstance(x, View)], [x.buf for x in outs if isinstance(x, View)]

    def dma(self, eng, out, in_, **kw):
        kw.setdefault("allow_slow_non_contiguous", True)
        r, w = self._rw([out], [in_])
        return self.op(eng, lambda e: e.dma_start(out=out.ap, in_=in_.ap, **kw), r, w, dma=True)

    def mm(self, out, lhsT, rhs, start=True, stop=True):
        r, w = self._rw([out], [lhsT, rhs])
        return self.op("pe", lambda e: e.matmul(out.ap, lhsT=lhsT.ap, rhs=rhs.ap, start=start, stop=stop), r, w)

    def tr(self, out, in_, ident):
        r, w = self._rw([out], [in_, ident])
        return self.op("pe", lambda e: e.transpose(out.ap, in_.ap, ident.ap), r, w)

    def act(self, out, in_, func, bias=0.0, scale=1.0, accum=None):
        r, w = self._rw([out, accum], [in_, bias, scale])
        kw = {}
        if accum is not None:
            kw["accum_out"] = accum.ap
        return self.op("act", lambda e: e.activation(out=out.ap, in_=in_.ap, func=func, bias=_ap(bias), scale=_ap(scale), **kw), r, w)

    def tt(self, eng, out, a, b, op):
        r, w = self._rw([out], [a, b])
        return self.op(eng, lambda e: e.tensor_tensor(out=out.ap, in0=a.ap, in1=b.ap, op=op), r, w)

    def ts(self, eng, out, a, s1, op0, s2=None, op1=None):
        r, w = self._rw([out], [a, s1, s2])
        if op1 is None:
            return self.op(eng, lambda e: e.tensor_scalar(out=out.ap, in0=a.ap, scalar1=_ap(s1), scalar2=None, op0=op0), r, w)
        return self.op(eng, lambda e: e.tensor_scalar(out=out.ap, in0=a.ap, scalar1=_ap(s1), scalar2=_ap(s2), op0=op0, op1=op1), r, w)

    def stt(self, eng, out, a, s, b, op0, op1):
        r, w = self._rw([out], [a, s, b])
        return self.op(eng, lambda e: e.scalar_tensor_tensor(out=out.ap, in0=a.ap, scalar=_ap(s), in1=b.ap, op0=op0, op1=op1), r, w)

    def cp(self, eng, out, in_):
        r, w = self._rw([out], [in_])
        if eng == "act":
            return self.op("act", lambda e: e.copy(out=out.ap, in_=in_.ap), r, w)
        return self.op(eng, lambda e: e.tensor_copy(out=out.ap, in_=in_.ap), r, w)

    def memset(self, eng, out, val):
        r, w = self._rw([out], [])
        return self.op(eng, lambda e: e.memset(out.ap, val), r, w)

    def recip(self, out, in_):
        r, w = self._rw([out], [in_])
        return self.op("dve", lambda e: e.reciprocal(out=out.ap, in_=in_.ap), r, w)

    def scan(self, out, d0, d1, init):
        r, w = self._rw([out], [d0, d1, init])
        return self.op("dve", lambda e: e.tensor_tensor_scan(out=out.ap, data0=d0.ap, data1=d1.ap, initial=_ap(init), op0=ALU.mult, op1=ALU.add), r, w)

    def barrier(self, tiles):
        prev = [o for o in self.last.values() if o is not None] + list(self.dma_last.values())
        a = []
        a.append(self.op("pool", lambda e: e.memset(tiles[0].t[0:1, 0:1], 0.0), [], [tiles[0]], extra=prev))
        for i, eng in enumerate(("dve", "act", "pe", "sp")):
            b = tiles[i + 1]
            if eng == "dve":
                o = self.op(eng, lambda e, b=b: e.memset(b.t[0:1, 0:1], 0.0), [tiles[0]], [b])
            elif eng == "act":
                o = self.op(eng, lambda e, b=b: e.copy(out=b.t[0:1, 0:1], in_=tiles[0].t[0:1, 0:1]), [tiles[0]], [b])
            elif eng == "pe":
                o = self.op(eng, lambda e, b=b: e.matmul(tiles[5].t[0:1, 0:1], lhsT=tiles[0].t[0:1, 0:1], rhs=tiles[0].t[0:1, 0:1], start=True, stop=True), [tiles[0]], [tiles[5]])
            else:
                o = self.op(eng, lambda e, b=b: e.dma_start(out=b.t[0:1, 0:1], in_=tiles[0].t[0:1, 0:1]), [tiles[0]], [b], dma=True)
            a.append(o)

    def emit(self):
        nc = self.nc
        for e in ("pe", "act", "dve", "pool"):
            c = 0
            for o in self.ops[e]:
                if not o.dma and o.need:
                    c += 1
                    o.sem = self.csem[e]
                    o.val = c
            assert c < 2 ** 30, (e, c)
        finals = [(o.sem, o.val) for o in self.dma_last.values()]
        for e in self.ENG:
            seen = {}
            for o in self.ops[e]:
                w = {}
                for d in o.deps:
                    k = d.sem
                    if seen.get(k.num, 0) >= d.val:
                        continue
                    if k.num not in w or w[k.num][1] < d.val:
                        w[k.num] = (k, d.val)
                for kn, (k, v) in w.items():
                    seen[kn] = v
                o.waits = list(w.values())
        with nc.Block() as block:
            def body(ename):
                def f(eng):
                    for o in self.ops[ename]:
                        for (s, v) in o.waits:
                            eng.wait_ge(s, v)
                        ins = o.fn(eng)
                        if o.dma:
                            ins.then_inc(o.sem, 16)
                        elif o.need:
                            ins.then_inc(o.sem, 1)
                    if ename == "sp":
                        for (s, v) in finals:
                            eng.wait_ge(s, v)
                return f
            block.tensor(body("pe"))
            block.scalar(body("act"))
            block.vector(body("dve"))
            block.gpsimd(body("pool"))
            block.sync(body("sp"))
        return {e: len(self.ops[e]) for e in self.ENG}

L = 2
DM = 1024
NCORE = 8
SPC = 4
ST = 8
STOK = SPC * ST
N_IN = 6400
OFF_POOL, OFF_RWKV, OFF_SB, OFF_S5, OFF_GATE = 0, 256, 1280, 2048, 2304
DFF = 2816
EPS = 1e-6
PIECE = 512
SUB = 256
CH = 64
WINS = (2, 4, 8, 16)
DECAY_C = -0.6065306597126334
TWO_PI = 6.283185307179586


class Ring:
    def __init__(self, P, shape, dt, n, name):
        self.b = [P.sb(shape, dt, "%s%d" % (name, i)) for i in range(n)]
        self.i = 0

    def next(self):
        b = self.b[self.i % len(self.b)]
        self.i += 1
        return b


def host_consts():
    f = np.float32
    c = {}
    c["c_ident"] = np.eye(128, dtype=f)
    o64 = np.zeros((128, 128), f); o64[:64, :64] = 1; o64[64:, 64:] = 1
    c["c_ones64"] = o64
    i = np.arange(128)
    c["c_tri"] = (i[:, None] > i[None, :]).astype(f)
    j = np.arange(64)
    su = (j[:, None] < j[None, :]).astype(f); iu = (j[:, None] <= j[None, :]).astype(f)
    c["c_msu"] = np.ascontiguousarray(np.broadcast_to(su[:, None, :], (64, 8, 64)))
    c["c_miu"] = np.ascontiguousarray(np.broadcast_to(iu[:, None, :], (64, 8, 64)))
    c["c_msl"] = np.ascontiguousarray(np.broadcast_to(su.T[:, None, :], (64, 8, 64)))
    c["c_id8"] = np.ascontiguousarray(np.broadcast_to(np.eye(64, dtype=f)[:, None, :], (64, 8, 64)))
    p = np.arange(128)[:, None, None]; d = np.arange(4)[None, :, None]; q = np.arange(512)[None, None, :]
    c["c_sbmask"] = ((128 * d + p) < q).astype(f)
    c["c_iota"] = np.ascontiguousarray(np.broadcast_to(np.arange(1, SUB + 1, dtype=f)[None, :], (128, SUB)))
    row = np.arange(64)
    selw = np.zeros((64, 4), f); rc = np.zeros((64, 4, 16), f)
    for g, w in enumerate(WINS):
        sel = (row // 16 == g).astype(f)
        selw[:, g] = sel / w
        rc[:, g, :] = sel[:, None] / np.minimum(np.arange(16) + 1, w)[None, :]
    c["c_selw"] = selw; c["c_rc16"] = rc
    s = np.arange(8)[:, None]; hq = np.arange(32)[None, :]
    c["c_newmask"] = (s < (hq % 8)).astype(f)
    return c

CONST_SHAPES = {"c_ident": [128, 128], "c_ones64": [128, 128], "c_tri": [128, 128], "c_msu": [64, 8, 64],
                "c_miu": [64, 8, 64], "c_msl": [64, 8, 64], "c_id8": [64, 8, 64], "c_sbmask": [128, 4, 512],
                "c_iota": [128, SUB], "c_selw": [64, 4], "c_rc16": [64, 4, 16], "c_newmask": [8, 32]}

WEIGHT_SHAPES = {
    "norm_mix": [L, DM], "norm_cross": [L, DM], "norm_mem": [L, DM], "norm_ffn": [L, DM],
    "w_in": [L, DM, N_IN], "pool_w": [L, 4, 64, 64], "pool_scale": [L, 256], "rwkv_mu": [L, 1024],
    "rwkv_w0": [L, 256], "rwkv_w_up": [L, 64, 256], "rwkv_a0": [L, 256], "rwkv_a_up": [L, 64, 256],
    "rwkv_g_up": [L, 128, 256], "rwkv_k_k": [L, 256], "rwkv_k_a": [L, 256], "rwkv_r_k": [L, 256],
    "rwkv_ln_w": [L, 256], "rwkv_ln_b": [L, 256], "sb_q_norm": [L, 64], "sb_k_norm": [L, 64], "sb_bias": [L, 4],
    "s5_a_re": [L, 1024], "s5_a_im": [L, 1024], "s5_log_dt": [L, 16], "s5_b_re": [L, 1024, 16], "s5_b_im": [L, 1024, 16],
    "s5_c_re": [L, 16, 16, 64], "s5_c_im": [L, 16, 16, 64], "s5_d": [L, 256], "s5_w_glu": [L, 256, 512],
    "w_branch": [L, 4, 256, DM], "w_out": [L, DM, DM], "xa_w_q": [L, DM, 256], "xa_w_k": [L, DM, 256],
    "xa_w_v": [L, DM, 256], "xa_q_norm": [L, 64], "xa_k_norm": [L, 64], "xa_w_o": [L, 256, DM],
    "ffn_w_gate": [L, DM, DFF], "ffn_w_up": [L, DM, DFF], "ffn_w_down": [L, DFF, DM],
}


def build_program(SEQ, NPAGE, NPOOL, STAGES):
    NPC = SEQ // PIECE
    nc = bass.Bass("TRN2", target_bir_lowering=False)
    ges = ExitStack()
    with ges:
        P = Prog(nc, ges)
        D = {}
        DBGC = [None, False]

        def IN(n, s, dt=F32):
            D[n] = P.dram(n, s, dt, kind="ExternalInput")
            return D[n]

        def OUT(n, s, dt=F32):
            D[n] = P.dram(n, s, dt, kind="ExternalOutput")
            return D[n]

        xp = IN("xp", [SEQ, DM]); xs = IN("xs", [STOK, DM])
        for n, s in CONST_SHAPES.items():
            IN(n, s)
        for n, s in WEIGHT_SHAPES.items():
            IN(n, s)
        IN("mem_prompt", [256, DM])
        IN("st_pool", [L, SPC, 15, 256]); IN("st_shift", [L, SPC, 1024]); IN("st_wkv", [L, SPC, 4, 64, 64])
        IN("st_s5r", [L, SPC, 1024]); IN("st_s5i", [L, SPC, 1024])
        IN("cmk", [L, SPC, 256, 256]); IN("cmv", [L, SPC, 256, 256])
        IN("ptab", [SPC, NPAGE], I32)
        IN("ck", [L * NPOOL * 8, 4096]); IN("cv", [L * NPOOL * 8, 4096])
        o_y = OUT("o_y", [SEQ, DM]); o_ys = OUT("o_ys", [STOK, DM])
        o_sbk = OUT("o_sbk", [L, SEQ, 256]); o_sbv = OUT("o_sbv", [L, SEQ, 256])
        o_mk = OUT("o_mk", [L, 256, 256]); o_mv = OUT("o_mv", [L, 256, 256])
        o_pool = OUT("o_pool", [L, 15, 256]); o_shift = OUT("o_shift", [L, 1, 1024])
        o_wkv = OUT("o_wkv", [L, 4, 64, 64]); o_s5r = OUT("o_s5r", [L, 1024]); o_s5i = OUT("o_s5i", [L, 1024])
        o_sbks = OUT("o_sbks", [L, STOK, 256]); o_sbvs = OUT("o_sbvs", [L, STOK, 256])
        o_pools = OUT("o_pools", [L, SPC, 15, 256]); o_shifts = OUT("o_shifts", [L, SPC, 1024])
        o_wkvs = OUT("o_wkvs", [L, SPC, 4, 64, 64]); o_s5rs = OUT("o_s5rs", [L, SPC, 1024]); o_s5is = OUT("o_s5is", [L, SPC, 1024])

        xd_t = nc.dram_tensor("XD", [SEQ, DM], F32, kind="Internal").ap()
        XD = [Buf("XD%d" % p, xd_t[p * PIECE:(p + 1) * PIECE, :]) for p in range(NPC)]
        hd_t = nc.dram_tensor("HD", [NPC, 128, 8 * PIECE], BF16, kind="Internal").ap()
        HD = [Buf("HD%d" % p, hd_t[p]) for p in range(NPC)]
        ob_t = nc.dram_tensor("OB", [4, NPC, 384, PIECE], F32, kind="Internal").ap()
        OB = [[Buf("OB%d_%d" % (q, p), ob_t[q, p]) for p in range(NPC)] for q in range(4)]
        obs_t = nc.dram_tensor("OBS", [4, 384, STOK], F32, kind="Internal").ap()
        OBS = [Buf("OBS%d" % q, obs_t[q]) for q in range(4)]
        W16 = {}
        for n in ("w_in", "w_branch", "w_out", "xa_w_q", "xa_w_k", "xa_w_v", "xa_w_o", "ffn_w_down", "s5_w_glu"):
            W16[n] = P.dram(n + "16", WEIGHT_SHAPES[n], BF16)
        W16["gate_t"] = P.dram("gate_t16", [L, 32, 128, 1024], BF16)
        W16["ffg_t"] = P.dram("ffg_t16", [L, 22, 128, 1024], BF16)
        W16["ffu_t"] = P.dram("ffu_t16", [L, 22, 128, 1024], BF16)

        PB = [P.ps([128, 512], F32, "PB%d" % i) for i in range(8)]
        bar = [P.sb([1, 8], F32, "bar%d" % i) for i in range(5)] + [PB[7]]
        C = {}
        for n, s in CONST_SHAPES.items():
            C[n] = P.sb(s, F32, n)
            P.dma("sp", C[n][:], D[n][:])
        ident = C["c_ident"]
        trib = P.sb([128, 128], BF16, "trib"); onesb = P.sb([128, 128], BF16, "onesb")
        P.cp("dve", trib[:], C["c_tri"][:])
        P.memset("pool", onesb[:], 1.0)
        m0 = P.sb([64, 512], F32, "m0")
        P.memset("pool", m0[:], 1.0)
        P.memset("pool", m0[:].rearrange("p (c j) -> p c j", j=64)[:, :, 0:1], 0.0)
        XS = P.sb([STOK, DM], F32, "XS")
        P.dma("sp", XS[:], xs[:])
        hTs = P.sb([128, 8, STOK], BF16, "hTs")
        cast_i = [0]

        def cast(out, in_, pool_ok=True):
            e = ("dve", "act", "pool")[cast_i[0] % 3] if pool_ok else ("dve", "act")[cast_i[0] % 2]
            cast_i[0] += 1
            P.cp(e, out, in_)

        def col(dst, vec):
            P.dma("sp", dst, vec.rearrange("(p o) -> p o", o=1))

        def phase_begin():
            ph = ExitStack()
            P.es = ph
            return ph

        def phase_end(ph):
            P.barrier(bar)
            ph.close()
            P.es = ges

        ph = phase_begin()
        stg = Ring(P, [128, 2048], F32, 3, "stg"); stg16 = Ring(P, [128, 2048], BF16, 3, "stg16")

        def precast(dst, src, R, Cc):
            for r0 in range(0, R, 128):
                rr = min(128, R - r0)
                for c0 in range(0, Cc, 2048):
                    cw = min(2048, Cc - c0)
                    a = stg.next(); b = stg16.next()
                    P.dma("sp", a[0:rr, 0:cw], src[r0:r0 + rr, c0:c0 + cw])
                    cast(b[0:rr, 0:cw], a[0:rr, 0:cw])
                    P.dma("act", dst[r0:r0 + rr, c0:c0 + cw], b[0:rr, 0:cw])

        def precast_tiled(dst_l, src_l, c_base, ntile):
            for t in range(ntile):
                a = stg.next(); b = stg16.next()
                P.dma("sp", a[:, 0:1024].rearrange("p (k c) -> p k c", k=8),
                      src_l[:, c_base + t * 128:c_base + (t + 1) * 128].rearrange("(k p) c -> p k c", p=128))
                cast(b[:, 0:1024], a[:, 0:1024])
                P.dma("act", dst_l[t], b[:, 0:1024])

        for l in range(L):
            precast(W16["w_in"][l], D["w_in"][l], DM, OFF_GATE)
            precast_tiled(W16["gate_t"][l], D["w_in"][l], OFF_GATE, 32)
            precast_tiled(W16["ffg_t"][l], D["ffn_w_gate"][l], 0, 22)
            precast_tiled(W16["ffu_t"][l], D["ffn_w_up"][l], 0, 22)
            precast(W16["ffn_w_down"][l], D["ffn_w_down"][l], DFF, DM)
            for n_ in range(4):
                precast(W16["w_branch"][l, n_], D["w_branch"][l, n_], 256, DM)
            precast(W16["w_out"][l], D["w_out"][l], DM, DM)
            precast(W16["xa_w_q"][l], D["xa_w_q"][l], DM, 256)
            precast(W16["xa_w_k"][l], D["xa_w_k"][l], DM, 256)
            precast(W16["xa_w_v"][l], D["xa_w_v"][l], DM, 256)
            precast(W16["xa_w_o"][l], D["xa_w_o"][l], 256, DM)
            precast(W16["s5_w_glu"][l], D["s5_w_glu"][l], 256, 512)
        phase_end(ph)

        def rmsnorm_T(R, xview, nrows, gb, dst, c0):
            j = R["sqj"].next(); s = R["stat"].next(); h = R["hrow"].next()
            P.act(j[0:nrows, :], xview, AF.Square, accum=s[0:nrows, 0:1])
            P.ts("dve", s[0:nrows, 1:2], s[0:nrows, 0:1], 1.0 / DM, ALU.mult, EPS, ALU.add)
            P.act(s[0:nrows, 3:4], s[0:nrows, 1:2], AF.Sqrt)
            P.recip(s[0:nrows, 2:3], s[0:nrows, 3:4])
            P.stt("dve", h[0:nrows, :], xview, s[0:nrows, 2:3], gb[0:nrows, :], ALU.mult, ALU.mult)
            for half in range(2):
                pt = PB[6 + half]
                for q4 in range(4):
                    kc = half * 4 + q4
                    P.tr(pt[:, q4 * 128:q4 * 128 + nrows], h[0:nrows, kc * 128:(kc + 1) * 128], ident[0:nrows, 0:nrows])
                cast(dst[:, half * 4:half * 4 + 4, c0:c0 + nrows], pt[:].rearrange("p (q t) -> p q t", q=4)[:, :, 0:nrows], pool_ok=False)

        def head_norm_rows(R, dst, src, nrows, gain_bc):
            s = R["stat"].next(); j = R["sqj"].next()
            for hh in range(4):
                P.act(j[0:nrows, hh * 64:(hh + 1) * 64], src[0:nrows, hh * 64:(hh + 1) * 64], AF.Square, accum=s[0:nrows, hh:hh + 1])
            P.ts("dve", s[0:nrows, 0:4], s[0:nrows, 0:4], 1.0 / 64, ALU.mult, EPS, ALU.add)
            P.act(s[0:nrows, 4:8], s[0:nrows, 0:4], AF.Sqrt)
            P.recip(s[0:nrows, 0:4], s[0:nrows, 4:8])
            for hh in range(4):
                P.stt("dve", dst[0:nrows, hh * 64:(hh + 1) * 64], src[0:nrows, hh * 64:(hh + 1) * 64], s[0:nrows, hh:hh + 1], gain_bc[0:nrows, :], ALU.mult, ALU.mult)

        def std_rings():
            return {"sqj": Ring(P, [128, DM], F32, 2, "sqj"), "hrow": Ring(P, [128, DM], F32, 2, "hrow"),
                    "stat": Ring(P, [128, 8], F32, 4, "stat"), "ev": Ring(P, [128, 512], F32, 4, "ev")}

        def xsrc(l, p):
            return (xp[p * PIECE:(p + 1) * PIECE, :] if l == 0 else XD[p][:, :])

        def phase_A(l):
            ph = phase_begin(); R = std_rings()
            gbc = P.sb([128, DM], F32, "gbc"); P.dma("sp", gbc[:], D["norm_mix"][l:l + 1, :].pbc(128))
            knbc = P.sb([128, 64], F32, "knbc"); P.dma("sp", knbc[:], D["sb_k_norm"][l:l + 1, :].pbc(128))
            w3 = W16["w_in"][l].rearrange("(k p) c -> p k c", p=128)
            Wkv = P.sb([128, 8, 512], BF16, "Wkv"); P.dma("sp", Wkv[:], w3[:, :, 1536:2048])
            Wx = Ring(P, [128, 8, 512], BF16, 2, "Wx")
            xr = Ring(P, [128, DM], F32, 6, "xt")
            hT = Ring(P, [128, 8, PIECE], BF16, 2, "hTA")
            bk = [0]

            def kv_out(hTt, c0, nrows, ok, ov):
                pp = PB[bk[0] % 4]; bk[0] += 1
                for kc in range(8):
                    P.mm(pp[0:nrows, :], hTt[:, kc, c0:c0 + nrows], Wkv[:, kc, :], start=(kc == 0), stop=(kc == 7))
                o = R["ev"].next()
                head_norm_rows(R, o, pp, nrows, knbc)
                P.cp("act", o[0:nrows, 256:512], pp[0:nrows, 256:512])
                P.dma("sp", ok, o[0:nrows, 0:256]); P.dma("act", ov, o[0:nrows, 256:512])

            def extra_cols(hTt, c0, nrows, sink):
                for cg in range(3):
                    w = Wx.next(); P.dma("sp", w[:], w3[:, :, cg * 512:(cg + 1) * 512])
                    pp = PB[4 + cg % 2]
                    for kc in range(8):
                        P.mm(pp[0:nrows, :], hTt[:, kc, c0:c0 + nrows], w[:, kc, :], start=(kc == 0), stop=(kc == 7))
                    o = R["ev"].next(); P.cp("dve", o[0:nrows, :], pp[0:nrows, :])
                    sink(cg, o)

            def sink_p(cg, o):
                if cg == 0:
                    P.dma("sp", o_pool[l, :, :], o[113:128, 0:256])
                    P.dma("sp", o_shift[l, :, 0:256], o[127:128, 256:512])
                elif cg == 1:
                    P.dma("sp", o_shift[l, :, 256:768], o[127:128, :])
                else:
                    P.dma("sp", o_shift[l, :, 768:1024], o[127:128, 0:256])

            def sink_s(cg, o):
                for b in range(SPC):
                    r = b * 8 + 7
                    if cg == 0:
                        P.dma("sp", o_pools[l, b, 7:15, :], o[b * 8:(b + 1) * 8, 0:256])
                        P.dma("act", o_pools[l, b, 0:7, :], D["st_pool"][l, b, 8:15, :])
                        P.dma("sp", o_shifts[l, b:b + 1, 0:256], o[r:r + 1, 256:512])
                    elif cg == 1:
                        P.dma("sp", o_shifts[l, b:b + 1, 256:768], o[r:r + 1, :])
                    else:
                        P.dma("sp", o_shifts[l, b:b + 1, 768:1024], o[r:r + 1, 0:256])

            for p in range(NPC):
                h = hT.next()
                for t in range(4):
                    x = xr.next()
                    P.dma("sp" if t % 2 == 0 else "act", x[:], xsrc(l, p)[t * 128:(t + 1) * 128, :])
                    rmsnorm_T(R, x[:, :], 128, gbc, h, t * 128)
                P.dma("sp", HD[p][:, :], h[:].rearrange("p k t -> p (k t)"))
                for t in range(4):
                    r0 = p * PIECE + t * 128
                    kv_out(h, t * 128, 128, o_sbk[l, r0:r0 + 128, :], o_sbv[l, r0:r0 + 128, :])
                if p == NPC - 1:
                    extra_cols(h, 384, 128, sink_p)
            rmsnorm_T(R, XS[:, :], STOK, gbc, hTs, 0)
            kv_out(hTs, 0, STOK, o_sbks[l, :, :], o_sbvs[l, :, :])
            extra_cols(hTs, 0, STOK, sink_s)
            P.dma("sp", gbc[:], D["norm_mem"][l:l + 1, :].pbc(128))
            P.dma("sp", knbc[:], D["xa_k_norm"][l:l + 1, :].pbc(128))
            P.dma("sp", Wkv[:, :, 0:256], W16["xa_w_k"][l].rearrange("(k p) c -> p k c", p=128))
            P.dma("sp", Wkv[:, :, 256:512], W16["xa_w_v"][l].rearrange("(k p) c -> p k c", p=128))
            hm = hT.next()
            for i in range(2):
                x = xr.next(); P.dma("sp", x[:], D["mem_prompt"][i * 128:(i + 1) * 128, :])
                rmsnorm_T(R, x[:, :], 128, gbc, hm, i * 128)
                kv_out(hm, i * 128, 128, o_mk[l, i * 128:(i + 1) * 128, :], o_mv[l, i * 128:(i + 1) * 128, :])
            phase_end(ph)

        ktd_t = nc.dram_tensor("KTD", [NPC, 64, PIECE], BF16, kind="Internal").ap()
        KTD = [Buf("KTD%d" % p, ktd_t[p]) for p in range(NPC)]
        vd_t = nc.dram_tensor("VD", [NPC, 128, 4 * 64], BF16, kind="Internal").ap()
        VD = [Buf("VD%d" % p, vd_t[p]) for p in range(NPC)]
        MU_OFF = {3: None, 4: None, 5: None, 6: 768, 7: 832, 8: 896, 9: 960}

        def wcols(q):
            r = [(OFF_SB + 64 * q, 64), (OFF_SB + 256 + 64 * q, 64), (OFF_SB + 512 + 64 * q, 64),
                 (OFF_RWKV + 64 * q, 64), (OFF_RWKV + 256 + 64 * q, 64), (OFF_RWKV + 512 + 64 * q, 64),
                 (OFF_RWKV + 768, 64), (OFF_RWKV + 832, 64), (OFF_RWKV + 896, 64), (OFF_RWKV + 960, 64),
                 (OFF_S5 + 64 * q, 64)]
            return r

        def phase_B(l):
            ph = phase_begin()
            Cb = {}
            for n in ("c_msu", "c_miu", "c_msl", "c_id8", "c_sbmask", "c_iota", "c_selw", "c_rc16", "c_newmask"):
                Cb[n] = C[n]
            w3 = W16["w_in"][l].rearrange("(k p) c -> p k c", p=128)
            WB = P.sb([128, 8, 768], BF16, "WB")
            hTp = Ring(P, [128, 8, PIECE], BF16, 2, "hTp")
            PT = P.sb([64, 12, PIECE + 1], F32, "PT")
            PTs = P.sb([64, 12, SPC, ST + 1], F32, "PTs")
            OBst = P.sb([64, 6, PIECE], F32, "OBst")
            OBsts = P.sb([64, 6, STOK], F32, "OBsts")
            PV = P.sb([128, 48], F32, "PV")
            tmp = Ring(P, [128, SUB + 16], F32, 10, "tmp")
            big = Ring(P, [128, PIECE], F32, 4, "big")
            bigb = Ring(P, [128, PIECE], BF16, 4, "bigb")
            kring = Ring(P, [64, PIECE], BF16, 3, "kring"); vring = Ring(P, [128, 4, 64], BF16, 3, "vring")
            RW = {n: P.sb([64, SUB], F32, "rw_" + n) for n in
                  ("r", "k", "v", "a", "kk", "kp", "logd", "L", "gam", "At", "Bt", "Kt", "Rt", "BtT", "KtT", "VT",
                   "Tba", "Tbr", "Tka", "Tkr", "P0", "P1", "Q0", "Q1", "N0", "N1", "M0", "M1", "WT", "UT")}
            Sst = P.sb([64, 64], F32, "Sst")
            wup = P.sb([64, 64], F32, "wup"); aup = P.sb([64, 64], F32, "aup"); gup = P.sb([64, 2, 64], F32, "gup")
            S5 = {n: P.sb([128, 2, SUB], F32, "s5_" + n) for n in ("cos", "sin", "rho")}
            S5w = {n: P.sb([128, 2, 64], F32, "s5w_" + n) for n in ("ccre", "ccimn", "bpre", "bpim")}
            BBT = {n: P.sb([64, 2, 128], F32, "bbt_" + n) for n in ("re", "im")}
            s5st = P.sb([128, 2, 2], F32, "s5st")
            s5c = P.sb([128, 2, 24], F32, "s5c")
            s5b = P.sb([128, 2, 2, 16], F32, "s5b")
            poolx = P.sb([64, 15 + PIECE], F32, "poolx")
            ones64 = C["c_ones64"]

            MAGIC = 12582912.0

            def sin_of(out, x, shift, t, u):
                P.ts("dve", t, x, float(shift), ALU.add)
                P.ts("dve", u, t, 1.0 / TWO_PI, ALU.mult)
                P.ts("dve", u, u, MAGIC, ALU.add)
                P.ts("dve", u, u, -MAGIC, ALU.add)
                P.stt("dve", t, u, -TWO_PI, t, ALU.mult, ALU.add)
                P.ts("dve", t, t, -3.1415925, ALU.max)
                P.ts("dve", t, t, 3.1415925, ALU.min)
                P.act(out, t, AF.Sin)

            def setup_slice(q):
                for ct, (c0, w) in enumerate(wcols(q)):
                    P.dma("sp" if ct % 2 == 0 else "act", WB[:, :, ct * 64:(ct + 1) * 64], w3[:, :, c0:c0 + w])
                for g in range(4):
                    P.dma("sp", WB[:, :, 704 + 16 * g:704 + 16 * (g + 1)], w3[:, :, OFF_POOL + 64 * g + 16 * q:OFF_POOL + 64 * g + 16 * q + 16])
                mu = D["rwkv_mu"][l]
                for i, off in enumerate((64 * q, 256 + 64 * q, 512 + 64 * q, 768, 832, 896, 960)):
                    col(PV[0:64, i:i + 1], mu[off:off + 64])
                col(PV[0:64, 7:8], D["rwkv_w0"][l, 64 * q:64 * q + 64]); col(PV[0:64, 8:9], D["rwkv_a0"][l, 64 * q:64 * q + 64])
                P.ts("dve", PV[0:64, 7:9], PV[0:64, 7:9], -1.0, ALU.mult)
                col(PV[0:64, 9:10], D["rwkv_k_k"][l, 64 * q:64 * q + 64]); col(PV[0:64, 10:11], D["rwkv_k_a"][l, 64 * q:64 * q + 64])
                col(PV[0:64, 11:12], D["rwkv_r_k"][l, 64 * q:64 * q + 64])
                col(PV[0:64, 12:13], D["sb_q_norm"][l, :]); col(PV[0:64, 13:14], D["sb_k_norm"][l, :])
                P.ts("dve", PV[0:64, 12:13], PV[0:64, 12:13], 0.125, ALU.mult)
                P.dma("sp", PV[:, 14:15], D["sb_bias"][l:l + 1, q:q + 1].pbc(128))
                col(PV[0:64, 15:16], D["s5_d"][l, 64 * q:64 * q + 64])
                P.dma("sp", wup[:], D["rwkv_w_up"][l, :, 64 * q:64 * q + 64])
                P.dma("sp", aup[:], D["rwkv_a_up"][l, :, 64 * q:64 * q + 64])
                P.dma("sp", gup[:], D["rwkv_g_up"][l, :, 64 * q:64 * q + 64].rearrange("(k p) c -> p k c", p=64))
                for t in range(2):
                    base = (4 * q + 2 * t) * 64
                    c_ = s5c[:, t, :]
                    col(c_[:, 0:1], D["s5_a_re"][l, base:base + 128]); col(c_[:, 1:2], D["s5_a_im"][l, base:base + 128])
                    for gg in range(2):
                        g = 4 * q + 2 * t + gg
                        P.dma("sp", c_[gg * 64:(gg + 1) * 64, 2:3], D["s5_log_dt"][l:l + 1, g:g + 1].pbc(64))
                    P.act(c_[:, 3:4], c_[:, 2:3], AF.Exp)
                    P.tt("dve", c_[:, 4:5], c_[:, 3:4], c_[:, 0:1], ALU.mult)
                    P.act(c_[:, 5:6], c_[:, 4:5], AF.Exp)
                    P.tt("dve", c_[:, 6:7], c_[:, 3:4], c_[:, 1:2], ALU.mult)
                    sin_of(c_[:, 9:10], c_[:, 6:7], 0.0, c_[:, 7:8], c_[:, 8:9])
                    sin_of(c_[:, 10:11], c_[:, 6:7], 0.5 * np.pi, c_[:, 7:8], c_[:, 8:9])
                    P.tt("dve", c_[:, 11:12], c_[:, 5:6], c_[:, 10:11], ALU.mult)
                    P.tt("dve", c_[:, 12:13], c_[:, 5:6], c_[:, 9:10], ALU.mult)
                    P.tt("dve", c_[:, 13:14], c_[:, 0:1], c_[:, 0:1], ALU.mult)
                    P.stt("dve", c_[:, 14:15], c_[:, 1:2], c_[:, 1:2], c_[:, 13:14], ALU.mult, ALU.add)
                    P.recip(c_[:, 15:16], c_[:, 14:15])
                    P.ts("dve", c_[:, 16:17], c_[:, 11:12], -1.0, ALU.add)
                    P.tt("dve", c_[:, 17:18], c_[:, 16:17], c_[:, 0:1], ALU.mult)
                    P.stt("dve", c_[:, 18:19], c_[:, 12:13], c_[:, 1:2], c_[:, 17:18], ALU.mult, ALU.add)
                    P.tt("dve", c_[:, 19:20], c_[:, 18:19], c_[:, 15:16], ALU.mult)
                    P.tt("dve", c_[:, 17:18], c_[:, 16:17], c_[:, 1:2], ALU.mult)
                    P.stt("dve", c_[:, 18:19], c_[:, 12:13], c_[:, 0:1], c_[:, 17:18], ALU.mult, ALU.subtract)
                    P.tt("dve", c_[:, 20:21], c_[:, 18:19], c_[:, 15:16], ALU.mult)
                    P.dma("sp", s5b[:, t, 0, :], D["s5_b_re"][l, base:base + 128, :])
                    P.dma("sp", s5b[:, t, 1, :], D["s5_b_im"][l, base:base + 128, :])
                    P.memset("pool", S5w["bpre"][:, t, :], 0.0); P.memset("pool", S5w["bpim"][:, t, :], 0.0)
                    P.memset("pool", S5w["ccre"][:, t, :], 0.0); P.memset("pool", S5w["ccimn"][:, t, :], 0.0)
                    tb = tmp.next()
                    for gg in range(2):
                        rs = slice(gg * 64, (gg + 1) * 64); cs = slice(32 * t + 16 * gg, 32 * t + 16 * gg + 16)
                        g = 4 * q + 2 * t + gg
                        P.ts("dve", tb[rs, 0:16], s5b[rs, t, 1, :], c_[rs, 20:21], ALU.mult)
                        P.stt("dve", S5w["bpre"][rs, t, cs], s5b[rs, t, 0, :], c_[rs, 19:20], tb[rs, 0:16], ALU.mult, ALU.subtract)
                        P.ts("dve", tb[rs, 16:32], s5b[rs, t, 0, :], c_[rs, 20:21], ALU.mult)
                        P.stt("dve", S5w["bpim"][rs, t, cs], s5b[rs, t, 1, :], c_[rs, 19:20], tb[rs, 16:32], ALU.mult, ALU.add)
                        P.dma("sp", S5w["ccre"][rs, t, cs], D["s5_c_re"][l, g].rearrange("c p -> p c"), allow_slow_non_contiguous=True)
                        P.dma("sp", S5w["ccimn"][rs, t, cs], D["s5_c_im"][l, g].rearrange("c p -> p c"), allow_slow_non_contiguous=True)
                    P.ts("dve", S5w["ccimn"][:, t, :], S5w["ccimn"][:, t, :], -1.0, ALU.mult)
                    for nm, src in (("re", "bpre"), ("im", "bpim")):
                        P.tr(PB[0][0:64, 0:128], S5w[src][:, t, :], ident[:, :])
                        P.cp("dve", BBT[nm][:, t, :], PB[0][0:64, 0:128])
                    a1 = tmp.next(); a2 = tmp.next()
                    P.ts("dve", a1[:, 0:SUB], Cb["c_iota"][:, :], c_[:, 6:7], ALU.mult)
                    a3 = tmp.next()
                    sin_of(S5["sin"][:, t, :], a1[:, 0:SUB], 0.0, a2[:, 0:SUB], a3[:, 0:SUB])
                    sin_of(S5["cos"][:, t, :], a1[:, 0:SUB], 0.5 * np.pi, a2[:, 0:SUB], a3[:, 0:SUB])
                    P.ts("dve", S5["rho"][:, t, :], Cb["c_iota"][:, :], 0.0, ALU.mult, c_[:, 5:6], ALU.add)

            P.memset("pool", PV[:, 16:17], -float(np.pi))

            def rwkv(cur, prev, ntok, Cn, ob_o, ob_bv, ob_g, S):
                nch = ntok // Cn
                n = slice(0, ntok)
                pm = {}
                for i, (ct, nm) in enumerate(((3, "r"), (4, "k"), (5, "v"))):
                    d = tmp.next()
                    P.tt("pool", d[0:64, n], prev(ct), cur(ct), ALU.subtract)
                    P.stt("dve", RW[nm][:, n], d[0:64, n], PV[0:64, i:i + 1], cur(ct), ALU.mult, ALU.add)
                lo = {}
                for i, ct in ((3, 6), (4, 7), (5, 8), (6, 9)):
                    d = tmp.next(); o = tmp.next()
                    P.tt("pool", d[0:64, n], prev(ct), cur(ct), ALU.subtract)
                    P.stt("dve", o[0:64, n], d[0:64, n], PV[0:64, i:i + 1], cur(ct), ALU.mult, ALU.add)
                    lo[ct] = o
                th = tmp.next()
                P.act(th[0:64, n], lo[6][0:64, n], AF.Tanh)
                P.mm(PB[0][0:64, n], wup[:, :], th[0:64, n])
                e = tmp.next()
                P.act(e[0:64, n], PB[0][0:64, n], AF.Exp, bias=PV[0:64, 7:8], scale=-1.0)
                P.ts("dve", e[0:64, n], e[0:64, n], 1.0, ALU.add)
                P.recip(e[0:64, n], e[0:64, n])
                P.ts("dve", RW["logd"][:, n], e[0:64, n], DECAY_C, ALU.mult)
                P.mm(PB[1][0:64, n], aup[:, :], lo[7][0:64, n])
                e2 = tmp.next()
                P.act(e2[0:64, n], PB[1][0:64, n], AF.Exp, bias=PV[0:64, 8:9], scale=-1.0)
                P.ts("dve", e2[0:64, n], e2[0:64, n], 1.0, ALU.add)
                P.recip(RW["a"][:, n], e2[0:64, n])
                for i, ct in enumerate((8, 9)):
                    sg = lo[ct]
                    P.act(sg[0:64, n], sg[0:64, n], AF.Exp, scale=-1.0)
                    P.ts("dve", sg[0:64, n], sg[0:64, n], 1.0, ALU.add)
                    P.recip(sg[0:64, n], sg[0:64, n])
                    P.mm(PB[0][0:64, n], gup[:, i, :], sg[0:64, n], start=(i == 0), stop=(i == 1))
                P.cp("act", ob_g, PB[0][0:64, n])
                P.ts("dve", RW["kk"][:, n], RW["k"][:, n], PV[0:64, 9:10], ALU.mult)
                sq = tmp.next()
                P.tt("pool", sq[0:64, n], RW["kk"][:, n], RW["kk"][:, n], ALU.mult)
                P.mm(PB[1][0:64, n], ones64[0:64, 0:64], sq[0:64, n])
                rn = tmp.next()
                P.ts("dve", rn[0:64, n], PB[1][0:64, n], 1e-24, ALU.max)
                P.act(rn[0:64, n], rn[0:64, n], AF.Sqrt)
                P.recip(rn[0:64, n], rn[0:64, n])
                P.tt("dve", RW["kk"][:, n], RW["kk"][:, n], rn[0:64, n], ALU.mult)
                t1 = tmp.next()
                P.ts("dve", t1[0:64, n], RW["a"][:, n], -1.0, ALU.add, PV[0:64, 10:11], ALU.mult)
                P.stt("dve", RW["kp"][:, n], t1[0:64, n], 1.0, RW["k"][:, n], ALU.add, ALU.mult)
                pr = tmp.next()
                P.stt("dve", pr[0:64, n], RW["r"][:, n], PV[0:64, 11:12], RW["kp"][:, n], ALU.mult, ALU.mult)
                P.mm(PB[0][0:64, n], ones64[0:64, 0:64], pr[0:64, n])
                P.tt("dve", ob_bv, PB[0][0:64, n], RW["v"][:, n], ALU.mult)
                P.scan(RW["L"][:, n], m0[:, n], RW["logd"][:, n], 0.0)
                gi = tmp.next(); gp = tmp.next(); lp = tmp.next()
                P.act(RW["gam"][:, n], RW["L"][:, n], AF.Exp)
                P.act(gi[0:64, n], RW["L"][:, n], AF.Exp, scale=-1.0)
                P.tt("pool", lp[0:64, n], RW["L"][:, n], RW["logd"][:, n], ALU.subtract)
                P.act(gp[0:64, n], lp[0:64, n], AF.Exp)
                P.stt("dve", RW["At"][:, n], RW["kk"][:, n], -1.0, gp[0:64, n], ALU.mult, ALU.mult)
                t2 = tmp.next()
                P.tt("pool", t2[0:64, n], RW["kk"][:, n], RW["a"][:, n], ALU.mult)
                P.tt("dve", RW["Bt"][:, n], t2[0:64, n], gi[0:64, n], ALU.mult)
                P.tt("dve", RW["Kt"][:, n], RW["kp"][:, n], gi[0:64, n], ALU.mult)
                P.tt("dve", RW["Rt"][:, n], RW["r"][:, n], RW["gam"][:, n], ALU.mult)
                for nm, src, bank in (("BtT", "Bt", 2), ("KtT", "Kt", 3), ("VT", "v", 4)):
                    for c in range(nch):
                        P.tr(PB[bank][0:Cn, c * 64:(c + 1) * 64], RW[src][:, c * Cn:(c + 1) * Cn], ident[0:64, 0:64])
                    P.cp("act" if nm == "KtT" else "dve", RW[nm][0:Cn, 0:nch * 64], PB[bank][0:Cn, 0:nch * 64])
                def v3(t, rows=Cn):
                    return t[0:rows, 0:nch * Cn].rearrange("p (c j) -> p c j", j=Cn)
                prods = (("Tba", "Bt", "At", "c_msu", 2), ("Tbr", "Bt", "Rt", "c_miu", 3), ("Tka", "Kt", "At", "c_msu", 4),
                         ("Tkr", "Kt", "Rt", "c_miu", 5), ("Q0", "At", "Bt", "c_msl", 6))
                for nm, a_, b_, mk, bank in prods:
                    for c in range(nch):
                        cs = slice(c * Cn, (c + 1) * Cn)
                        P.mm(PB[bank][0:Cn, cs], RW[a_][:, cs], RW[b_][:, cs])
                    P.tt("dve", v3(RW[nm]), v3(PB[bank]), Cb[mk][0:Cn, 0:nch, 0:Cn], ALU.mult)
                P.cp("pool", RW["P0"][0:Cn, 0:nch * Cn], RW["Tba"][0:Cn, 0:nch * Cn])
                P.tt("dve", v3(RW["N0"]), v3(RW["Tba"]), Cb["c_id8"][0:Cn, 0:nch, 0:Cn], ALU.add)
                P.tt("dve", v3(RW["M0"]), v3(RW["Q0"]), Cb["c_id8"][0:Cn, 0:nch, 0:Cn], ALU.add)
                cu = 0
                m = 1
                while (1 << m) < Cn:
                    nx = 1 - cu
                    Pc, Qc, Nc, Mc = RW["P%d" % cu], RW["Q%d" % cu], RW["N%d" % cu], RW["M%d" % cu]
                    Pn, Qn, Nn, Mn = RW["P%d" % nx], RW["Q%d" % nx], RW["N%d" % nx], RW["M%d" % nx]
                    for c in range(nch):
                        cs = slice(c * Cn, (c + 1) * Cn)
                        P.mm(PB[2][0:Cn, cs], Qc[0:Cn, cs], Pc[0:Cn, cs])
                        P.mm(PB[3][0:Cn, cs], Pc[0:Cn, cs], Qc[0:Cn, cs])
                    P.cp("dve", Pn[0:Cn, 0:nch * Cn], PB[2][0:Cn, 0:nch * Cn])
                    P.cp("act", Qn[0:Cn, 0:nch * Cn], PB[3][0:Cn, 0:nch * Cn])
                    for c in range(nch):
                        cs = slice(c * Cn, (c + 1) * Cn)
                        P.mm(PB[4][0:Cn, cs], Mc[0:Cn, cs], Pn[0:Cn, cs])
                        P.mm(PB[6][0:Cn, cs], Pn[0:Cn, cs], Mc[0:Cn, cs])
                    P.tt("dve", Nn[0:Cn, 0:nch * Cn], Nc[0:Cn, 0:nch * Cn], PB[4][0:Cn, 0:nch * Cn], ALU.add)
                    P.tt("dve", Mn[0:Cn, 0:nch * Cn], Mc[0:Cn, 0:nch * Cn], PB[6][0:Cn, 0:nch * Cn], ALU.add)
                    cu = nx
                    m += 1
                Nf = RW["N%d" % cu]
                for c in range(nch):
                    cs = slice(c * Cn, (c + 1) * Cn); c64 = slice(c * 64, (c + 1) * 64)
                    P.mm(PB[0][0:Cn, 0:64], RW["At"][:, cs], S[:, :], start=True, stop=False)
                    P.mm(PB[0][0:Cn, 0:64], RW["Tka"][0:Cn, cs], RW["VT"][0:Cn, c64], start=False, stop=True)
                    P.cp("act", RW["WT"][0:Cn, 0:64], PB[0][0:Cn, 0:64])
                    P.mm(PB[1][0:Cn, 0:64], Nf[0:Cn, cs], RW["WT"][0:Cn, 0:64])
                    P.cp("dve", RW["UT"][0:Cn, 0:64], PB[1][0:Cn, 0:64])
                    P.mm(PB[7][0:64, cs], S[:, :], RW["Rt"][:, cs], start=True, stop=False)
                    P.mm(PB[7][0:64, cs], RW["UT"][0:Cn, 0:64], RW["Tbr"][0:Cn, cs], start=False, stop=False)
                    P.mm(PB[7][0:64, cs], RW["VT"][0:Cn, c64], RW["Tkr"][0:Cn, cs], start=False, stop=True)
                    P.mm(PB[3][0:64, 0:64], RW["BtT"][0:Cn, c64], RW["UT"][0:Cn, 0:64], start=True, stop=False)
                    P.mm(PB[3][0:64, 0:64], RW["KtT"][0:Cn, c64], RW["VT"][0:Cn, c64], start=False, stop=True)
                    sn = tmp.next()
                    P.tt("dve", sn[0:64, 0:64], PB[3][0:64, 0:64], S[:, :], ALU.add)
                    P.ts("dve", S[:, :], sn[0:64, 0:64], RW["gam"][:, (c + 1) * Cn - 1:(c + 1) * Cn], ALU.mult)
                P.cp("act", ob_o, PB[7][0:64, n])

            def s5mix(cur, ntok, ob_y, st):
                n = slice(0, ntok)
                u = cur(10)
                for t in range(2):
                    P.mm(PB[0][:, n], BBT["re"][:, t, :], u)
                    P.mm(PB[1][:, n], BBT["im"][:, t, :], u)
                    cs_, sn_ = S5["cos"][:, t, n], S5["sin"][:, t, n]
                    a1 = tmp.next(); a2 = tmp.next(); zr = tmp.next(); zi = tmp.next()
                    P.tt("dve", a1[:, n], PB[0][:, n], cs_, ALU.mult)
                    P.tt("dve", a2[:, n], PB[1][:, n], sn_, ALU.mult)
                    P.tt("pool", a1[:, n], a1[:, n], a2[:, n], ALU.add)
                    a3 = tmp.next(); a4 = tmp.next()
                    P.tt("dve", a3[:, n], PB[1][:, n], cs_, ALU.mult)
                    P.tt("dve", a4[:, n], PB[0][:, n], sn_, ALU.mult)
                    P.tt("pool", a3[:, n], a3[:, n], a4[:, n], ALU.subtract)
                    P.scan(zr[:, n], S5["rho"][:, t, n], a1[:, n], st[:, t, 0:1])
                    P.scan(zi[:, n], S5["rho"][:, t, n], a3[:, n], st[:, t, 1:2])
                    sr = tmp.next(); si = tmp.next()
                    P.tt("dve", sr[:, n], zr[:, n], cs_, ALU.mult)
                    P.tt("pool", a2[:, n], zi[:, n], sn_, ALU.mult)
                    P.tt("dve", sr[:, n], sr[:, n], a2[:, n], ALU.subtract)
                    P.tt("dve", si[:, n], zi[:, n], cs_, ALU.mult)
                    P.tt("pool", a4[:, n], zr[:, n], sn_, ALU.mult)
                    P.tt("dve", si[:, n], si[:, n], a4[:, n], ALU.add)
                    P.cp("act", st[:, t, 0:1], sr[:, ntok - 1:ntok])
                    P.cp("act", st[:, t, 1:2], si[:, ntok - 1:ntok])
                    P.mm(PB[6][0:64, n], S5w["ccre"][:, t, :], sr[:, n], start=(t == 0), stop=False)
                    P.mm(PB[6][0:64, n], S5w["ccimn"][:, t, :], si[:, n], start=False, stop=(t == 1))
                P.stt("dve", ob_y, u, PV[0:64, 15:16], PB[6][0:64, n], ALU.mult, ALU.add)

            def poolmix(ext, ntok, ob_p, first):
                W = 15 + ntok
                s2 = tmp.next() if W <= SUB + 16 else big.next()
                s4 = tmp.next() if W <= SUB + 16 else big.next()
                s8 = tmp.next() if W <= SUB + 16 else big.next()
                s16 = tmp.next() if W <= SUB + 16 else big.next()
                P.tt("dve", s2[0:64, 1:W], ext[:, 1:W], ext[:, 0:W - 1], ALU.add)
                P.tt("dve", s4[0:64, 3:W], s2[0:64, 3:W], s2[0:64, 1:W - 2], ALU.add)
                P.tt("dve", s8[0:64, 7:W], s4[0:64, 7:W], s4[0:64, 3:W - 4], ALU.add)
                P.tt("dve", s16[0:64, 15:W], s8[0:64, 15:W], s8[0:64, 7:W - 8], ALU.add)
                sl = slice(15, W)
                acc = tmp.next() if W <= SUB + 16 else big.next()
                sw = Cb["c_selw"]
                P.ts("dve", acc[0:64, sl], s2[0:64, sl], sw[:, 0:1], ALU.mult)
                P.stt("dve", acc[0:64, sl], s4[0:64, sl], sw[:, 1:2], acc[0:64, sl], ALU.mult, ALU.add)
                P.stt("dve", acc[0:64, sl], s8[0:64, sl], sw[:, 2:3], acc[0:64, sl], ALU.mult, ALU.add)
                P.stt("dve", acc[0:64, sl], s16[0:64, sl], sw[:, 3:4], acc[0:64, sl], ALU.mult, ALU.add)
                if first:
                    f = slice(15, 31); rc = Cb["c_rc16"]
                    t1 = tmp.next()
                    P.tt("dve", acc[0:64, f], s2[0:64, f], rc[:, 0, :], ALU.mult)
                    for gi_, sK in ((1, s4), (2, s8), (3, s16)):
                        P.tt("dve", t1[0:64, 0:16], sK[0:64, f], rc[:, gi_, :], ALU.mult)
                        P.tt("dve", acc[0:64, f], acc[0:64, f], t1[0:64, 0:16], ALU.add)
                P.tt("dve", ob_p, acc[0:64, sl], ext[:, sl], ALU.subtract)

            Qs = P.sb([64, PIECE], BF16, "Qs"); Kn = P.sb([64, PIECE], BF16, "Kn"); vtb = P.sb([128, 4, 64], BF16, "vtb")
            sbmask = Cb["c_sbmask"]

            def sb_prompt(G):
                for ct, gcol, dst in ((0, 12, Qs), (1, 13, Kn)):
                    x = PT[:, ct, 1:PIECE + 1]
                    sq = big.next(); P.tt("pool", sq[0:64, :], x, x, ALU.mult)
                    P.mm(PB[0][0:64, :], ones64[0:64, 0:64], sq[0:64, :])
                    rs = big.next()
                    P.ts("dve", rs[0:64, :], PB[0][0:64, :], 1.0 / 64, ALU.mult, EPS, ALU.add)
                    P.act(rs[0:64, :], rs[0:64, :], AF.Sqrt)
                    P.recip(rs[0:64, :], rs[0:64, :])
                    P.stt("dve", dst[:, :], x, PV[0:64, gcol:gcol + 1], rs[0:64, :], ALU.mult, ALU.mult)
                P.dma("sp", KTD[G][:, :], Kn[:, :])
                for t in range(4):
                    P.tr(PB[6][:, t * 64:(t + 1) * 64], PT[:, 2, 1 + t * 128:1 + (t + 1) * 128], ident[0:64, 0:64])
                P.cp("dve", vtb[:].rearrange("p a d -> p (a d)"), PB[6][:, 0:256])
                P.dma("act", VD[G][:, :], vtb[:].rearrange("p a d -> p (a d)"))
                first = True
                for kg in range(G, -1, -1):
                    kt = kring.next(); P.dma("sp", kt[:, :], KTD[kg][:, :])
                    vt = vring.next(); P.dma("act", vt[:].rearrange("p a d -> p (a d)"), VD[kg][:, :])
                    diag = (kg == G)
                    for b4 in range(3, -1, -1):
                        P.mm(PB[2][:, :], kt[:, b4 * 128:(b4 + 1) * 128], Qs[:, :])
                        e1 = big.next(); P.act(e1[:, :], PB[2][:, :], AF.Exp, bias=PV[:, 14:15])
                        sp = big.next(); P.act(sp[:, :], e1[:, :], AF.Ln, bias=1.0)
                        if diag:
                            P.tt("pool", sp[:, :], sp[:, :], sbmask[:, b4, :], ALU.mult)
                        spb = bigb.next(); P.cp("pool", spb[:, :], sp[:, :])
                        P.mm(PB[3][:, :], trib[:, :], spb[:, :])
                        t1 = big.next()
                        P.tt("dve", t1[:, :], PB[2][:, :], sp[:, :], ALU.subtract)
                        P.tt("dve", t1[:, :], t1[:, :], PB[3][:, :], ALU.subtract)
                        if not first:
                            P.tt("dve", t1[:, :], t1[:, :], PB[4][:, :], ALU.subtract)
                        P.mm(PB[4][:, :], onesb[:, :], spb[:, :], start=first, stop=True)
                        att = bigb.next(); P.act(att[:, :], t1[:, :], AF.Exp, bias=PV[:, 14:15])
                        if diag:
                            P.tt("pool", att[:, :], att[:, :], sbmask[:, b4, :], ALU.mult)
                        P.mm(PB[5][0:64, :], vt[:, b4, :], att[:, :], start=first, stop=(kg == 0 and b4 == 0))
                        first = False
                P.cp("act", OBst[:, 0, :], PB[5][0:64, :])

            Sst_s = P.sb([64, 64], F32, "Sst_s"); s5st_s = P.sb([128, 2, 2], F32, "s5st_s")
            poolx_s = P.sb([64, 15 + ST], F32, "poolx_s")
            RW_OFF = {3: None, 4: None, 5: None, 6: 768, 7: 832, 8: 896, 9: 960}

            def run_slice(q):
                setup_slice(q)
                P.memset("pool", Sst[:], 0.0); P.memset("pool", s5st[:], 0.0)
                P.memset("pool", PT[:, :, 0:1], 0.0); P.memset("pool", poolx[:, 0:15], 0.0)
                for G in range(NPC):
                    h = hTp.next(); P.dma("sp", h[:].rearrange("p k t -> p (k t)"), HD[G][:, :])
                    if G > 0:
                        P.cp("pool", PT[:, :, 0:1], PT[:, :, PIECE:PIECE + 1])
                    for ct in range(12):
                        pb = PB[ct % 2]
                        for kc in range(8):
                            P.mm(pb[0:64, :], WB[:, kc, ct * 64:(ct + 1) * 64], h[:, kc, :], start=(kc == 0), stop=(kc == 7))
                        P.cp("act" if ct % 2 else "dve", PT[:, ct, 1:PIECE + 1], pb[0:64, :])
                    if "sb" in STAGES:
                        sb_prompt(G)
                    P.cp("pool", poolx[:, 15:15 + PIECE], PT[:, 11, 1:PIECE + 1])
                    for sub in range(PIECE // SUB):
                        o = sub * SUB
                        cur = lambda ct, o=o: PT[:, ct, 1 + o:1 + o + SUB]
                        prev = lambda ct, o=o: PT[:, ct, o:o + SUB]
                        if "rwkv" in STAGES:
                            rwkv(cur, prev, SUB, CH, OBst[:, 1, o:o + SUB], OBst[:, 2, o:o + SUB], OBst[:, 3, o:o + SUB], Sst)
                        if "s5" in STAGES:
                            s5mix(cur, SUB, OBst[:, 4, o:o + SUB], s5st)
                        poolmix(poolx[:, o:o + 15 + SUB], SUB, OBst[:, 5, o:o + SUB], first=(G == 0 and sub == 0))
                    P.cp("pool", poolx[:, 0:15], poolx[:, PIECE:PIECE + 15])
                    P.dma("sp", OB[q][G][:, :].rearrange("(k r) t -> r k t", r=64), OBst[:])
                P.dma("sp", o_wkv[l, q].rearrange("v k -> k v"), Sst[:], allow_slow_non_contiguous=True)
                for t in range(2):
                    base = (4 * q + 2 * t) * 64
                    P.dma("sp", o_s5r[l, base:base + 128].rearrange("(p o) -> p o", o=1), s5st[:, t, 0:1])
                    P.dma("sp", o_s5i[l, base:base + 128].rearrange("(p o) -> p o", o=1), s5st[:, t, 1:2])
                for ct in range(12):
                    pb = PB[ct % 2]
                    for kc in range(8):
                        P.mm(pb[0:64, 0:STOK], WB[:, kc, ct * 64:(ct + 1) * 64], hTs[:, kc, :], start=(kc == 0), stop=(kc == 7))
                    P.cp("act" if ct % 2 else "dve", PTs[:, ct, :, 1:ST + 1], pb[0:64, 0:STOK].rearrange("p (b t) -> p b t", t=ST))
                for ct, off in ((3, 64 * q), (4, 256 + 64 * q), (5, 512 + 64 * q), (6, 768), (7, 832), (8, 896), (9, 960)):
                    P.dma("sp", PTs[:, ct, :, 0:1], D["st_shift"][l, :, off:off + 64].rearrange("b (p o) -> p b o", o=1))
                for b in range(SPC):
                    cur = lambda ct, b=b: PTs[:, ct, b, 1:ST + 1]
                    prev = lambda ct, b=b: PTs[:, ct, b, 0:ST]
                    bs = slice(b * ST, (b + 1) * ST)
                    if "rwkv" in STAGES:
                        P.dma("sp", Sst_s[:], D["st_wkv"][l, b, q].rearrange("v k -> k v"), allow_slow_non_contiguous=True)
                        rwkv(cur, prev, ST, ST, OBsts[:, 1, bs], OBsts[:, 2, bs], OBsts[:, 3, bs], Sst_s)
                        P.dma("sp", o_wkvs[l, b, q].rearrange("v k -> k v"), Sst_s[:], allow_slow_non_contiguous=True)
                    if "s5" in STAGES:
                        for t in range(2):
                            base = (4 * q + 2 * t) * 64
                            col(s5st_s[:, t, 0:1], D["st_s5r"][l, b, base:base + 128])
                            col(s5st_s[:, t, 1:2], D["st_s5i"][l, b, base:base + 128])
                        s5mix(cur, ST, OBsts[:, 4, bs], s5st_s)
                        for t in range(2):
                            base = (4 * q + 2 * t) * 64
                            P.dma("sp", o_s5rs[l, b, base:base + 128].rearrange("(p o) -> p o", o=1), s5st_s[:, t, 0:1])
                            P.dma("sp", o_s5is[l, b, base:base + 128].rearrange("(p o) -> p o", o=1), s5st_s[:, t, 1:2])
                    for g in range(4):
                        c0 = 64 * g + 16 * q
                        P.dma("sp", poolx_s[16 * g:16 * g + 16, 0:15], D["st_pool"][l, b, :, c0:c0 + 16].rearrange("r c -> c r"), allow_slow_non_contiguous=True)
                    P.cp("pool", poolx_s[:, 15:15 + ST], PTs[:, 11, b, 1:ST + 1])
                    poolmix(poolx_s[:, :], ST, OBsts[:, 5, bs], first=False)
                P.dma("sp", OBS[q][:, :].rearrange("(k r) t -> r k t", r=64)[:, 1:6, :], OBsts[:, 1:6, :])

            for q in range(4):
                run_slice(q)
            phase_end(ph)

        def phase_C(l):
            ph = phase_begin(); R = std_rings()
            gb = {}
            for nm in ("norm_cross", "norm_ffn"):
                gb[nm] = P.sb([128, DM], F32, "gb_" + nm); P.dma("sp", gb[nm][:], D[nm][l:l + 1, :].pbc(128))
            PVc = P.sb([128, 16], F32, "PVc")
            for p in range(2):
                col(PVc[:, p:p + 1], D["pool_scale"][l, p * 128:(p + 1) * 128])
                col(PVc[:, 2 + p:3 + p], D["rwkv_ln_w"][l, p * 128:(p + 1) * 128])
                col(PVc[:, 4 + p:5 + p], D["rwkv_ln_b"][l, p * 128:(p + 1) * 128])
                col(PVc[p * 64:(p + 1) * 64, 6:7], D["xa_q_norm"][l, :])
            P.ts("dve", PVc[:, 6:7], PVc[:, 6:7], 0.125, ALU.mult)
            poolW = P.sb([128, 2, 128], F32, "poolW"); P.memset("pool", poolW[:], 0.0)
            for g in range(4):
                P.dma("sp", poolW[(g % 2) * 64:(g % 2) * 64 + 64, g // 2, (g % 2) * 64:(g % 2) * 64 + 64], D["pool_w"][l, g])
            glu = P.sb([128, 2, 512], BF16, "glu"); P.dma("sp", glu[:], W16["s5_w_glu"][l].rearrange("(k p) c -> p k c", p=128))
            Wbr = P.sb([128, 4, 2, DM], BF16, "Wbr")
            for nb in range(4):
                P.dma("act", Wbr[:, nb, :, :], W16["w_branch"][l, nb].rearrange("(k p) c -> p k c", p=128))
            Wq = P.sb([128, 8, 256], BF16, "Wq"); P.dma("sp", Wq[:], W16["xa_w_q"][l].rearrange("(k p) c -> p k c", p=128))
            Wo = P.sb([128, 2, DM], BF16, "Wo"); P.dma("sp", Wo[:], W16["xa_w_o"][l].rearrange("(k p) c -> p k c", p=128))
            onespad = P.sb([128, 2, 128], BF16, "onespad"); P.memset("pool", onespad[:], 0.0)
            P.memset("pool", onespad[:, 0, 0:64], 1.0); P.memset("pool", onespad[:, 1, 64:128], 1.0)
            KmT = P.sb([128, 2, 256], BF16, "KmT"); Vpad = P.sb([128, 2, 4, 128], BF16, "Vpad")
            memf = P.sb([128, 2, 256], F32, "memf")
            xt = [P.sb([128, DM], F32, "xc%d" % i) for i in range(4)]
            hTg = P.sb([128, 8, PIECE], BF16, "hTg"); hT2 = P.sb([128, 8, PIECE], BF16, "hT2")
            inr = Ring(P, [128, PIECE], F32, 5, "inr")
            tf = Ring(P, [128, PIECE], F32, 5, "tf")
            accr = Ring(P, [128, PIECE], F32, 2, "accr")
            BR = [[P.sb([128, PIECE], BF16, "BR%d_%d" % (nb, p)) for p in range(2)] for nb in range(4)]
            merged = P.sb([128, 8, PIECE], BF16, "merged")
            actT = P.sb([128, 22, PIECE], BF16, "actT")
            wr = Ring(P, [128, 1024], BF16, 6, "wr")
            qn = [P.sb([128, PIECE], BF16, "qn%d" % p) for p in range(2)]
            xo = [P.sb([128, PIECE], BF16, "xo%d" % p) for p in range(2)]
            eb = Ring(P, [128, PIECE], BF16, 3, "eb")
            ones64 = C["c_ones64"]

            def load_mem(kview, vview):
                P.memset("pool", Vpad[:], 0.0)
                P.dma("sp", memf[:], kview.rearrange("(t p) c -> p t c", p=128))
                for mt in range(2):
                    for hp in range(2):
                        P.tr(PB[0][:, (mt * 2 + hp) * 128:(mt * 2 + hp + 1) * 128], memf[:, mt, hp * 128:(hp + 1) * 128], ident[:, :])
                P.cp("dve", KmT[:].rearrange("p h (t m) -> p h t m", t=2), PB[0][:, :].rearrange("p (t h m) -> p h t m", t=2, h=2))
                vf = inr.next()
                P.dma("act", vf[:, 0:512].rearrange("p (t c) -> p t c", t=2), vview.rearrange("(t p) c -> p t c", p=128))
                for mt in range(2):
                    for h in range(4):
                        cast(Vpad[:, mt, h, (h % 2) * 64:(h % 2) * 64 + 64], vf[:, mt * 256 + h * 64:mt * 256 + (h + 1) * 64])

            def proc(xts, nrows, ntok, hsrc, obsrc, mem_cols, xdst):
                n = slice(0, ntok)
                def load_in(k, p):
                    t = inr.next()
                    if k == 5:
                        for gg in range(2):
                            g = 2 * p + gg
                            for qq in range(4):
                                P.dma("sp" if qq % 2 else "act", t[gg * 64 + 16 * qq:gg * 64 + 16 * qq + 16, n], obsrc(qq)[5 * 64 + 16 * g:5 * 64 + 16 * g + 16, :])
                    else:
                        for hh in range(2):
                            P.dma("sp" if hh else "act", t[hh * 64:(hh + 1) * 64, n], obsrc(2 * p + hh)[k * 64:(k + 1) * 64, :])
                    return t
                for p in range(2):
                    ip = load_in(5, p)
                    P.mm(PB[0][:, n], poolW[:, p, :], ip[:, n])
                    P.ts("dve", BR[0][p][:, n], PB[0][:, n], PVc[:, p:p + 1], ALU.mult)
                    isb = load_in(0, p)
                    cast(BR[2][p][:, n], isb[:, n])
                    io = load_in(1, p); ibv = load_in(2, p); ig = load_in(3, p)
                    P.mm(PB[1][:, n], ones64[:, :], io[:, n])
                    cen = tf.next()
                    P.stt("dve", cen[:, n], PB[1][:, n], -1.0 / 64, io[:, n], ALU.mult, ALU.add)
                    sq = tf.next(); P.tt("pool", sq[:, n], cen[:, n], cen[:, n], ALU.mult)
                    P.mm(PB[2][:, n], ones64[:, :], sq[:, n])
                    rs = tf.next()
                    P.ts("dve", rs[:, n], PB[2][:, n], 1.0 / 64, ALU.mult, 64e-5, ALU.add)
                    P.act(rs[:, n], rs[:, n], AF.Sqrt)
                    P.recip(rs[:, n], rs[:, n])
                    P.tt("dve", cen[:, n], cen[:, n], rs[:, n], ALU.mult)
                    P.ts("dve", cen[:, n], cen[:, n], PVc[:, 2 + p:3 + p], ALU.mult, PVc[:, 4 + p:5 + p], ALU.add)
                    P.tt("pool", cen[:, n], cen[:, n], ibv[:, n], ALU.add)
                    P.tt("dve", BR[1][p][:, n], cen[:, n], ig[:, n], ALU.mult)
                ge = []
                for p in range(2):
                    iy = load_in(4, p)
                    x2 = tf.next(); P.tt("pool", x2[:, n], iy[:, n], iy[:, n], ALU.mult)
                    P.ts("dve", x2[:, n], x2[:, n], 0.044715, ALU.mult, 1.0, ALU.add)
                    P.stt("dve", x2[:, n], x2[:, n], 0.7978845608028654, iy[:, n], ALU.mult, ALU.mult)
                    P.act(x2[:, n], x2[:, n], AF.Tanh)
                    g_ = eb.next()
                    P.stt("dve", x2[:, n], x2[:, n], 1.0, iy[:, n], ALU.add, ALU.mult)
                    P.ts("dve", g_[:, n], x2[:, n], 0.5, ALU.mult)
                    ge.append(g_)
                for p in range(2):
                    for kc in range(2):
                        P.mm(PB[3][:, n], glu[:, kc, (2 + p) * 128:(3 + p) * 128], ge[kc][:, n], start=(kc == 0), stop=(kc == 1))
                    for kc in range(2):
                        P.mm(PB[4][:, n], glu[:, kc, p * 128:(p + 1) * 128], ge[kc][:, n], start=(kc == 0), stop=(kc == 1))
                    sg = tf.next(); P.act(sg[:, n], PB[3][:, n], AF.Sigmoid)
                    P.stt("dve", BR[3][p][:, n], PB[4][:, n], 1.0, sg[:, n], ALU.mult, ALU.mult)
                hsrc()
                for dt in range(8):
                    acc = accr.next()
                    for nb in range(4):
                        w = wr.next(); P.dma("sp" if nb % 2 else "act", w[:, :], W16["gate_t"][l, nb * 8 + dt])
                        for kc in range(8):
                            P.mm(PB[5][:, n], w[:, kc * 128:(kc + 1) * 128], hTg[:, kc, n], start=(kc == 0), stop=(kc == 7))
                        sg = tf.next(); P.act(sg[:, n], PB[5][:, n], AF.Sigmoid)
                        for kc in range(2):
                            P.mm(PB[6][:, n], Wbr[:, nb, kc, dt * 128:(dt + 1) * 128], BR[nb][kc][:, n], start=(kc == 0), stop=(kc == 1))
                        if nb == 0:
                            P.tt("dve", acc[:, n], sg[:, n], PB[6][:, n], ALU.mult)
                        else:
                            pr = tf.next(); P.tt("dve", pr[:, n], sg[:, n], PB[6][:, n], ALU.mult)
                            if nb < 3:
                                P.tt("pool", acc[:, n], acc[:, n], pr[:, n], ALU.add)
                            else:
                                P.tt("pool", merged[:, dt, n], acc[:, n], pr[:, n], ALU.add)

                if DBGC[0] is not None and ntok == PIECE and not DBGC[1]:
                    DBGC[1] = True
                    for nb in range(4):
                        for p in range(2):
                            P.dma("sp", DBGC[0]["br"][nb * 2 + p], BR[nb][p][:, :])
                    P.dma("sp", DBGC[0]["mg"][:, :], merged[:].rearrange("p k t -> p (k t)"))

                def tok_major_add(lhs_fn, w_fn, nk):
                    for kc in range(nk):
                        w = w_fn(kc)
                        c = 0
                        for ti, nr in enumerate(nrows):
                            for ch in range(2):
                                P.mm(PB[ti * 2 + ch][0:nr, :], lhs_fn(kc)[:, c:c + nr], w[:, ch * 512:(ch + 1) * 512], start=(kc == 0), stop=(kc == nk - 1))
                            c += nr
                    for ti, nr in enumerate(nrows):
                        for ch in range(2):
                            P.tt("dve", xts[ti][0:nr, ch * 512:(ch + 1) * 512], xts[ti][0:nr, ch * 512:(ch + 1) * 512], PB[ti * 2 + ch][0:nr, :], ALU.add)

                def w_stream(name, kc):
                    w = wr.next(); P.dma("sp" if kc % 2 else "act", w[:, :], W16[name][l, kc * 128:(kc + 1) * 128, :])
                    return w
                tok_major_add(lambda kc: merged[:, kc, :], lambda kc: w_stream("w_out", kc), 8)
                c = 0
                for ti, nr in enumerate(nrows):
                    rmsnorm_T(R, xts[ti][0:nr, :], nr, gb["norm_cross"], hT2, c); c += nr
                for p in range(2):
                    for kc in range(8):
                        P.mm(PB[0][:, n], Wq[:, kc, p * 128:(p + 1) * 128], hT2[:, kc, n], start=(kc == 0), stop=(kc == 7))
                    sq = tf.next(); P.act(sq[:, n], PB[0][:, n], AF.Square)
                    P.mm(PB[1][:, n], ones64[:, :], sq[:, n])
                    rs = tf.next()
                    P.ts("dve", rs[:, n], PB[1][:, n], 1.0 / 64, ALU.mult, EPS, ALU.add)
                    P.act(rs[:, n], rs[:, n], AF.Sqrt)
                    P.recip(rs[:, n], rs[:, n])
                    P.stt("dve", qn[p][:, n], PB[0][:, n], PVc[:, 6:7], rs[:, n], ALU.mult, ALU.mult)
                for (mem_loader, cs) in mem_cols:
                    if mem_loader is not None:
                        mem_loader()
                    for p in range(2):
                        for par in range(2):
                            h = 2 * p + par
                            ps_ = slice(par * 64, (par + 1) * 64)
                            for mt in range(2):
                                P.mm(PB[2][:, cs], KmT[ps_, p, mt * 128:(mt + 1) * 128], qn[p][ps_, cs])
                                e = eb.next(); P.act(e[:, cs], PB[2][:, cs], AF.Exp)
                                fst = (par == 0 and mt == 0); lst = (par == 1 and mt == 1)
                                P.mm(PB[3 + p][:, cs], Vpad[:, mt, h, :], e[:, cs], start=fst, stop=lst)
                                P.mm(PB[5 + p][:, cs], onespad[:, par, :], e[:, cs], start=fst, stop=lst)
                for p in range(2):
                    rc = tf.next(); P.recip(rc[:, n], PB[5 + p][:, n])
                    P.tt("dve", xo[p][:, n], PB[3 + p][:, n], rc[:, n], ALU.mult)
                tok_major_add(lambda kc: xo[kc], lambda kc: Wo[:, kc, :], 2)
                c = 0
                for ti, nr in enumerate(nrows):
                    rmsnorm_T(R, xts[ti][0:nr, :], nr, gb["norm_ffn"], hT2, c); c += nr
                for ft in range(22):
                    wg = wr.next(); wu = wr.next()
                    P.dma("sp", wg[:, :], W16["ffg_t"][l, ft]); P.dma("act", wu[:, :], W16["ffu_t"][l, ft])
                    pg = PB[(ft % 2) * 2]; pu = PB[(ft % 2) * 2 + 1]
                    for kc in range(8):
                        P.mm(pg[:, n], wg[:, kc * 128:(kc + 1) * 128], hT2[:, kc, n], start=(kc == 0), stop=(kc == 7))
                    for kc in range(8):
                        P.mm(pu[:, n], wu[:, kc * 128:(kc + 1) * 128], hT2[:, kc, n], start=(kc == 0), stop=(kc == 7))
                    sg = tf.next(); P.act(sg[:, n], pg[:, n], AF.Silu)
                    P.tt("dve", actT[:, ft, n], sg[:, n], pu[:, n], ALU.mult)
                tok_major_add(lambda kc: actT[:, kc, :], lambda kc: w_stream("ffn_w_down", kc), 22)
                xdst()

            for G in range(NPC):
                for t in range(4):
                    P.dma("sp" if t % 2 else "act", xt[t][:], xsrc(l, G)[t * 128:(t + 1) * 128, :])

                def hsrc(G=G):
                    P.dma("sp", hTg[:].rearrange("p k t -> p (k t)"), HD[G][:, :])

                def xdst(G=G):
                    for t in range(4):
                        P.dma("sp" if t % 2 else "act", XD[G][t * 128:(t + 1) * 128, :], xt[t][:])
                mem = [((lambda: load_mem(o_mk[l], o_mv[l])) if G == 0 else None, slice(0, PIECE))]
                proc([x[:, :] for x in xt], [128] * 4, PIECE, hsrc, lambda qq, G=G: OB[qq][G], mem, xdst)

            def hsrc_s():
                P.cp("pool", hTg[:, :, 0:STOK], hTs[:, :, :])
            mem_s = [((lambda b=b: load_mem(D["cmk"][l, b], D["cmv"][l, b])), slice(b * ST, (b + 1) * ST)) for b in range(SPC)]
            proc([XS[:, :]], [STOK], STOK, hsrc_s, lambda qq: OBS[qq], mem_s, lambda: None)
            phase_end(ph)

        def phase_Bs(l):
            ph = phase_begin()
            NG = NPAGE
            w3 = W16["w_in"][l].rearrange("(k p) c -> p k c", p=128)
            Wsb = P.sb([128, 8, 768], BF16, "Wsb"); P.dma("sp", Wsb[:], w3[:, :, OFF_SB:OFF_SB + 768])
            onesf = P.sb([128, 128], F32, "onesf"); P.memset("pool", onesf[:], 1.0)
            ones64 = C["c_ones64"]
            PVs = P.sb([128, 16], F32, "PVs")
            for par in range(2):
                col(PVs[par * 64:(par + 1) * 64, 0:1], D["sb_q_norm"][l, :]); col(PVs[par * 64:(par + 1) * 64, 1:2], D["sb_k_norm"][l, :])
            P.ts("dve", PVs[:, 0:1], PVs[:, 0:1], 0.125, ALU.mult)
            P.dma("sp", PVs[:, 4:8], D["sb_bias"][l:l + 1, :].pbc(128))
            QKV = P.sb([128, 6, STOK], F32, "QKV")
            for i in range(6):
                for kc in range(8):
                    P.mm(PB[i % 2][:, 0:STOK], Wsb[:, kc, i * 128:(i + 1) * 128], hTs[:, kc, :], start=(kc == 0), stop=(kc == 7))
                if i < 4:
                    sq = P.sb([128, STOK], F32, "sqs%d" % i); rs = P.sb([128, STOK], F32, "rss%d" % i)
                    P.act(sq[:, :], PB[i % 2][:, 0:STOK], AF.Square)
                    P.mm(PB[2][:, 0:STOK], ones64[:, :], sq[:, :])
                    P.ts("dve", rs[:, :], PB[2][:, 0:STOK], 1.0 / 64, ALU.mult, EPS, ALU.add)
                    P.act(rs[:, :], rs[:, :], AF.Sqrt)
                    P.recip(rs[:, :], rs[:, :])
                    P.stt("dve", QKV[:, i, :], PB[i % 2][:, 0:STOK], PVs[:, (0 if i < 2 else 1):(1 if i < 2 else 2)], rs[:, :], ALU.mult, ALU.mult)
                else:
                    P.cp("dve", QKV[:, i, :], PB[i % 2][:, 0:STOK])
            Zt = P.sb([128, 32, 128], F32, "Zt"); SPt = P.sb([128, 32, 128], F32, "SPt")
            INC = P.sb([128, 32, 128], F32, "INC")
            m0s = P.sb([128, 32, 128], F32, "m0s"); P.memset("pool", m0s[:], 1.0); P.memset("pool", m0s[:, :, 0:1], 0.0)
            kvr = Ring(P, [128, 16, 256], F32, 2, "kvr")
            KT = Ring(P, [128, 512], F32, 3, "KTs")
            idx = P.sb([128, 1], I32, "idx"); idxf = P.sb([128, 1], F32, "idxf")
            idxr = Ring(P, [128, 1], I32, 4, "idxr")
            Qblk = P.sb([128, 2, 16], F32, "Qblk")
            sm = {n: P.sb([128, 32], F32, "sm_" + n) for n in ("zb", "e1", "sp", "aft", "cn", "rs", "cg", "cgb", "attn")}
            vnew = P.sb([8, 256], F32, "vnew")
            osb = P.sb([128, 2, ST], F32, "osb")
            ckv = D["ck"][:, :]; cvv = D["cv"][:, :]
            for b in range(SPC):
                bs = slice(b * ST, (b + 1) * ST)
                P.dma("sp", idx[0:NG, :], D["ptab"][b, :].rearrange("(p o) -> p o", o=1))
                P.cp("dve", idxf[0:NG, :], idx[0:NG, :])
                P.ts("dve", idxf[0:NG, :], idxf[0:NG, :], 8.0, ALU.mult, float(l * NPOOL * 8), ALU.add)
                P.memset("pool", Qblk[:], 0.0)
                for hp in range(2):
                    P.cp("dve", Qblk[0:64, hp, 0:8], QKV[0:64, hp, bs])
                    P.cp("dve", Qblk[64:128, hp, 8:16], QKV[64:128, hp, bs])
                for hp in range(2):
                    P.mm(PB[3][0:ST, hp * 16:(hp + 1) * 16], QKV[:, 2 + hp, bs], Qblk[:, hp, :])
                for h in range(4):
                    P.ts("dve", sm["zb"][0:ST, h * 8:(h + 1) * 8], PB[3][0:ST, h * 8:(h + 1) * 8], PVs[0:ST, 4 + h:5 + h], ALU.add)
                P.act(sm["e1"][0:ST, :], sm["zb"][0:ST, :], AF.Exp)
                P.act(sm["sp"][0:ST, :], sm["e1"][0:ST, :], AF.Ln, bias=1.0)
                P.tt("dve", sm["sp"][0:ST, :], sm["sp"][0:ST, :], C["c_newmask"][:, :], ALU.mult)
                P.mm(PB[4][0:ST, 0:32], C["c_tri"][0:ST, 0:ST], sm["sp"][0:ST, :])
                P.mm(PB[5][:, 0:32], onesf[0:ST, :], sm["sp"][0:ST, :])
                P.cp("dve", sm["cn"][:, :], PB[5][:, 0:32])
                P.tt("dve", sm["aft"][0:ST, :], sm["zb"][0:ST, :], sm["sp"][0:ST, :], ALU.subtract)
                P.tt("dve", sm["aft"][0:ST, :], sm["aft"][0:ST, :], PB[4][0:ST, 0:32], ALU.subtract)
                P.act(sm["attn"][0:ST, :], sm["aft"][0:ST, :], AF.Exp)
                P.tt("dve", sm["attn"][0:ST, :], sm["attn"][0:ST, :], C["c_newmask"][:, :], ALU.mult)
                for si in range(7, -1, -1):
                    kb = kvr.next(); ix = idxr.next()
                    P.ts("dve", ix[0:NG, :], idxf[0:NG, :], float(si), ALU.add)
                    P.op("pool", lambda e, kb=kb, ix=ix: e.indirect_dma_start(
                        out=kb.t[0:NG, :, :].rearrange("p t c -> p (t c)"), out_offset=None,
                        in_=ckv.ap,
                        in_offset=bass.IndirectOffsetOnAxis(ap=ix.t[0:NG, :], axis=0)), [ix, ckv.buf], [kb], dma=True)
                    for tl in range(15, -1, -1):
                        jl = 15 - tl
                        kt = KT.next() if jl % 2 == 0 else kt
                        for hp in range(2):
                            P.tr(PB[0 + (jl % 2)][:, hp * 128:hp * 128 + NG], kb[0:NG, tl, hp * 128:(hp + 1) * 128], ident[0:NG, 0:NG])
                        cast(kt[:, (jl % 2) * 256:(jl % 2) * 256 + 256], PB[0 + (jl % 2)][:, 0:256], pool_ok=False)
                        for hp in range(2):
                            P.mm(PB[2][0:NG, jl * 32 + hp * 16:jl * 32 + (hp + 1) * 16], kt[:, (jl % 2) * 256 + hp * 128:(jl % 2) * 256 + hp * 128 + NG], Qblk[:, hp, :])
                    j0 = (7 - si) * 16
                    P.cp("dve", Zt[0:NG, :, j0:j0 + 16], PB[2][0:NG, :].rearrange("p (j h) -> p h j", h=32))
                g_ = slice(0, NG)
                for h in range(4):
                    hs = slice(h * 8, (h + 1) * 8)
                    P.act(SPt[g_, hs, :], Zt[g_, hs, :], AF.Exp, bias=PVs[g_, 4 + h:5 + h])
                P.act(SPt[g_, :, :], SPt[g_, :, :], AF.Ln, bias=1.0)
                fl = lambda v: v.rearrange("p h j -> p (h j)")
                P.scan(fl(INC[g_, :, :]), fl(m0s[g_, :, :]), fl(SPt[g_, :, :]), 0.0)
                P.cp("dve", sm["rs"][g_, :], INC[g_, :, 127])
                P.mm(PB[4][g_, 0:32], C["c_tri"][g_, g_], sm["rs"][g_, :])
                P.tt("dve", sm["cg"][g_, :], sm["cn"][g_, :], PB[4][g_, 0:32], ALU.add)
                for h in range(4):
                    P.ts("dve", sm["cgb"][g_, h * 8:(h + 1) * 8], sm["cg"][g_, h * 8:(h + 1) * 8], -1.0, ALU.mult, PVs[g_, 4 + h:5 + h], ALU.add)
                P.tt("pool", INC[g_, :, :], Zt[g_, :, :], INC[g_, :, :], ALU.subtract)
                for hq in range(32):
                    P.act(Zt[g_, hq, :], INC[g_, hq, :], AF.Exp, bias=sm["cgb"][g_, hq:hq + 1])
                P.dma("sp", vnew[:, :], o_sbvs[l, bs, :])
                first = True
                for si in range(7, -1, -1):
                    vb = kvr.next(); ix = idxr.next()
                    P.ts("dve", ix[0:NG, :], idxf[0:NG, :], float(si), ALU.add)
                    P.op("pool", lambda e, vb=vb, ix=ix: e.indirect_dma_start(
                        out=vb.t[0:NG, :, :].rearrange("p t c -> p (t c)"), out_offset=None,
                        in_=cvv.ap,
                        in_offset=bass.IndirectOffsetOnAxis(ap=ix.t[0:NG, :], axis=0)), [ix, cvv.buf], [vb], dma=True)
                    for tl in range(15, -1, -1):
                        j = (7 - si) * 16 + (15 - tl)
                        for p in range(2):
                            P.mm(PB[5 + p][:, 0:32], vb[0:NG, tl, p * 128:(p + 1) * 128], Zt[0:NG, :, j], start=first, stop=False)
                        first = False
                for p in range(2):
                    P.mm(PB[5 + p][:, 0:32], vnew[0:ST, p * 128:(p + 1) * 128], sm["attn"][0:ST, :], start=False, stop=True)
                    for par in range(2):
                        h = 2 * p + par
                        P.cp("dve", osb[par * 64:(par + 1) * 64, p, :], PB[5 + p][par * 64:(par + 1) * 64, h * 8:(h + 1) * 8])
                for h in range(4):
                    P.dma("sp", OBS[h][0:64, bs], osb[(h % 2) * 64:(h % 2) * 64 + 64, h // 2, :])
            phase_end(ph)

        DBG = "dbg" in STAGES
        if DBG:
            DBGC[0] = {"br": OUT("o_dbg_br", [8, 128, PIECE], BF16), "mg": OUT("o_dbg_mg", [128, 8 * PIECE], BF16)}
            d_ob = OUT("o_dbg_ob", [4, NPC, 384, PIECE]); d_x0 = OUT("o_dbg_x0", [SEQ, DM]); d_obs = OUT("o_dbg_obs", [4, 384, STOK])
        for l in range(L):
            phase_A(l)
            if "B" in STAGES:
                phase_B(l)
            if "Bs" in STAGES:
                phase_Bs(l)
            if DBG and l == 0:
                for q in range(4):
                    for G in range(NPC):
                        P.dma("sp", d_ob[q, G], OB[q][G][:, :])
                    P.dma("sp", d_obs[q], OBS[q][:, :])
            if "C" in STAGES:
                phase_C(l)
            if DBG and l == 0:
                for G in range(NPC):
                    P.dma("sp", d_x0[G * PIECE:(G + 1) * PIECE, :], XD[G][:, :])
        for G in range(NPC):
            P.dma("sp" if G % 2 else "act", o_y[G * PIECE:(G + 1) * PIECE, :], (XD[G][:, :] if "C" in STAGES else xp[G * PIECE:(G + 1) * PIECE, :]))
        P.dma("sp", o_ys[:, :], XS[:, :])
        counts = P.emit()
    return nc, counts


_CACHE = {}
ALL_STAGES = ("B", "Bs", "C", "sb", "rwkv", "s5")


def kernel(**inp):
    f32 = np.float32
    SEQ = inp["x_prompt"].shape[1]
    NPAGE = inp["page_table"].shape[1]
    NPOOL = inp["cache_sb_k"].shape[1]
    stages = tuple(os.environ.get("MK_STAGES", ",".join(ALL_STAGES)).split(","))
    key = (SEQ, NPAGE, NPOOL, stages)
    if key not in _CACHE:
        _CACHE[key] = build_program(SEQ, NPAGE, NPOOL, stages)
    nc, counts = _CACHE[key]
    A = lambda n: np.ascontiguousarray(np.asarray(inp[n], f32))
    shared = {"xp": A("x_prompt").reshape(SEQ, DM), "mem_prompt": A("mem_prompt").reshape(256, DM)}
    shared.update(host_consts())
    for n, s in WEIGHT_SHAPES.items():
        shared[n] = A(n).reshape(s)
    shared["ck"] = A("cache_sb_k").reshape(L * NPOOL * 8, 4096)
    shared["cv"] = A("cache_sb_v").reshape(L * NPOOL * 8, 4096)
    xs = A("x_sample").reshape(32 * ST, DM)
    in_maps = []
    for c in range(NCORE):
        b0, b1 = c * SPC, (c + 1) * SPC
        m = dict(shared)
        m["xs"] = np.ascontiguousarray(xs[c * STOK:(c + 1) * STOK])
        m["st_pool"] = np.ascontiguousarray(A("state_pool")[:, b0:b1])
        m["st_shift"] = np.ascontiguousarray(A("state_rwkv_shift")[:, b0:b1])
        m["st_wkv"] = np.ascontiguousarray(A("state_rwkv_wkv")[:, b0:b1])
        m["st_s5r"] = np.ascontiguousarray(A("state_s5_re")[:, b0:b1].reshape(L, SPC, 1024))
        m["st_s5i"] = np.ascontiguousarray(A("state_s5_im")[:, b0:b1].reshape(L, SPC, 1024))
        m["cmk"] = np.ascontiguousarray(A("cache_mem_k")[:, b0:b1].reshape(L, SPC, 256, 256))
        m["cmv"] = np.ascontiguousarray(A("cache_mem_v")[:, b0:b1].reshape(L, SPC, 256, 256))
        m["ptab"] = np.ascontiguousarray(np.asarray(inp["page_table"], np.int32)[b0:b1])
        in_maps.append(m)
    res = run_bass_kernel_spmd(nc, in_maps, core_ids=list(range(NCORE))).results
    cat = lambda k, ax: np.concatenate([r[k] for r in res], axis=ax)
    r0 = res[0]
    global _DBG
    _DBG = {k: [r[k] for r in res] for k in r0 if k.startswith("o_dbg")}
    return (r0["o_y"].reshape(1, SEQ, DM), cat("o_ys", 0).reshape(32, ST, DM),
            r0["o_sbk"].reshape(L, 1, SEQ, 4, 64), r0["o_sbv"].reshape(L, 1, SEQ, 4, 64),
            r0["o_mk"].reshape(L, 1, 256, 4, 64), r0["o_mv"].reshape(L, 1, 256, 4, 64),
            r0["o_pool"].reshape(L, 1, 15, 256), r0["o_shift"].reshape(L, 1, 1024),
            r0["o_wkv"].reshape(L, 1, 4, 64, 64), r0["o_s5r"].reshape(L, 1, 16, 64), r0["o_s5i"].reshape(L, 1, 16, 64),
            cat("o_sbks", 1).reshape(L, 32, ST, 4, 64), cat("o_sbvs", 1).reshape(L, 32, ST, 4, 64),
            cat("o_pools", 1), cat("o_shifts", 1), cat("o_wkvs", 1),
            cat("o_s5rs", 1).reshape(L, 32, 16, 64), cat("o_s5is", 1).reshape(L, 32, 16, 64))
```

```python
import os
import numpy as np
from contextlib import ExitStack
import concourse.bass as bass
import concourse.mybir as mybir
from concourse.bass_utils import run_bass_kernel_spmd

F32 = mybir.dt.float32
BF16 = mybir.dt.bfloat16
I32 = mybir.dt.int32
AF = mybir.ActivationFunctionType
ALU = mybir.AluOpType
AX = mybir.AxisListType


class View:
    __slots__ = ("buf", "ap")

    def __init__(self, buf, ap):
        self.buf = buf
        self.ap = ap

    def __getitem__(self, k):
        return View(self.buf, self.ap[k])

    def rearrange(self, s, **kw):
        return View(self.buf, self.ap.rearrange(s, **kw))

    def pbc(self, n):
        return View(self.buf, self.ap.partition_broadcast(n))

    def bc(self, shape):
        return View(self.buf, self.ap.to_broadcast(shape))


class Buf:
    __slots__ = ("name", "t", "lw", "rd")

    def __init__(self, name, t=None):
        self.name = name
        self.t = t
        self.lw = None
        self.rd = []

    def __getitem__(self, k):
        return View(self, self.t[k])


class Op:
    __slots__ = ("eng", "fn", "deps", "dma", "sem", "val", "need", "waits")


def _ap(x):
    return x.ap if isinstance(x, View) else x


class Prog:
    ENG = ("pe", "act", "dve", "pool", "sp")
    R = 8

    def __init__(self, nc, es):
        self.nc = nc
        self.es = es
        self.ges = es
        self.ops = {e: [] for e in self.ENG}
        self.ndma = {e: 0 for e in self.ENG}
        self.dma_last = {}
        self.csem = {e: es.enter_context(nc.semaphore("c_" + e)) for e in ("pe", "act", "dve", "pool")}
        self.dsem = {e: [es.enter_context(nc.semaphore("d_%s%d" % (e, i))) for i in range(self.R)]
                     for e in ("sp", "act", "pool")}
        self.nbuf = 0
        self.last = {e: None for e in self.ENG}

    def sb(self, shape, dt=F32, name=None):
        self.nbuf += 1
        name = (name or "sb") + "_%d" % self.nbuf
        t = self.es.enter_context(self.nc.sbuf_tensor(name, list(shape), dt))
        return Buf(name, t)

    def ps(self, shape, dt=F32, name=None):
        self.nbuf += 1
        name = (name or "ps") + "_%d" % self.nbuf
        t = self.es.enter_context(self.nc.psum_tensor(name, list(shape), dt))
        return Buf(name, t)

    def dram(self, name, shape, dt=F32, kind="Internal"):
        t = self.nc.dram_tensor(name, list(shape), dt, kind=kind)
        return Buf(name, t.ap())

    def op(self, eng, fn, reads=(), writes=(), dma=False, extra=()):
        o = Op()
        o.eng, o.fn, o.dma = eng, fn, dma
        o.need = dma
        o.sem = None
        o.val = 0
        deps = [(d, True) for d in extra]
        for b in reads:
            if b.lw is not None:
                deps.append((b.lw, True))
        for b in writes:
            if b.lw is not None:
                deps.append((b.lw, False))
            for r in b.rd:
                deps.append((r, False))
        if dma:
            n = self.ndma[eng]
            self.ndma[eng] = n + 1
            key = (eng, n % self.R)
            prev = self.dma_last.get(key)
            if prev is not None:
                deps.append((prev, True))
            self.dma_last[key] = o
            o.sem = self.dsem[eng][n % self.R]
            o.val = 16 * (n // self.R + 1)
        o.deps = []
        for (d, raw) in deps:
            if d is o:
                continue
            if (not d.dma) and d.eng == eng:
                if eng == "pe" or not raw:
                    continue
            d.need = True
            o.deps.append(d)
        for b in reads:
            b.rd.append(o)
        for b in writes:
            b.lw = o
            b.rd = []
        self.ops[eng].append(o)
        if not dma:
            self.last[eng] = o
        return o

    def _rw(self, outs, ins):
        return [x.buf for x in ins if isinstance(x, View)], [x.buf for x in outs if isinstance(x, View)]

    def dma(self, eng, out, in_, **kw):
        kw.setdefault("allow_slow_non_contiguous", True)
        r, w = self._rw([out], [in_])
        return self.op(eng, lambda e: e.dma_start(out=out.ap, in_=in_.ap, **kw), r, w, dma=True)

    def mm(self, out, lhsT, rhs, start=True, stop=True):
        r, w = self._rw([out], [lhsT, rhs])
        return self.op("pe", lambda e: e.matmul(out.ap, lhsT=lhsT.ap, rhs=rhs.ap, start=start, stop=stop), r, w)

    def tr(self, out, in_, ident):
        r, w = self._rw([out], [in_, ident])
        return self.op("pe", lambda e: e.transpose(out.ap, in_.ap, ident.ap), r, w)

    def act(self, out, in_, func, bias=0.0, scale=1.0, accum=None):
        r, w = self._rw([out, accum], [in_, bias, scale])
        kw = {}
        if accum is not None:
            kw["accum_out"] = accum.ap
        return self.op("act", lambda e: e.activation(out=out.ap, in_=in_.ap, func=func, bias=_ap(bias), scale=_ap(scale), **kw), r, w)

    def tt(self, eng, out, a, b, op):
        r, w = self._rw([out], [a, b])
        return self.op(eng, lambda e: e.tensor_tensor(out=out.ap, in0=a.ap, in1=b.ap, op=op), r, w)

    def ts(self, eng, out, a, s1, op0, s2=None, op1=None):
        r, w = self._rw([out], [a, s1, s2])
        if op1 is None:
            return self.op(eng, lambda e: e.tensor_scalar(out=out.ap, in0=a.ap, scalar1=_ap(s1), scalar2=None, op0=op0), r, w)
        return self.op(eng, lambda e: e.tensor_scalar(out=out.ap, in0=a.ap, scalar1=_ap(s1), scalar2=_ap(s2), op0=op0, op1=op1), r, w)

    def stt(self, eng, out, a, s, b, op0, op1):
        r, w = self._rw([out], [a, s, b])
        return self.op(eng, lambda e: e.scalar_tensor_tensor(out=out.ap, in0=a.ap, scalar=_ap(s), in1=b.ap, op0=op0, op1=op1), r, w)

    def cp(self, eng, out, in_):
        r, w = self._rw([out], [in_])
        if eng == "act":
            return self.op("act", lambda e: e.copy(out=out.ap, in_=in_.ap), r, w)
        return self.op(eng, lambda e: e.tensor_copy(out=out.ap, in_=in_.ap), r, w)

    def memset(self, eng, out, val):
        r, w = self._rw([out], [])
        return self.op(eng, lambda e: e.memset(out.ap, val), r, w)

    def recip(self, out, in_):
        r, w = self._rw([out], [in_])
        return self.op("dve", lambda e: e.reciprocal(out=out.ap, in_=in_.ap), r, w)

    def scan(self, out, d0, d1, init):
        r, w = self._rw([out], [d0, d1, init])
        return self.op("dve", lambda e: e.tensor_tensor_scan(out=out.ap, data0=d0.ap, data1=d1.ap, initial=_ap(init), op0=ALU.mult, op1=ALU.add), r, w)

    def barrier(self, tiles):
        prev = [o for o in self.last.values() if o is not None] + list(self.dma_last.values())
        a = []
        a.append(self.op("pool", lambda e: e.memset(tiles[0].t[0:1, 0:1], 0.0), [], [tiles[0]], extra=prev))
        for i, eng in enumerate(("dve", "act", "pe", "sp")):
            b = tiles[i + 1]
            if eng == "dve":
                o = self.op(eng, lambda e, b=b: e.memset(b.t[0:1, 0:1], 0.0), [tiles[0]], [b])
            elif eng == "act":
                o = self.op(eng, lambda e, b=b: e.copy(out=b.t[0:1, 0:1], in_=tiles[0].t[0:1, 0:1]), [tiles[0]], [b])
            elif eng == "pe":
                o = self.op(eng, lambda e, b=b: e.matmul(tiles[5].t[0:1, 0:1], lhsT=tiles[0].t[0:1, 0:1], rhs=tiles[0].t[0:1, 0:1], start=True, stop=True), [tiles[0]], [tiles[5]])
            else:
                o = self.op(eng, lambda e, b=b: e.dma_start(out=b.t[0:1, 0:1], in_=tiles[0].t[0:1, 0:1]), [tiles[0]], [b], dma=True)
            a.append(o)

    def emit(self):
        nc = self.nc
        for e in ("pe", "act", "dve", "pool"):
            c = 0
            for o in self.ops[e]:
                if not o.dma and o.need:
                    c += 1
                    o.sem = self.csem[e]
                    o.val = c
            assert c < 2 ** 30, (e, c)
        finals = [(o.sem, o.val) for o in self.dma_last.values()]
        for e in self.ENG:
            seen = {}
            for o in self.ops[e]:
                w = {}
                for d in o.deps:
                    k = d.sem
                    if seen.get(k.num, 0) >= d.val:
                        continue
                    if k.num not in w or w[k.num][1] < d.val:
                        w[k.num] = (k, d.val)
                for kn, (k, v) in w.items():
                    seen[kn] = v
                o.waits = list(w.values())
        with nc.Block() as block:
            def body(ename):
                def f(eng):
                    for o in self.ops[ename]:
                        for (s, v) in o.waits:
                            eng.wait_ge(s, v)
                        ins = o.fn(eng)
                        if o.dma:
                            ins.then_inc(o.sem, 16)
                        elif o.need:
                            ins.then_inc(o.sem, 1)
                    if ename == "sp":
                        for (s, v) in finals:
                            eng.wait_ge(s, v)
                return f
            block.tensor(body("pe"))
            block.scalar(body("act"))
            block.vector(body("dve"))
            block.gpsimd(body("pool"))
            block.sync(body("sp"))
        return {e: len(self.ops[e]) for e in self.ENG}

L = 2
DM = 1024
NCORE = 8
SPC = 4
ST = 8
STOK = SPC * ST
N_IN = 6400
OFF_POOL, OFF_RWKV, OFF_SB, OFF_S5, OFF_GATE = 0, 256, 1280, 2048, 2304
DFF = 2816
EPS = 1e-6
PIECE = 512
SUB = 256
CH = 64
WINS = (2, 4, 8, 16)
DECAY_C = -0.6065306597126334
TWO_PI = 6.283185307179586


class Ring:
    def __init__(self, P, shape, dt, n, name):
        self.b = [P.sb(shape, dt, "%s%d" % (name, i)) for i in range(n)]
        self.i = 0

    def next(self):
        b = self.b[self.i % len(self.b)]
        self.i += 1
        return b


def host_consts():
    f = np.float32
    c = {}
    c["c_ident"] = np.eye(128, dtype=f)
    o64 = np.zeros((128, 128), f); o64[:64, :64] = 1; o64[64:, 64:] = 1
    c["c_ones64"] = o64
    i = np.arange(128)
    c["c_tri"] = (i[:, None] > i[None, :]).astype(f)
    j = np.arange(64)
    su = (j[:, None] < j[None, :]).astype(f); iu = (j[:, None] <= j[None, :]).astype(f)
    c["c_msu"] = np.ascontiguousarray(np.broadcast_to(su[:, None, :], (64, 8, 64)))
    c["c_miu"] = np.ascontiguousarray(np.broadcast_to(iu[:, None, :], (64, 8, 64)))
    c["c_msl"] = np.ascontiguousarray(np.broadcast_to(su.T[:, None, :], (64, 8, 64)))
    c["c_id8"] = np.ascontiguousarray(np.broadcast_to(np.eye(64, dtype=f)[:, None, :], (64, 8, 64)))
    p = np.arange(128)[:, None, None]; d = np.arange(4)[None, :, None]; q = np.arange(512)[None, None, :]
    c["c_sbmask"] = ((128 * d + p) < q).astype(f)
    c["c_iota"] = np.ascontiguousarray(np.broadcast_to(np.arange(1, SUB + 1, dtype=f)[None, :], (128, SUB)))
    row = np.arange(64)
    selw = np.zeros((64, 4), f); rc = np.zeros((64, 4, 16), f)
    for g, w in enumerate(WINS):
        sel = (row // 16 == g).astype(f)
        selw[:, g] = sel / w
        rc[:, g, :] = sel[:, None] / np.minimum(np.arange(16) + 1, w)[None, :]
    c["c_selw"] = selw; c["c_rc16"] = rc
    s = np.arange(8)[:, None]; hq = np.arange(32)[None, :]
    c["c_newmask"] = (s < (hq % 8)).astype(f)
    return c

CONST_SHAPES = {"c_ident": [128, 128], "c_ones64": [128, 128], "c_tri": [128, 128], "c_msu": [64, 8, 64],
                "c_miu": [64, 8, 64], "c_msl": [64, 8, 64], "c_id8": [64, 8, 64], "c_sbmask": [128, 4, 512],
                "c_iota": [128, SUB], "c_selw": [64, 4], "c_rc16": [64, 4, 16], "c_newmask": [8, 32]}

WEIGHT_SHAPES = {
    "norm_mix": [L, DM], "norm_cross": [L, DM], "norm_mem": [L, DM], "norm_ffn": [L, DM],
    "w_in": [L, DM, N_IN], "pool_w": [L, 4, 64, 64], "pool_scale": [L, 256], "rwkv_mu": [L, 1024],
    "rwkv_w0": [L, 256], "rwkv_w_up": [L, 64, 256], "rwkv_a0": [L, 256], "rwkv_a_up": [L, 64, 256],
    "rwkv_g_up": [L, 128, 256], "rwkv_k_k": [L, 256], "rwkv_k_a": [L, 256], "rwkv_r_k": [L, 256],
    "rwkv_ln_w": [L, 256], "rwkv_ln_b": [L, 256], "sb_q_norm": [L, 64], "sb_k_norm": [L, 64], "sb_bias": [L, 4],
    "s5_a_re": [L, 1024], "s5_a_im": [L, 1024], "s5_log_dt": [L, 16], "s5_b_re": [L, 1024, 16], "s5_b_im": [L, 1024, 16],
    "s5_c_re": [L, 16, 16, 64], "s5_c_im": [L, 16, 16, 64], "s5_d": [L, 256], "s5_w_glu": [L, 256, 512],
    "w_branch": [L, 4, 256, DM], "w_out": [L, DM, DM], "xa_w_q": [L, DM, 256], "xa_w_k": [L, DM, 256],
    "xa_w_v": [L, DM, 256], "xa_q_norm": [L, 64], "xa_k_norm": [L, 64], "xa_w_o": [L, 256, DM],
    "ffn_w_gate": [L, DM, DFF], "ffn_w_up": [L, DM, DFF], "ffn_w_down": [L, DFF, DM],
}


def build_program(SEQ, NPAGE, NPOOL, STAGES):
    NPC = SEQ // PIECE
    nc = bass.Bass("TRN2", target_bir_lowering=False)
    ges = ExitStack()
    with ges:
        P = Prog(nc, ges)
        D = {}
        DBGC = [None, False]

        def IN(n, s, dt=F32):
            D[n] = P.dram(n, s, dt, kind="ExternalInput")
            return D[n]

        def OUT(n, s, dt=F32):
            D[n] = P.dram(n, s, dt, kind="ExternalOutput")
            return D[n]

        xp = IN("xp", [SEQ, DM]); xs = IN("xs", [STOK, DM])
        for n, s in CONST_SHAPES.items():
            IN(n, s)
        for n, s in WEIGHT_SHAPES.items():
            IN(n, s)
        IN("mem_prompt", [256, DM])
        IN("st_pool", [L, SPC, 15, 256]); IN("st_shift", [L, SPC, 1024]); IN("st_wkv", [L, SPC, 4, 64, 64])
        IN("st_s5r", [L, SPC, 1024]); IN("st_s5i", [L, SPC, 1024])
        IN("cmk", [L, SPC, 256, 256]); IN("cmv", [L, SPC, 256, 256])
        IN("ptab", [SPC, NPAGE], I32)
        IN("ck", [L * NPOOL * 8, 4096]); IN("cv", [L * NPOOL * 8, 4096])
        o_y = OUT("o_y", [SEQ, DM]); o_ys = OUT("o_ys", [STOK, DM])
        o_sbk = OUT("o_sbk", [L, SEQ, 256]); o_sbv = OUT("o_sbv", [L, SEQ, 256])
        o_mk = OUT("o_mk", [L, 256, 256]); o_mv = OUT("o_mv", [L, 256, 256])
        o_pool = OUT("o_pool", [L, 15, 256]); o_shift = OUT("o_shift", [L, 1, 1024])
        o_wkv = OUT("o_wkv", [L, 4, 64, 64]); o_s5r = OUT("o_s5r", [L, 1024]); o_s5i = OUT("o_s5i", [L, 1024])
        o_sbks = OUT("o_sbks", [L, STOK, 256]); o_sbvs = OUT("o_sbvs", [L, STOK, 256])
        o_pools = OUT("o_pools", [L, SPC, 15, 256]); o_shifts = OUT("o_shifts", [L, SPC, 1024])
        o_wkvs = OUT("o_wkvs", [L, SPC, 4, 64, 64]); o_s5rs = OUT("o_s5rs", [L, SPC, 1024]); o_s5is = OUT("o_s5is", [L, SPC, 1024])

        xd_t = nc.dram_tensor("XD", [SEQ, DM], F32, kind="Internal").ap()
        XD = [Buf("XD%d" % p, xd_t[p * PIECE:(p + 1) * PIECE, :]) for p in range(NPC)]
        hd_t = nc.dram_tensor("HD", [NPC, 128, 8 * PIECE], BF16, kind="Internal").ap()
        HD = [Buf("HD%d" % p, hd_t[p]) for p in range(NPC)]
        ob_t = nc.dram_tensor("OB", [4, NPC, 384, PIECE], F32, kind="Internal").ap()
        OB = [[Buf("OB%d_%d" % (q, p), ob_t[q, p]) for p in range(NPC)] for q in range(4)]
        obs_t = nc.dram_tensor("OBS", [4, 384, STOK], F32, kind="Internal").ap()
        OBS = [Buf("OBS%d" % q, obs_t[q]) for q in range(4)]
        W16 = {}
        for n in ("w_in", "w_branch", "w_out", "xa_w_q", "xa_w_k", "xa_w_v", "xa_w_o", "ffn_w_down", "s5_w_glu"):
            W16[n] = P.dram(n + "16", WEIGHT_SHAPES[n], BF16)
        W16["gate_t"] = P.dram("gate_t16", [L, 32, 128, 1024], BF16)
        W16["ffg_t"] = P.dram("ffg_t16", [L, 22, 128, 1024], BF16)
        W16["ffu_t"] = P.dram("ffu_t16", [L, 22, 128, 1024], BF16)

        PB = [P.ps([128, 512], F32, "PB%d" % i) for i in range(8)]
        bar = [P.sb([1, 8], F32, "bar%d" % i) for i in range(5)] + [PB[7]]
        C = {}
        for n, s in CONST_SHAPES.items():
            C[n] = P.sb(s, F32, n)
            P.dma("sp", C[n][:], D[n][:])
        ident = C["c_ident"]
        trib = P.sb([128, 128], BF16, "trib"); onesb = P.sb([128, 128], BF16, "onesb")
        P.cp("dve", trib[:], C["c_tri"][:])
        P.memset("pool", onesb[:], 1.0)
        m0 = P.sb([64, 512], F32, "m0")
        P.memset("pool", m0[:], 1.0)
        P.memset("pool", m0[:].rearrange("p (c j) -> p c j", j=64)[:, :, 0:1], 0.0)
        XS = P.sb([STOK, DM], F32, "XS")
        P.dma("sp", XS[:], xs[:])
        hTs = P.sb([128, 8, STOK], BF16, "hTs")
        cast_i = [0]

        def cast(out, in_, pool_ok=True):
            e = ("dve", "act", "pool")[cast_i[0] % 3] if pool_ok else ("dve", "act")[cast_i[0] % 2]
            cast_i[0] += 1
            P.cp(e, out, in_)

        def col(dst, vec):
            P.dma("sp", dst, vec.rearrange("(p o) -> p o", o=1))

        def phase_begin():
            ph = ExitStack()
            P.es = ph
            return ph

        def phase_end(ph):
            P.barrier(bar)
            ph.close()
            P.es = ges

        ph = phase_begin()
        stg = Ring(P, [128, 2048], F32, 3, "stg"); stg16 = Ring(P, [128, 2048], BF16, 3, "stg16")

        def precast(dst, src, R, Cc):
            for r0 in range(0, R, 128):
                rr = min(128, R - r0)
                for c0 in range(0, Cc, 2048):
                    cw = min(2048, Cc - c0)
                    a = stg.next(); b = stg16.next()
                    P.dma("sp", a[0:rr, 0:cw], src[r0:r0 + rr, c0:c0 + cw])
                    cast(b[0:rr, 0:cw], a[0:rr, 0:cw])
                    P.dma("act", dst[r0:r0 + rr, c0:c0 + cw], b[0:rr, 0:cw])

        def precast_tiled(dst_l, src_l, c_base, ntile):
            for t in range(ntile):
                a = stg.next(); b = stg16.next()
                P.dma("sp", a[:, 0:1024].rearrange("p (k c) -> p k c", k=8),
                      src_l[:, c_base + t * 128:c_base + (t + 1) * 128].rearrange("(k p) c -> p k c", p=128))
                cast(b[:, 0:1024], a[:, 0:1024])
                P.dma("act", dst_l[t], b[:, 0:1024])

        for l in range(L):
            precast(W16["w_in"][l], D["w_in"][l], DM, OFF_GATE)
            precast_tiled(W16["gate_t"][l], D["w_in"][l], OFF_GATE, 32)
            precast_tiled(W16["ffg_t"][l], D["ffn_w_gate"][l], 0, 22)
            precast_tiled(W16["ffu_t"][l], D["ffn_w_up"][l], 0, 22)
            precast(W16["ffn_w_down"][l], D["ffn_w_down"][l], DFF, DM)
            for n_ in range(4):
                precast(W16["w_branch"][l, n_], D["w_branch"][l, n_], 256, DM)
            precast(W16["w_out"][l], D["w_out"][l], DM, DM)
            precast(W16["xa_w_q"][l], D["xa_w_q"][l], DM, 256)
            precast(W16["xa_w_k"][l], D["xa_w_k"][l], DM, 256)
            precast(W16["xa_w_v"][l], D["xa_w_v"][l], DM, 256)
            precast(W16["xa_w_o"][l], D["xa_w_o"][l], 256, DM)
            precast(W16["s5_w_glu"][l], D["s5_w_glu"][l], 256, 512)
        phase_end(ph)

        def rmsnorm_T(R, xview, nrows, gb, dst, c0):
            j = R["sqj"].next(); s = R["stat"].next(); h = R["hrow"].next()
            P.act(j[0:nrows, :], xview, AF.Square, accum=s[0:nrows, 0:1])
            P.ts("dve", s[0:nrows, 1:2], s[0:nrows, 0:1], 1.0 / DM, ALU.mult, EPS, ALU.add)
            P.act(s[0:nrows, 3:4], s[0:nrows, 1:2], AF.Sqrt)
            P.recip(s[0:nrows, 2:3], s[0:nrows, 3:4])
            P.stt("dve", h[0:nrows, :], xview, s[0:nrows, 2:3], gb[0:nrows, :], ALU.mult, ALU.mult)
            for half in range(2):
                pt = PB[6 + half]
                for q4 in range(4):
                    kc = half * 4 + q4
                    P.tr(pt[:, q4 * 128:q4 * 128 + nrows], h[0:nrows, kc * 128:(kc + 1) * 128], ident[0:nrows, 0:nrows])
                cast(dst[:, half * 4:half * 4 + 4, c0:c0 + nrows], pt[:].rearrange("p (q t) -> p q t", q=4)[:, :, 0:nrows], pool_ok=False)

        def head_norm_rows(R, dst, src, nrows, gain_bc):
            s = R["stat"].next(); j = R["sqj"].next()
            for hh in range(4):
                P.act(j[0:nrows, hh * 64:(hh + 1) * 64], src[0:nrows, hh * 64:(hh + 1) * 64], AF.Square, accum=s[0:nrows, hh:hh + 1])
            P.ts("dve", s[0:nrows, 0:4], s[0:nrows, 0:4], 1.0 / 64, ALU.mult, EPS, ALU.add)
            P.act(s[0:nrows, 4:8], s[0:nrows, 0:4], AF.Sqrt)
            P.recip(s[0:nrows, 0:4], s[0:nrows, 4:8])
            for hh in range(4):
                P.stt("dve", dst[0:nrows, hh * 64:(hh + 1) * 64], src[0:nrows, hh * 64:(hh + 1) * 64], s[0:nrows, hh:hh + 1], gain_bc[0:nrows, :], ALU.mult, ALU.mult)

        def std_rings():
            return {"sqj": Ring(P, [128, DM], F32, 2, "sqj"), "hrow": Ring(P, [128, DM], F32, 2, "hrow"),
                    "stat": Ring(P, [128, 8], F32, 4, "stat"), "ev": Ring(P, [128, 512], F32, 4, "ev")}

        def xsrc(l, p):
            return (xp[p * PIECE:(p + 1) * PIECE, :] if l == 0 else XD[p][:, :])

        def phase_A(l):
            ph = phase_begin(); R = std_rings()
            gbc = P.sb([128, DM], F32, "gbc"); P.dma("sp", gbc[:], D["norm_mix"][l:l + 1, :].pbc(128))
            knbc = P.sb([128, 64], F32, "knbc"); P.dma("sp", knbc[:], D["sb_k_norm"][l:l + 1, :].pbc(128))
            w3 = W16["w_in"][l].rearrange("(k p) c -> p k c", p=128)
            Wkv = P.sb([128, 8, 512], BF16, "Wkv"); P.dma("sp", Wkv[:], w3[:, :, 1536:2048])
            Wx = Ring(P, [128, 8, 512], BF16, 2, "Wx")
            xr = Ring(P, [128, DM], F32, 6, "xt")
            hT = Ring(P, [128, 8, PIECE], BF16, 2, "hTA")
            bk = [0]

            def kv_out(hTt, c0, nrows, ok, ov):
                pp = PB[bk[0] % 4]; bk[0] += 1
                for kc in range(8):
                    P.mm(pp[0:nrows, :], hTt[:, kc, c0:c0 + nrows], Wkv[:, kc, :], start=(kc == 0), stop=(kc == 7))
                o = R["ev"].next()
                head_norm_rows(R, o, pp, nrows, knbc)
                P.cp("act", o[0:nrows, 256:512], pp[0:nrows, 256:512])
                P.dma("sp", ok, o[0:nrows, 0:256]); P.dma("act", ov, o[0:nrows, 256:512])

            def extra_cols(hTt, c0, nrows, sink):
                for cg in range(3):
                    w = Wx.next(); P.dma("sp", w[:], w3[:, :, cg * 512:(cg + 1) * 512])
                    pp = PB[4 + cg % 2]
                    for kc in range(8):
                        P.mm(pp[0:nrows, :], hTt[:, kc, c0:c0 + nrows], w[:, kc, :], start=(kc == 0), stop=(kc == 7))
                    o = R["ev"].next(); P.cp("dve", o[0:nrows, :], pp[0:nrows, :])
                    sink(cg, o)

            def sink_p(cg, o):
                if cg == 0:
                    P.dma("sp", o_pool[l, :, :], o[113:128, 0:256])
                    P.dma("sp", o_shift[l, :, 0:256], o[127:128, 256:512])
                elif cg == 1:
                    P.dma("sp", o_shift[l, :, 256:768], o[127:128, :])
                else:
                    P.dma("sp", o_shift[l, :, 768:1024], o[127:128, 0:256])

            def sink_s(cg, o):
                for b in range(SPC):
                    r = b * 8 + 7
                    if cg == 0:
                        P.dma("sp", o_pools[l, b, 7:15, :], o[b * 8:(b + 1) * 8, 0:256])
                        P.dma("act", o_pools[l, b, 0:7, :], D["st_pool"][l, b, 8:15, :])
                        P.dma("sp", o_shifts[l, b:b + 1, 0:256], o[r:r + 1, 256:512])
                    elif cg == 1:
                        P.dma("sp", o_shifts[l, b:b + 1, 256:768], o[r:r + 1, :])
                    else:
                        P.dma("sp", o_shifts[l, b:b + 1, 768:1024], o[r:r + 1, 0:256])

            for p in range(NPC):
                h = hT.next()
                for t in range(4):
                    x = xr.next()
                    P.dma("sp" if t % 2 == 0 else "act", x[:], xsrc(l, p)[t * 128:(t + 1) * 128, :])
                    rmsnorm_T(R, x[:, :], 128, gbc, h, t * 128)
                P.dma("sp", HD[p][:, :], h[:].rearrange("p k t -> p (k t)"))
                for t in range(4):
                    r0 = p * PIECE + t * 128
                    kv_out(h, t * 128, 128, o_sbk[l, r0:r0 + 128, :], o_sbv[l, r0:r0 + 128, :])
                if p == NPC - 1:
                    extra_cols(h, 384, 128, sink_p)
            rmsnorm_T(R, XS[:, :], STOK, gbc, hTs, 0)
            kv_out(hTs, 0, STOK, o_sbks[l, :, :], o_sbvs[l, :, :])
            extra_cols(hTs, 0, STOK, sink_s)
            P.dma("sp", gbc[:], D["norm_mem"][l:l + 1, :].pbc(128))
            P.dma("sp", knbc[:], D["xa_k_norm"][l:l + 1, :].pbc(128))
            P.dma("sp", Wkv[:, :, 0:256], W16["xa_w_k"][l].rearrange("(k p) c -> p k c", p=128))
            P.dma("sp", Wkv[:, :, 256:512], W16["xa_w_v"][l].rearrange("(k p) c -> p k c", p=128))
            hm = hT.next()
            for i in range(2):
                x = xr.next(); P.dma("sp", x[:], D["mem_prompt"][i * 128:(i + 1) * 128, :])
                rmsnorm_T(R, x[:, :], 128, gbc, hm, i * 128)
                kv_out(hm, i * 128, 128, o_mk[l, i * 128:(i + 1) * 128, :], o_mv[l, i * 128:(i + 1) * 128, :])
            phase_end(ph)

        ktd_t = nc.dram_tensor("KTD", [NPC, 64, PIECE], BF16, kind="Internal").ap()
        KTD = [Buf("KTD%d" % p, ktd_t[p]) for p in range(NPC)]
        vd_t = nc.dram_tensor("VD", [NPC, 128, 4 * 64], BF16, kind="Internal").ap()
        VD = [Buf("VD%d" % p, vd_t[p]) for p in range(NPC)]
        MU_OFF = {3: None, 4: None, 5: None, 6: 768, 7: 832, 8: 896, 9: 960}

        def wcols(q):
            r = [(OFF_SB + 64 * q, 64), (OFF_SB + 256 + 64 * q, 64), (OFF_SB + 512 + 64 * q, 64),
                 (OFF_RWKV + 64 * q, 64), (OFF_RWKV + 256 + 64 * q, 64), (OFF_RWKV + 512 + 64 * q, 64),
                 (OFF_RWKV + 768, 64), (OFF_RWKV + 832, 64), (OFF_RWKV + 896, 64), (OFF_RWKV + 960, 64),
                 (OFF_S5 + 64 * q, 64)]
            return r

        def phase_B(l):
            ph = phase_begin()
            Cb = {}
            for n in ("c_msu", "c_miu", "c_msl", "c_id8", "c_sbmask", "c_iota", "c_selw", "c_rc16", "c_newmask"):
                Cb[n] = C[n]
            w3 = W16["w_in"][l].rearrange("(k p) c -> p k c", p=128)
            WB = P.sb([128, 8, 768], BF16, "WB")
            hTp = Ring(P, [128, 8, PIECE], BF16, 2, "hTp")
            PT = P.sb([64, 12, PIECE + 1], F32, "PT")
            PTs = P.sb([64, 12, SPC, ST + 1], F32, "PTs")
            OBst = P.sb([64, 6, PIECE], F32, "OBst")
            OBsts = P.sb([64, 6, STOK], F32, "OBsts")
            PV = P.sb([128, 48], F32, "PV")
            tmp = Ring(P, [128, SUB + 16], F32, 10, "tmp")
            big = Ring(P, [128, PIECE], F32, 3, "big")
            bigb = Ring(P, [128, PIECE], BF16, 1, "bigb")
            kring = Ring(P, [64, PIECE], BF16, 3, "kring"); vring = Ring(P, [128, 4, 64], BF16, 4, "vring")
            RW = {n: P.sb([64, SUB], F32, "rw_" + n) for n in
                  ("r", "k", "v", "a", "kk", "kp", "logd", "L", "gam", "At", "Bt", "Kt", "Rt", "BtT", "KtT", "VT",
                   "Tba", "Tbr", "Tka", "Tkr", "P0", "P1", "Q0", "Q1", "N0", "N1", "M0", "M1", "WT", "UT")}
            Sst = P.sb([64, 64], F32, "Sst")
            wup = P.sb([64, 64], F32, "wup"); aup = P.sb([64, 64], F32, "aup"); gup = P.sb([64, 2, 64], F32, "gup")
            S5 = {n: P.sb([128, 2, SUB], F32, "s5_" + n) for n in ("cos", "sin", "rho")}
            S5w = {n: P.sb([128, 2, 64], F32, "s5w_" + n) for n in ("ccre", "ccimn", "bpre", "bpim")}
            BBT = {n: P.sb([64, 2, 128], F32, "bbt_" + n) for n in ("re", "im")}
            s5st = P.sb([128, 2, 2], F32, "s5st")
            s5c = P.sb([128, 2, 24], F32, "s5c")
            s5b = P.sb([128, 2, 2, 16], F32, "s5b")
            poolx = P.sb([64, 15 + PIECE], F32, "poolx")
            ones64 = C["c_ones64"]

            MAGIC = 12582912.0

            def sin_of(out, x, shift, t, u):
                P.ts("dve", t, x, float(shift), ALU.add)
                P.ts("dve", u, t, 1.0 / TWO_PI, ALU.mult)
                P.ts("dve", u, u, MAGIC, ALU.add)
                P.ts("dve", u, u, -MAGIC, ALU.add)
                P.stt("dve", t, u, -TWO_PI, t, ALU.mult, ALU.add)
                P.ts("dve", t, t, -3.1415925, ALU.max)
                P.ts("dve", t, t, 3.1415925, ALU.min)
                P.act(out, t, AF.Sin)

            def setup_slice(q):
                for ct, (c0, w) in enumerate(wcols(q)):
                    P.dma("sp" if ct % 2 == 0 else "act", WB[:, :, ct * 64:(ct + 1) * 64], w3[:, :, c0:c0 + w])
                for g in range(4):
                    P.dma("sp", WB[:, :, 704 + 16 * g:704 + 16 * (g + 1)], w3[:, :, OFF_POOL + 64 * g + 16 * q:OFF_POOL + 64 * g + 16 * q + 16])
                mu = D["rwkv_mu"][l]
                for i, off in enumerate((64 * q, 256 + 64 * q, 512 + 64 * q, 768, 832, 896, 960)):
                    col(PV[0:64, i:i + 1], mu[off:off + 64])
                col(PV[0:64, 7:8], D["rwkv_w0"][l, 64 * q:64 * q + 64]); col(PV[0:64, 8:9], D["rwkv_a0"][l, 64 * q:64 * q + 64])
                P.ts("dve", PV[0:64, 7:9], PV[0:64, 7:9], -1.0, ALU.mult)
                col(PV[0:64, 9:10], D["rwkv_k_k"][l, 64 * q:64 * q + 64]); col(PV[0:64, 10:11], D["rwkv_k_a"][l, 64 * q:64 * q + 64])
                col(PV[0:64, 11:12], D["rwkv_r_k"][l, 64 * q:64 * q + 64])
                col(PV[0:64, 12:13], D["sb_q_norm"][l, :]); col(PV[0:64, 13:14], D["sb_k_norm"][l, :])
                P.ts("dve", PV[0:64, 12:13], PV[0:64, 12:13], 0.125, ALU.mult)
                P.dma("sp", PV[:, 14:15], D["sb_bias"][l:l + 1, q:q + 1].pbc(128))
                col(PV[0:64, 15:16], D["s5_d"][l, 64 * q:64 * q + 64])
                P.dma("sp", wup[:], D["rwkv_w_up"][l, :, 64 * q:64 * q + 64])
                P.dma("sp", aup[:], D["rwkv_a_up"][l, :, 64 * q:64 * q + 64])
                P.dma("sp", gup[:], D["rwkv_g_up"][l, :, 64 * q:64 * q + 64].rearrange("(k p) c -> p k c", p=64))
                for t in range(2):
                    base = (4 * q + 2 * t) * 64
                    c_ = s5c[:, t, :]
                    col(c_[:, 0:1], D["s5_a_re"][l, base:base + 128]); col(c_[:, 1:2], D["s5_a_im"][l, base:base + 128])
                    for gg in range(2):
                        g = 4 * q + 2 * t + gg
                        P.dma("sp", c_[gg * 64:(gg + 1) * 64, 2:3], D["s5_log_dt"][l:l + 1, g:g + 1].pbc(64))
                    P.act(c_[:, 3:4], c_[:, 2:3], AF.Exp)
                    P.tt("dve", c_[:, 4:5], c_[:, 3:4], c_[:, 0:1], ALU.mult)
                    P.act(c_[:, 5:6], c_[:, 4:5], AF.Exp)
                    P.tt("dve", c_[:, 6:7], c_[:, 3:4], c_[:, 1:2], ALU.mult)
                    sin_of(c_[:, 9:10], c_[:, 6:7], 0.0, c_[:, 7:8], c_[:, 8:9])
                    sin_of(c_[:, 10:11], c_[:, 6:7], 0.5 * np.pi, c_[:, 7:8], c_[:, 8:9])
                    P.tt("dve", c_[:, 11:12], c_[:, 5:6], c_[:, 10:11], ALU.mult)
                    P.tt("dve", c_[:, 12:13], c_[:, 5:6], c_[:, 9:10], ALU.mult)
                    P.tt("dve", c_[:, 13:14], c_[:, 0:1], c_[:, 0:1], ALU.mult)
                    P.stt("dve", c_[:, 14:15], c_[:, 1:2], c_[:, 1:2], c_[:, 13:14], ALU.mult, ALU.add)
                    P.recip(c_[:, 15:16], c_[:, 14:15])
                    P.ts("dve", c_[:, 16:17], c_[:, 11:12], -1.0, ALU.add)
                    P.tt("dve", c_[:, 17:18], c_[:, 16:17], c_[:, 0:1], ALU.mult)
                    P.stt("dve", c_[:, 18:19], c_[:, 12:13], c_[:, 1:2], c_[:, 17:18], ALU.mult, ALU.add)
                    P.tt("dve", c_[:, 19:20], c_[:, 18:19], c_[:, 15:16], ALU.mult)
                    P.tt("dve", c_[:, 17:18], c_[:, 16:17], c_[:, 1:2], ALU.mult)
                    P.stt("dve", c_[:, 18:19], c_[:, 12:13], c_[:, 0:1], c_[:, 17:18], ALU.mult, ALU.subtract)
                    P.tt("dve", c_[:, 20:21], c_[:, 18:19], c_[:, 15:16], ALU.mult)
                    P.dma("sp", s5b[:, t, 0, :], D["s5_b_re"][l, base:base + 128, :])
                    P.dma("sp", s5b[:, t, 1, :], D["s5_b_im"][l, base:base + 128, :])
                    P.memset("pool", S5w["bpre"][:, t, :], 0.0); P.memset("pool", S5w["bpim"][:, t, :], 0.0)
                    P.memset("pool", S5w["ccre"][:, t, :], 0.0); P.memset("pool", S5w["ccimn"][:, t, :], 0.0)
                    tb = tmp.next()
                    for gg in range(2):
                        rs = slice(gg * 64, (gg + 1) * 64); cs = slice(32 * t + 16 * gg, 32 * t + 16 * gg + 16)
                        g = 4 * q + 2 * t + gg
                        P.ts("dve", tb[rs, 0:16], s5b[rs, t, 1, :], c_[rs, 20:21], ALU.mult)
                        P.stt("dve", S5w["bpre"][rs, t, cs], s5b[rs, t, 0, :], c_[rs, 19:20], tb[rs, 0:16], ALU.mult, ALU.subtract)
                        P.ts("dve", tb[rs, 16:32], s5b[rs, t, 0, :], c_[rs, 20:21], ALU.mult)
                        P.stt("dve", S5w["bpim"][rs, t, cs], s5b[rs, t, 1, :], c_[rs, 19:20], tb[rs, 16:32], ALU.mult, ALU.add)
                        P.dma("sp", S5w["ccre"][rs, t, cs], D["s5_c_re"][l, g].rearrange("c p -> p c"), allow_slow_non_contiguous=True)
                        P.dma("sp", S5w["ccimn"][rs, t, cs], D["s5_c_im"][l, g].rearrange("c p -> p c"), allow_slow_non_contiguous=True)
                    P.ts("dve", S5w["ccimn"][:, t, :], S5w["ccimn"][:, t, :], -1.0, ALU.mult)
                    for nm, src in (("re", "bpre"), ("im", "bpim")):
                        P.tr(PB[0][0:64, 0:128], S5w[src][:, t, :], ident[:, :])
                        P.cp("dve", BBT[nm][:, t, :], PB[0][0:64, 0:128])
                    a1 = tmp.next(); a2 = tmp.next()
                    P.ts("dve", a1[:, 0:SUB], Cb["c_iota"][:, :], c_[:, 6:7], ALU.mult)
                    a3 = tmp.next()
                    sin_of(S5["sin"][:, t, :], a1[:, 0:SUB], 0.0, a2[:, 0:SUB], a3[:, 0:SUB])
                    sin_of(S5["cos"][:, t, :], a1[:, 0:SUB], 0.5 * np.pi, a2[:, 0:SUB], a3[:, 0:SUB])
                    P.ts("dve", S5["rho"][:, t, :], Cb["c_iota"][:, :], 0.0, ALU.mult, c_[:, 5:6], ALU.add)

            P.memset("pool", PV[:, 16:17], -float(np.pi))

            def rwkv(cur, prev, ntok, Cn, ob_o, ob_bv, ob_g, S):
                nch = ntok // Cn
                n = slice(0, ntok)
                pm = {}
                for i, (ct, nm) in enumerate(((3, "r"), (4, "k"), (5, "v"))):
                    d = tmp.next()
                    P.tt("pool", d[0:64, n], prev(ct), cur(ct), ALU.subtract)
                    P.stt("dve", RW[nm][:, n], d[0:64, n], PV[0:64, i:i + 1], cur(ct), ALU.mult, ALU.add)
                lo = {}
                for i, ct in ((3, 6), (4, 7), (5, 8), (6, 9)):
                    d = tmp.next(); o = tmp.next()
                    P.tt("pool", d[0:64, n], prev(ct), cur(ct), ALU.subtract)
                    P.stt("dve", o[0:64, n], d[0:64, n], PV[0:64, i:i + 1], cur(ct), ALU.mult, ALU.add)
                    lo[ct] = o
                th = tmp.next()
                P.act(th[0:64, n], lo[6][0:64, n], AF.Tanh)
                P.mm(PB[0][0:64, n], wup[:, :], th[0:64, n])
                e = tmp.next()
                P.act(e[0:64, n], PB[0][0:64, n], AF.Exp, bias=PV[0:64, 7:8], scale=-1.0)
                P.ts("dve", e[0:64, n], e[0:64, n], 1.0, ALU.add)
                P.recip(e[0:64, n], e[0:64, n])
                P.ts("dve", RW["logd"][:, n], e[0:64, n], DECAY_C, ALU.mult)
                P.mm(PB[1][0:64, n], aup[:, :], lo[7][0:64, n])
                e2 = tmp.next()
                P.act(e2[0:64, n], PB[1][0:64, n], AF.Exp, bias=PV[0:64, 8:9], scale=-1.0)
                P.ts("dve", e2[0:64, n], e2[0:64, n], 1.0, ALU.add)
                P.recip(RW["a"][:, n], e2[0:64, n])
                for i, ct in enumerate((8, 9)):
                    sg = lo[ct]
                    P.act(sg[0:64, n], sg[0:64, n], AF.Exp, scale=-1.0)
                    P.ts("dve", sg[0:64, n], sg[0:64, n], 1.0, ALU.add)
                    P.recip(sg[0:64, n], sg[0:64, n])
                    P.mm(PB[0][0:64, n], gup[:, i, :], sg[0:64, n], start=(i == 0), stop=(i == 1))
                P.cp("act", ob_g, PB[0][0:64, n])
                P.ts("dve", RW["kk"][:, n], RW["k"][:, n], PV[0:64, 9:10], ALU.mult)
                sq = tmp.next()
                P.tt("pool", sq[0:64, n], RW["kk"][:, n], RW["kk"][:, n], ALU.mult)
                P.mm(PB[1][0:64, n], ones64[0:64, 0:64], sq[0:64, n])
                rn = tmp.next()
                P.ts("dve", rn[0:64, n], PB[1][0:64, n], 1e-24, ALU.max)
                P.act(rn[0:64, n], rn[0:64, n], AF.Sqrt)
                P.recip(rn[0:64, n], rn[0:64, n])
                P.tt("dve", RW["kk"][:, n], RW["kk"][:, n], rn[0:64, n], ALU.mult)
                t1 = tmp.next()
                P.ts("dve", t1[0:64, n], RW["a"][:, n], -1.0, ALU.add, PV[0:64, 10:11], ALU.mult)
                P.stt("dve", RW["kp"][:, n], t1[0:64, n], 1.0, RW["k"][:, n], ALU.add, ALU.mult)
                pr = tmp.next()
                P.stt("dve", pr[0:64, n], RW["r"][:, n], PV[0:64, 11:12], RW["kp"][:, n], ALU.mult, ALU.mult)
                P.mm(PB[0][0:64, n], ones64[0:64, 0:64], pr[0:64, n])
                P.tt("dve", ob_bv, PB[0][0:64, n], RW["v"][:, n], ALU.mult)
                P.scan(RW["L"][:, n], m0[:, n], RW["logd"][:, n], 0.0)
                gi = tmp.next(); gp = tmp.next(); lp = tmp.next()
                P.act(RW["gam"][:, n], RW["L"][:, n], AF.Exp)
                P.act(gi[0:64, n], RW["L"][:, n], AF.Exp, scale=-1.0)
                P.tt("pool", lp[0:64, n], RW["L"][:, n], RW["logd"][:, n], ALU.subtract)
                P.act(gp[0:64, n], lp[0:64, n], AF.Exp)
                P.stt("dve", RW["At"][:, n], RW["kk"][:, n], -1.0, gp[0:64, n], ALU.mult, ALU.mult)
                t2 = tmp.next()
                P.tt("pool", t2[0:64, n], RW["kk"][:, n], RW["a"][:, n], ALU.mult)
                P.tt("dve", RW["Bt"][:, n], t2[0:64, n], gi[0:64, n], ALU.mult)
                P.tt("dve", RW["Kt"][:, n], RW["kp"][:, n], gi[0:64, n], ALU.mult)
                P.tt("dve", RW["Rt"][:, n], RW["r"][:, n], RW["gam"][:, n], ALU.mult)
                for nm, src, bank in (("BtT", "Bt", 2), ("KtT", "Kt", 3), ("VT", "v", 4)):
                    for c in range(nch):
                        P.tr(PB[bank][0:Cn, c * 64:(c + 1) * 64], RW[src][:, c * Cn:(c + 1) * Cn], ident[0:64, 0:64])
                    P.cp("act" if nm == "KtT" else "dve", RW[nm][0:Cn, 0:nch * 64], PB[bank][0:Cn, 0:nch * 64])
                def v3(t, rows=Cn):
                    return t[0:rows, 0:nch * Cn].rearrange("p (c j) -> p c j", j=Cn)
                prods = (("Tba", "Bt", "At", "c_msu", 2), ("Tbr", "Bt", "Rt", "c_miu", 3), ("Tka", "Kt", "At", "c_msu", 4),
                         ("Tkr", "Kt", "Rt", "c_miu", 5), ("Q0", "At", "Bt", "c_msl", 6))
                for nm, a_, b_, mk, bank in prods:
                    for c in range(nch):
                        cs = slice(c * Cn, (c + 1) * Cn)
                        P.mm(PB[bank][0:Cn, cs], RW[a_][:, cs], RW[b_][:, cs])
                    P.tt("dve", v3(RW[nm]), v3(PB[bank]), Cb[mk][0:Cn, 0:nch, 0:Cn], ALU.mult)
                P.cp("pool", RW["P0"][0:Cn, 0:nch * Cn], RW["Tba"][0:Cn, 0:nch * Cn])
                P.tt("dve", v3(RW["N0"]), v3(RW["Tba"]), Cb["c_id8"][0:Cn, 0:nch, 0:Cn], ALU.add)
                P.tt("dve", v3(RW["M0"]), v3(RW["Q0"]), Cb["c_id8"][0:Cn, 0:nch, 0:Cn], ALU.add)
                cu = 0
                m = 1
                while (1 << m) < Cn:
                    nx = 1 - cu
                    Pc, Qc, Nc, Mc = RW["P%d" % cu], RW["Q%d" % cu], RW["N%d" % cu], RW["M%d" % cu]
                    Pn, Qn, Nn, Mn = RW["P%d" % nx], RW["Q%d" % nx], RW["N%d" % nx], RW["M%d" % nx]
                    for c in range(nch):
                        cs = slice(c * Cn, (c + 1) * Cn)
                        P.mm(PB[2][0:Cn, cs], Qc[0:Cn, cs], Pc[0:Cn, cs])
                        P.mm(PB[3][0:Cn, cs], Pc[0:Cn, cs], Qc[0:Cn, cs])
                    P.cp("dve", Pn[0:Cn, 0:nch * Cn], PB[2][0:Cn, 0:nch * Cn])
                    P.cp("act", Qn[0:Cn, 0:nch * Cn], PB[3][0:Cn, 0:nch * Cn])
                    for c in range(nch):
                        cs = slice(c * Cn, (c + 1) * Cn)
                        P.mm(PB[4][0:Cn, cs], Mc[0:Cn, cs], Pn[0:Cn, cs])
                        P.mm(PB[6][0:Cn, cs], Pn[0:Cn, cs], Mc[0:Cn, cs])
                    P.tt("dve", Nn[0:Cn, 0:nch * Cn], Nc[0:Cn, 0:nch * Cn], PB[4][0:Cn, 0:nch * Cn], ALU.add)
                    P.tt("dve", Mn[0:Cn, 0:nch * Cn], Mc[0:Cn, 0:nch * Cn], PB[6][0:Cn, 0:nch * Cn], ALU.add)
                    cu = nx
                    m += 1
                Nf = RW["N%d" % cu]
                for c in range(nch):
                    cs = slice(c * Cn, (c + 1) * Cn); c64 = slice(c * 64, (c + 1) * 64)
                    P.mm(PB[0][0:Cn, 0:64], RW["At"][:, cs], S[:, :], start=True, stop=False)
                    P.mm(PB[0][0:Cn, 0:64], RW["Tka"][0:Cn, cs], RW["VT"][0:Cn, c64], start=False, stop=True)
                    P.cp("act", RW["WT"][0:Cn, 0:64], PB[0][0:Cn, 0:64])
                    P.mm(PB[1][0:Cn, 0:64], Nf[0:Cn, cs], RW["WT"][0:Cn, 0:64])
                    P.cp("dve", RW["UT"][0:Cn, 0:64], PB[1][0:Cn, 0:64])
                    P.mm(PB[7][0:64, cs], S[:, :], RW["Rt"][:, cs], start=True, stop=False)
                    P.mm(PB[7][0:64, cs], RW["UT"][0:Cn, 0:64], RW["Tbr"][0:Cn, cs], start=False, stop=False)
                    P.mm(PB[7][0:64, cs], RW["VT"][0:Cn, c64], RW["Tkr"][0:Cn, cs], start=False, stop=True)
                    P.mm(PB[3][0:64, 0:64], RW["BtT"][0:Cn, c64], RW["UT"][0:Cn, 0:64], start=True, stop=False)
                    P.mm(PB[3][0:64, 0:64], RW["KtT"][0:Cn, c64], RW["VT"][0:Cn, c64], start=False, stop=True)
                    sn = tmp.next()
                    P.tt("dve", sn[0:64, 0:64], PB[3][0:64, 0:64], S[:, :], ALU.add)
                    P.ts("dve", S[:, :], sn[0:64, 0:64], RW["gam"][:, (c + 1) * Cn - 1:(c + 1) * Cn], ALU.mult)
                P.cp("act", ob_o, PB[7][0:64, n])

            def s5mix(cur, ntok, ob_y, st):
                n = slice(0, ntok)
                u = cur(10)
                for t in range(2):
                    P.mm(PB[0][:, n], BBT["re"][:, t, :], u)
                    P.mm(PB[1][:, n], BBT["im"][:, t, :], u)
                    cs_, sn_ = S5["cos"][:, t, n], S5["sin"][:, t, n]
                    a1 = tmp.next(); a2 = tmp.next(); zr = tmp.next(); zi = tmp.next()
                    P.tt("dve", a1[:, n], PB[0][:, n], cs_, ALU.mult)
                    P.tt("dve", a2[:, n], PB[1][:, n], sn_, ALU.mult)
                    P.tt("pool", a1[:, n], a1[:, n], a2[:, n], ALU.add)
                    a3 = tmp.next(); a4 = tmp.next()
                    P.tt("dve", a3[:, n], PB[1][:, n], cs_, ALU.mult)
                    P.tt("dve", a4[:, n], PB[0][:, n], sn_, ALU.mult)
                    P.tt("pool", a3[:, n], a3[:, n], a4[:, n], ALU.subtract)
                    P.scan(zr[:, n], S5["rho"][:, t, n], a1[:, n], st[:, t, 0:1])
                    P.scan(zi[:, n], S5["rho"][:, t, n], a3[:, n], st[:, t, 1:2])
                    sr = tmp.next(); si = tmp.next()
                    P.tt("dve", sr[:, n], zr[:, n], cs_, ALU.mult)
                    P.tt("pool", a2[:, n], zi[:, n], sn_, ALU.mult)
                    P.tt("dve", sr[:, n], sr[:, n], a2[:, n], ALU.subtract)
                    P.tt("dve", si[:, n], zi[:, n], cs_, ALU.mult)
                    P.tt("pool", a4[:, n], zr[:, n], sn_, ALU.mult)
                    P.tt("dve", si[:, n], si[:, n], a4[:, n], ALU.add)
                    P.cp("act", st[:, t, 0:1], sr[:, ntok - 1:ntok])
                    P.cp("act", st[:, t, 1:2], si[:, ntok - 1:ntok])
                    P.mm(PB[6][0:64, n], S5w["ccre"][:, t, :], sr[:, n], start=(t == 0), stop=False)
                    P.mm(PB[6][0:64, n], S5w["ccimn"][:, t, :], si[:, n], start=False, stop=(t == 1))
                P.stt("dve", ob_y, u, PV[0:64, 15:16], PB[6][0:64, n], ALU.mult, ALU.add)

            def poolmix(ext, ntok, ob_p, first):
                W = 15 + ntok
                s2 = tmp.next() if W <= SUB + 16 else big.next()
                s4 = tmp.next() if W <= SUB + 16 else big.next()
                s8 = tmp.next() if W <= SUB + 16 else big.next()
                s16 = tmp.next() if W <= SUB + 16 else big.next()
                P.tt("dve", s2[0:64, 1:W], ext[:, 1:W], ext[:, 0:W - 1], ALU.add)
                P.tt("dve", s4[0:64, 3:W], s2[0:64, 3:W], s2[0:64, 1:W - 2], ALU.add)
                P.tt("dve", s8[0:64, 7:W], s4[0:64, 7:W], s4[0:64, 3:W - 4], ALU.add)
                P.tt("dve", s16[0:64, 15:W], s8[0:64, 15:W], s8[0:64, 7:W - 8], ALU.add)
                sl = slice(15, W)
                acc = tmp.next() if W <= SUB + 16 else big.next()
                sw = Cb["c_selw"]
                P.ts("dve", acc[0:64, sl], s2[0:64, sl], sw[:, 0:1], ALU.mult)
                P.stt("dve", acc[0:64, sl], s4[0:64, sl], sw[:, 1:2], acc[0:64, sl], ALU.mult, ALU.add)
                P.stt("dve", acc[0:64, sl], s8[0:64, sl], sw[:, 2:3], acc[0:64, sl], ALU.mult, ALU.add)
                P.stt("dve", acc[0:64, sl], s16[0:64, sl], sw[:, 3:4], acc[0:64, sl], ALU.mult, ALU.add)
                if first:
                    f = slice(15, 31); rc = Cb["c_rc16"]
                    t1 = tmp.next()
                    P.tt("dve", acc[0:64, f], s2[0:64, f], rc[:, 0, :], ALU.mult)
                    for gi_, sK in ((1, s4), (2, s8), (3, s16)):
                        P.tt("dve", t1[0:64, 0:16], sK[0:64, f], rc[:, gi_, :], ALU.mult)
                        P.tt("dve", acc[0:64, f], acc[0:64, f], t1[0:64, 0:16], ALU.add)
                P.tt("dve", ob_p, acc[0:64, sl], ext[:, sl], ALU.subtract)

            Qs = P.sb([64, PIECE], BF16, "Qs"); Kn = P.sb([64, PIECE], BF16, "Kn"); vtb = P.sb([128, 4, 64], BF16, "vtb")
            sbmask = Cb["c_sbmask"]
            e1r = Ring(P, [128, PIECE], F32, 2, "e1r"); spr = Ring(P, [128, PIECE], F32, 4, "spr")
            t1r = Ring(P, [128, PIECE], F32, 6, "t1r")
            spbr = Ring(P, [128, PIECE], BF16, 6, "spbr"); attr = Ring(P, [128, PIECE], BF16, 5, "attr")

            def sb_prompt(G):
                for ct, gcol, dst in ((0, 12, Qs), (1, 13, Kn)):
                    x = PT[:, ct, 1:PIECE + 1]
                    sq = big.next(); P.tt("pool", sq[0:64, :], x, x, ALU.mult)
                    P.mm(PB[0][0:64, :], ones64[0:64, 0:64], sq[0:64, :])
                    rs = big.next()
                    P.ts("dve", rs[0:64, :], PB[0][0:64, :], 1.0 / 64, ALU.mult, EPS, ALU.add)
                    P.act(rs[0:64, :], rs[0:64, :], AF.Sqrt)
                    P.recip(rs[0:64, :], rs[0:64, :])
                    P.stt("dve", dst[:, :], x, PV[0:64, gcol:gcol + 1], rs[0:64, :], ALU.mult, ALU.mult)
                P.dma("sp", KTD[G][:, :], Kn[:, :])
                for t in range(4):
                    P.tr(PB[6][:, t * 64:(t + 1) * 64], PT[:, 2, 1 + t * 128:1 + (t + 1) * 128], ident[0:64, 0:64])
                P.cp("dve", vtb[:].rearrange("p a d -> p (a d)"), PB[6][:, 0:256])
                P.dma("act", VD[G][:, :], vtb[:].rearrange("p a d -> p (a d)"))
                blocks = [(kg, b4) for kg in range(G, -1, -1) for b4 in range(3, -1, -1)]
                nblk = len(blocks)
                zbank = (PB[2], PB[6]); tbank = (PB[3], PB[7])
                kv = {}
                S = {}

                def ok(j):
                    return 0 <= j < nblk

                for i in range(-3, nblk + 1):
                    j = i + 3
                    if ok(j):
                        kg, b4 = blocks[j]
                        if b4 == 3:
                            kt = kring.next(); P.dma("sp", kt[:, :], KTD[kg][:, :])
                            vt = vring.next(); P.dma("act", vt[:].rearrange("p a d -> p (a d)"), VD[kg][:, :])
                            kv[kg] = (kt, vt)
                        kt, vt = kv[kg]
                        S[j] = {"vt": vt, "b4": b4, "diag": (kg == G)}
                        P.mm(zbank[j % 2][:, :], kt[:, b4 * 128:(b4 + 1) * 128], Qs[:, :])
                    if ok(i):
                        s = S[i]
                        s["att"] = attr.next()
                        P.act(s["att"][:, :], s["t1"][:, :], AF.Exp, bias=PV[:, 14:15])
                        if s["diag"]:
                            P.tt("pool", s["att"][:, :], s["att"][:, :], sbmask[:, s["b4"], :], ALU.mult)
                    if ok(j):
                        s = S[j]
                        e1 = e1r.next(); P.act(e1[:, :], zbank[j % 2][:, :], AF.Exp, bias=PV[:, 14:15])
                        s["sp"] = spr.next(); P.act(s["sp"][:, :], e1[:, :], AF.Ln, bias=1.0)
                        if s["diag"]:
                            P.tt("pool", s["sp"][:, :], s["sp"][:, :], sbmask[:, s["b4"], :], ALU.mult)
                        s["spb"] = spbr.next(); P.cp("pool", s["spb"][:, :], s["sp"][:, :])
                    j = i + 2
                    if ok(j):
                        s = S[j]
                        P.mm(tbank[j % 2][:, :], trib[:, :], s["spb"][:, :])
                        s["t1"] = t1r.next()
                        P.tt("dve", s["t1"][:, :], zbank[j % 2][:, :], s["sp"][:, :], ALU.subtract)
                    j = i + 1
                    if ok(j):
                        s = S[j]
                        P.tt("dve", s["t1"][:, :], s["t1"][:, :], tbank[j % 2][:, :], ALU.subtract)
                        if j > 0:
                            P.tt("dve", s["t1"][:, :], s["t1"][:, :], PB[4][:, :], ALU.subtract)
                    j = i - 1
                    if ok(j):
                        s = S.pop(j)
                        P.mm(PB[5][0:64, :], s["vt"][:, s["b4"], :], s["att"][:, :], start=(j == 0), stop=(j == nblk - 1))
                    j = i + 1
                    if ok(j):
                        P.mm(PB[4][:, :], onesb[:, :], S[j]["spb"][:, :], start=(j == 0), stop=True)
                P.cp("act", OBst[:, 0, :], PB[5][0:64, :])

            Sst_s = P.sb([64, 64], F32, "Sst_s"); s5st_s = P.sb([128, 2, 2], F32, "s5st_s")
            poolx_s = P.sb([64, 15 + ST], F32, "poolx_s")
            RW_OFF = {3: None, 4: None, 5: None, 6: 768, 7: 832, 8: 896, 9: 960}

            def run_slice(q):
                setup_slice(q)
                P.memset("pool", Sst[:], 0.0); P.memset("pool", s5st[:], 0.0)
                P.memset("pool", PT[:, :, 0:1], 0.0); P.memset("pool", poolx[:, 0:15], 0.0)
                for G in range(NPC):
                    h = hTp.next(); P.dma("sp", h[:].rearrange("p k t -> p (k t)"), HD[G][:, :])
                    if G > 0:
                        P.cp("pool", PT[:, :, 0:1], PT[:, :, PIECE:PIECE + 1])
                    for ct in range(12):
                        pb = PB[ct % 2]
                        for kc in range(8):
                            P.mm(pb[0:64, :], WB[:, kc, ct * 64:(ct + 1) * 64], h[:, kc, :], start=(kc == 0), stop=(kc == 7))
                        P.cp("act" if ct % 2 else "dve", PT[:, ct, 1:PIECE + 1], pb[0:64, :])
                    if "sb" in STAGES:
                        sb_prompt(G)
                    P.cp("pool", poolx[:, 15:15 + PIECE], PT[:, 11, 1:PIECE + 1])
                    for sub in range(PIECE // SUB):
                        o = sub * SUB
                        cur = lambda ct, o=o: PT[:, ct, 1 + o:1 + o + SUB]
                        prev = lambda ct, o=o: PT[:, ct, o:o + SUB]
                        if "rwkv" in STAGES:
                            rwkv(cur, prev, SUB, CH, OBst[:, 1, o:o + SUB], OBst[:, 2, o:o + SUB], OBst[:, 3, o:o + SUB], Sst)
                        if "s5" in STAGES:
                            s5mix(cur, SUB, OBst[:, 4, o:o + SUB], s5st)
                        poolmix(poolx[:, o:o + 15 + SUB], SUB, OBst[:, 5, o:o + SUB], first=(G == 0 and sub == 0))
                    P.cp("pool", poolx[:, 0:15], poolx[:, PIECE:PIECE + 15])
                    P.dma("sp", OB[q][G][:, :].rearrange("(k r) t -> r k t", r=64), OBst[:])
                P.dma("sp", o_wkv[l, q].rearrange("v k -> k v"), Sst[:], allow_slow_non_contiguous=True)
                for t in range(2):
                    base = (4 * q + 2 * t) * 64
                    P.dma("sp", o_s5r[l, base:base + 128].rearrange("(p o) -> p o", o=1), s5st[:, t, 0:1])
                    P.dma("sp", o_s5i[l, base:base + 128].rearrange("(p o) -> p o", o=1), s5st[:, t, 1:2])
                for ct in range(12):
                    pb = PB[ct % 2]
                    for kc in range(8):
                        P.mm(pb[0:64, 0:STOK], WB[:, kc, ct * 64:(ct + 1) * 64], hTs[:, kc, :], start=(kc == 0), stop=(kc == 7))
                    P.cp("act" if ct % 2 else "dve", PTs[:, ct, :, 1:ST + 1], pb[0:64, 0:STOK].rearrange("p (b t) -> p b t", t=ST))
                for ct, off in ((3, 64 * q), (4, 256 + 64 * q), (5, 512 + 64 * q), (6, 768), (7, 832), (8, 896), (9, 960)):
                    P.dma("sp", PTs[:, ct, :, 0:1], D["st_shift"][l, :, off:off + 64].rearrange("b (p o) -> p b o", o=1))
                for b in range(SPC):
                    cur = lambda ct, b=b: PTs[:, ct, b, 1:ST + 1]
                    prev = lambda ct, b=b: PTs[:, ct, b, 0:ST]
                    bs = slice(b * ST, (b + 1) * ST)
                    if "rwkv" in STAGES:
                        P.dma("sp", Sst_s[:], D["st_wkv"][l, b, q].rearrange("v k -> k v"), allow_slow_non_contiguous=True)
                        rwkv(cur, prev, ST, ST, OBsts[:, 1, bs], OBsts[:, 2, bs], OBsts[:, 3, bs], Sst_s)
                        P.dma("sp", o_wkvs[l, b, q].rearrange("v k -> k v"), Sst_s[:], allow_slow_non_contiguous=True)
                    if "s5" in STAGES:
                        for t in range(2):
                            base = (4 * q + 2 * t) * 64
                            col(s5st_s[:, t, 0:1], D["st_s5r"][l, b, base:base + 128])
                            col(s5st_s[:, t, 1:2], D["st_s5i"][l, b, base:base + 128])
                        s5mix(cur, ST, OBsts[:, 4, bs], s5st_s)
                        for t in range(2):
                            base = (4 * q + 2 * t) * 64
                            P.dma("sp", o_s5rs[l, b, base:base + 128].rearrange("(p o) -> p o", o=1), s5st_s[:, t, 0:1])
                            P.dma("sp", o_s5is[l, b, base:base + 128].rearrange("(p o) -> p o", o=1), s5st_s[:, t, 1:2])
                    for g in range(4):
                        c0 = 64 * g + 16 * q
                        P.dma("sp", poolx_s[16 * g:16 * g + 16, 0:15], D["st_pool"][l, b, :, c0:c0 + 16].rearrange("r c -> c r"), allow_slow_non_contiguous=True)
                    P.cp("pool", poolx_s[:, 15:15 + ST], PTs[:, 11, b, 1:ST + 1])
                    poolmix(poolx_s[:, :], ST, OBsts[:, 5, bs], first=False)
                P.dma("sp", OBS[q][:, :].rearrange("(k r) t -> r k t", r=64)[:, 1:6, :], OBsts[:, 1:6, :])

            if os.environ.get("MK_VERBOSE"):
                print("phase B sbuf remaining", nc.sbuf_bytes_remaining, flush=True)
            for q in range(4):
                run_slice(q)
            phase_end(ph)

        def phase_C(l):
            ph = phase_begin(); R = std_rings()
            gb = {}
            for nm in ("norm_cross", "norm_ffn"):
                gb[nm] = P.sb([128, DM], F32, "gb_" + nm); P.dma("sp", gb[nm][:], D[nm][l:l + 1, :].pbc(128))
            PVc = P.sb([128, 16], F32, "PVc")
            for p in range(2):
                col(PVc[:, p:p + 1], D["pool_scale"][l, p * 128:(p + 1) * 128])
                col(PVc[:, 2 + p:3 + p], D["rwkv_ln_w"][l, p * 128:(p + 1) * 128])
                col(PVc[:, 4 + p:5 + p], D["rwkv_ln_b"][l, p * 128:(p + 1) * 128])
                col(PVc[p * 64:(p + 1) * 64, 6:7], D["xa_q_norm"][l, :])
            P.ts("dve", PVc[:, 6:7], PVc[:, 6:7], 0.125, ALU.mult)
            poolW = P.sb([128, 2, 128], F32, "poolW"); P.memset("pool", poolW[:], 0.0)
            for g in range(4):
                P.dma("sp", poolW[(g % 2) * 64:(g % 2) * 64 + 64, g // 2, (g % 2) * 64:(g % 2) * 64 + 64], D["pool_w"][l, g])
            glu = P.sb([128, 2, 512], BF16, "glu"); P.dma("sp", glu[:], W16["s5_w_glu"][l].rearrange("(k p) c -> p k c", p=128))
            Wbr = P.sb([128, 4, 2, DM], BF16, "Wbr")
            for nb in range(4):
                P.dma("act", Wbr[:, nb, :, :], W16["w_branch"][l, nb].rearrange("(k p) c -> p k c", p=128))
            Wq = P.sb([128, 8, 256], BF16, "Wq"); P.dma("sp", Wq[:], W16["xa_w_q"][l].rearrange("(k p) c -> p k c", p=128))
            Wo = P.sb([128, 2, DM], BF16, "Wo"); P.dma("sp", Wo[:], W16["xa_w_o"][l].rearrange("(k p) c -> p k c", p=128))
            onespad = P.sb([128, 2, 128], BF16, "onespad"); P.memset("pool", onespad[:], 0.0)
            P.memset("pool", onespad[:, 0, 0:64], 1.0); P.memset("pool", onespad[:, 1, 64:128], 1.0)
            KmT = P.sb([128, 2, 256], BF16, "KmT"); Vpad = P.sb([128, 2, 4, 128], BF16, "Vpad")
            memf = P.sb([128, 2, 256], F32, "memf")
            xt = [P.sb([128, DM], F32, "xc%d" % i) for i in range(4)]
            hTg = P.sb([128, 8, PIECE], BF16, "hTg"); hT2 = P.sb([128, 8, PIECE], BF16, "hT2")
            inr = Ring(P, [128, PIECE], F32, 5, "inr")
            tf = Ring(P, [128, PIECE], F32, 5, "tf")
            accr = Ring(P, [128, PIECE], F32, 2, "accr")
            BR = [[P.sb([128, PIECE], BF16, "BR%d_%d" % (nb, p)) for p in range(2)] for nb in range(4)]
            merged = P.sb([128, 8, PIECE], BF16, "merged")
            actT = P.sb([128, 22, PIECE], BF16, "actT")
            wr = Ring(P, [128, 1024], BF16, 6, "wr")
            qn = [P.sb([128, PIECE], BF16, "qn%d" % p) for p in range(2)]
            xo = [P.sb([128, PIECE], BF16, "xo%d" % p) for p in range(2)]
            eb = Ring(P, [128, PIECE], BF16, 3, "eb")
            ones64 = C["c_ones64"]
            if os.environ.get("MK_VERBOSE"):
                print("phase C sbuf remaining", nc.sbuf_bytes_remaining, flush=True)

            def load_mem(kview, vview):
                P.memset("pool", Vpad[:], 0.0)
                P.dma("sp", memf[:], kview.rearrange("(t p) c -> p t c", p=128))
                for mt in range(2):
                    for hp in range(2):
                        P.tr(PB[0][:, (mt * 2 + hp) * 128:(mt * 2 + hp + 1) * 128], memf[:, mt, hp * 128:(hp + 1) * 128], ident[:, :])
                P.cp("dve", KmT[:].rearrange("p h (t m) -> p h t m", t=2), PB[0][:, :].rearrange("p (t h m) -> p h t m", t=2, h=2))
                vf = inr.next()
                P.dma("act", vf[:, 0:512].rearrange("p (t c) -> p t c", t=2), vview.rearrange("(t p) c -> p t c", p=128))
                for mt in range(2):
                    for h in range(4):
                        cast(Vpad[:, mt, h, (h % 2) * 64:(h % 2) * 64 + 64], vf[:, mt * 256 + h * 64:mt * 256 + (h + 1) * 64])

            def proc(xts, nrows, ntok, hsrc, obsrc, mem_cols, xdst):
                n = slice(0, ntok)
                def load_in(k, p):
                    t = inr.next()
                    if k == 5:
                        for gg in range(2):
                            g = 2 * p + gg
                            for qq in range(4):
                                P.dma("sp" if qq % 2 else "act", t[gg * 64 + 16 * qq:gg * 64 + 16 * qq + 16, n], obsrc(qq)[5 * 64 + 16 * g:5 * 64 + 16 * g + 16, :])
                    else:
                        for hh in range(2):
                            P.dma("sp" if hh else "act", t[hh * 64:(hh + 1) * 64, n], obsrc(2 * p + hh)[k * 64:(k + 1) * 64, :])
                    return t
                for p in range(2):
                    ip = load_in(5, p)
                    P.mm(PB[0][:, n], poolW[:, p, :], ip[:, n])
                    P.ts("dve", BR[0][p][:, n], PB[0][:, n], PVc[:, p:p + 1], ALU.mult)
                    isb = load_in(0, p)
                    cast(BR[2][p][:, n], isb[:, n])
                    io = load_in(1, p); ibv = load_in(2, p); ig = load_in(3, p)
                    P.mm(PB[1][:, n], ones64[:, :], io[:, n])
                    cen = tf.next()
                    P.stt("dve", cen[:, n], PB[1][:, n], -1.0 / 64, io[:, n], ALU.mult, ALU.add)
                    sq = tf.next(); P.tt("pool", sq[:, n], cen[:, n], cen[:, n], ALU.mult)
                    P.mm(PB[2][:, n], ones64[:, :], sq[:, n])
                    rs = tf.next()
                    P.ts("dve", rs[:, n], PB[2][:, n], 1.0 / 64, ALU.mult, 64e-5, ALU.add)
                    P.act(rs[:, n], rs[:, n], AF.Sqrt)
                    P.recip(rs[:, n], rs[:, n])
                    P.tt("dve", cen[:, n], cen[:, n], rs[:, n], ALU.mult)
                    P.ts("dve", cen[:, n], cen[:, n], PVc[:, 2 + p:3 + p], ALU.mult, PVc[:, 4 + p:5 + p], ALU.add)
                    P.tt("pool", cen[:, n], cen[:, n], ibv[:, n], ALU.add)
                    P.tt("dve", BR[1][p][:, n], cen[:, n], ig[:, n], ALU.mult)
                ge = []
                for p in range(2):
                    iy = load_in(4, p)
                    x2 = tf.next(); P.tt("pool", x2[:, n], iy[:, n], iy[:, n], ALU.mult)
                    P.ts("dve", x2[:, n], x2[:, n], 0.044715, ALU.mult, 1.0, ALU.add)
                    P.stt("dve", x2[:, n], x2[:, n], 0.7978845608028654, iy[:, n], ALU.mult, ALU.mult)
                    P.act(x2[:, n], x2[:, n], AF.Tanh)
                    g_ = eb.next()
                    P.stt("dve", x2[:, n], x2[:, n], 1.0, iy[:, n], ALU.add, ALU.mult)
                    P.ts("dve", g_[:, n], x2[:, n], 0.5, ALU.mult)
                    ge.append(g_)
                for p in range(2):
                    for kc in range(2):
                        P.mm(PB[3][:, n], glu[:, kc, (2 + p) * 128:(3 + p) * 128], ge[kc][:, n], start=(kc == 0), stop=(kc == 1))
                    for kc in range(2):
                        P.mm(PB[4][:, n], glu[:, kc, p * 128:(p + 1) * 128], ge[kc][:, n], start=(kc == 0), stop=(kc == 1))
                    sg = tf.next(); P.act(sg[:, n], PB[3][:, n], AF.Sigmoid)
                    P.stt("dve", BR[3][p][:, n], PB[4][:, n], 1.0, sg[:, n], ALU.mult, ALU.mult)
                hsrc()
                for dt in range(8):
                    acc = accr.next()
                    for nb in range(4):
                        w = wr.next(); P.dma("sp" if nb % 2 else "act", w[:, :], W16["gate_t"][l, nb * 8 + dt])
                        for kc in range(8):
                            P.mm(PB[5][:, n], w[:, kc * 128:(kc + 1) * 128], hTg[:, kc, n], start=(kc == 0), stop=(kc == 7))
                        sg = tf.next(); P.act(sg[:, n], PB[5][:, n], AF.Sigmoid)
                        for kc in range(2):
                            P.mm(PB[6][:, n], Wbr[:, nb, kc, dt * 128:(dt + 1) * 128], BR[nb][kc][:, n], start=(kc == 0), stop=(kc == 1))
                        if nb == 0:
                            P.tt("dve", acc[:, n], sg[:, n], PB[6][:, n], ALU.mult)
                        else:
                            pr = tf.next(); P.tt("dve", pr[:, n], sg[:, n], PB[6][:, n], ALU.mult)
                            if nb < 3:
                                P.tt("pool", acc[:, n], acc[:, n], pr[:, n], ALU.add)
                            else:
                                P.tt("pool", merged[:, dt, n], acc[:, n], pr[:, n], ALU.add)

                if DBGC[0] is not None and ntok == PIECE and not DBGC[1]:
                    DBGC[1] = True
                    for nb in range(4):
                        for p in range(2):
                            P.dma("sp", DBGC[0]["br"][nb * 2 + p], BR[nb][p][:, :])
                    P.dma("sp", DBGC[0]["mg"][:, :], merged[:].rearrange("p k t -> p (k t)"))

                def tok_major_add(lhs_fn, w_fn, nk):
                    for kc in range(nk):
                        w = w_fn(kc)
                        c = 0
                        for ti, nr in enumerate(nrows):
                            for ch in range(2):
                                P.mm(PB[ti * 2 + ch][0:nr, :], lhs_fn(kc)[:, c:c + nr], w[:, ch * 512:(ch + 1) * 512], start=(kc == 0), stop=(kc == nk - 1))
                            c += nr
                    for ti, nr in enumerate(nrows):
                        for ch in range(2):
                            P.tt("dve", xts[ti][0:nr, ch * 512:(ch + 1) * 512], xts[ti][0:nr, ch * 512:(ch + 1) * 512], PB[ti * 2 + ch][0:nr, :], ALU.add)

                def w_stream(name, kc):
                    w = wr.next(); P.dma("sp" if kc % 2 else "act", w[:, :], W16[name][l, kc * 128:(kc + 1) * 128, :])
                    return w
                tok_major_add(lambda kc: merged[:, kc, :], lambda kc: w_stream("w_out", kc), 8)
                c = 0
                for ti, nr in enumerate(nrows):
                    rmsnorm_T(R, xts[ti][0:nr, :], nr, gb["norm_cross"], hT2, c); c += nr
                for p in range(2):
                    for kc in range(8):
                        P.mm(PB[0][:, n], Wq[:, kc, p * 128:(p + 1) * 128], hT2[:, kc, n], start=(kc == 0), stop=(kc == 7))
                    sq = tf.next(); P.act(sq[:, n], PB[0][:, n], AF.Square)
                    P.mm(PB[1][:, n], ones64[:, :], sq[:, n])
                    rs = tf.next()
                    P.ts("dve", rs[:, n], PB[1][:, n], 1.0 / 64, ALU.mult, EPS, ALU.add)
                    P.act(rs[:, n], rs[:, n], AF.Sqrt)
                    P.recip(rs[:, n], rs[:, n])
                    P.stt("dve", qn[p][:, n], PB[0][:, n], PVc[:, 6:7], rs[:, n], ALU.mult, ALU.mult)
                for (mem_loader, cs) in mem_cols:
                    if mem_loader is not None:
                        mem_loader()
                    for p in range(2):
                        for par in range(2):
                            h = 2 * p + par
                            ps_ = slice(par * 64, (par + 1) * 64)
                            for mt in range(2):
                                P.mm(PB[2][:, cs], KmT[ps_, p, mt * 128:(mt + 1) * 128], qn[p][ps_, cs])
                                e = eb.next(); P.act(e[:, cs], PB[2][:, cs], AF.Exp)
                                fst = (par == 0 and mt == 0); lst = (par == 1 and mt == 1)
                                P.mm(PB[3 + p][:, cs], Vpad[:, mt, h, :], e[:, cs], start=fst, stop=lst)
                                P.mm(PB[5 + p][:, cs], onespad[:, par, :], e[:, cs], start=fst, stop=lst)
                for p in range(2):
                    rc = tf.next(); P.recip(rc[:, n], PB[5 + p][:, n])
                    P.tt("dve", xo[p][:, n], PB[3 + p][:, n], rc[:, n], ALU.mult)
                tok_major_add(lambda kc: xo[kc], lambda kc: Wo[:, kc, :], 2)
                c = 0
                for ti, nr in enumerate(nrows):
                    rmsnorm_T(R, xts[ti][0:nr, :], nr, gb["norm_ffn"], hT2, c); c += nr
                for ft in range(22):
                    wg = wr.next(); wu = wr.next()
                    P.dma("sp", wg[:, :], W16["ffg_t"][l, ft]); P.dma("act", wu[:, :], W16["ffu_t"][l, ft])
                    pg = PB[(ft % 2) * 2]; pu = PB[(ft % 2) * 2 + 1]
                    for kc in range(8):
                        P.mm(pg[:, n], wg[:, kc * 128:(kc + 1) * 128], hT2[:, kc, n], start=(kc == 0), stop=(kc == 7))
                    for kc in range(8):
                        P.mm(pu[:, n], wu[:, kc * 128:(kc + 1) * 128], hT2[:, kc, n], start=(kc == 0), stop=(kc == 7))
                    sg = tf.next(); P.act(sg[:, n], pg[:, n], AF.Silu)
                    P.tt("dve", actT[:, ft, n], sg[:, n], pu[:, n], ALU.mult)
                tok_major_add(lambda kc: actT[:, kc, :], lambda kc: w_stream("ffn_w_down", kc), 22)
                xdst()

            for G in range(NPC):
                for t in range(4):
                    P.dma("sp" if t % 2 else "act", xt[t][:], xsrc(l, G)[t * 128:(t + 1) * 128, :])

                def hsrc(G=G):
                    P.dma("sp", hTg[:].rearrange("p k t -> p (k t)"), HD[G][:, :])

                def xdst(G=G):
                    for t in range(4):
                        P.dma("sp" if t % 2 else "act", XD[G][t * 128:(t + 1) * 128, :], xt[t][:])
                mem = [((lambda: load_mem(o_mk[l], o_mv[l])) if G == 0 else None, slice(0, PIECE))]
                proc([x[:, :] for x in xt], [128] * 4, PIECE, hsrc, lambda qq, G=G: OB[qq][G], mem, xdst)

            def hsrc_s():
                P.cp("pool", hTg[:, :, 0:STOK], hTs[:, :, :])
            mem_s = [((lambda b=b: load_mem(D["cmk"][l, b], D["cmv"][l, b])), slice(b * ST, (b + 1) * ST)) for b in range(SPC)]
            proc([XS[:, :]], [STOK], STOK, hsrc_s, lambda qq: OBS[qq], mem_s, lambda: None)
            phase_end(ph)

        def phase_Bs(l):
            ph = phase_begin()
            NG = NPAGE
            w3 = W16["w_in"][l].rearrange("(k p) c -> p k c", p=128)
            Wsb = P.sb([128, 8, 768], BF16, "Wsb"); P.dma("sp", Wsb[:], w3[:, :, OFF_SB:OFF_SB + 768])
            onesf = P.sb([128, 128], F32, "onesf"); P.memset("pool", onesf[:], 1.0)
            ones64 = C["c_ones64"]
            PVs = P.sb([128, 16], F32, "PVs")
            for par in range(2):
                col(PVs[par * 64:(par + 1) * 64, 0:1], D["sb_q_norm"][l, :]); col(PVs[par * 64:(par + 1) * 64, 1:2], D["sb_k_norm"][l, :])
            P.ts("dve", PVs[:, 0:1], PVs[:, 0:1], 0.125, ALU.mult)
            P.dma("sp", PVs[:, 4:8], D["sb_bias"][l:l + 1, :].pbc(128))
            QKV = P.sb([128, 6, STOK], F32, "QKV")
            for i in range(6):
                for kc in range(8):
                    P.mm(PB[i % 2][:, 0:STOK], Wsb[:, kc, i * 128:(i + 1) * 128], hTs[:, kc, :], start=(kc == 0), stop=(kc == 7))
                if i < 4:
                    sq = P.sb([128, STOK], F32, "sqs%d" % i); rs = P.sb([128, STOK], F32, "rss%d" % i)
                    P.act(sq[:, :], PB[i % 2][:, 0:STOK], AF.Square)
                    P.mm(PB[2][:, 0:STOK], ones64[:, :], sq[:, :])
                    P.ts("dve", rs[:, :], PB[2][:, 0:STOK], 1.0 / 64, ALU.mult, EPS, ALU.add)
                    P.act(rs[:, :], rs[:, :], AF.Sqrt)
                    P.recip(rs[:, :], rs[:, :])
                    P.stt("dve", QKV[:, i, :], PB[i % 2][:, 0:STOK], PVs[:, (0 if i < 2 else 1):(1 if i < 2 else 2)], rs[:, :], ALU.mult, ALU.mult)
                else:
                    P.cp("dve", QKV[:, i, :], PB[i % 2][:, 0:STOK])
            Zt = P.sb([128, 32, 128], F32, "Zt"); SPt = P.sb([128, 32, 128], F32, "SPt")
            INC = P.sb([128, 32, 128], F32, "INC")
            m0s = P.sb([128, 32, 128], F32, "m0s"); P.memset("pool", m0s[:], 1.0); P.memset("pool", m0s[:, :, 0:1], 0.0)
            kvr = Ring(P, [128, 16, 256], F32, 2, "kvr")
            KT = Ring(P, [128, 512], F32, 3, "KTs")
            idx = P.sb([128, 1], I32, "idx"); idxf = P.sb([128, 1], F32, "idxf")
            idxr = Ring(P, [128, 1], I32, 4, "idxr")
            Qblk = P.sb([128, 2, 16], F32, "Qblk")
            sm = {n: P.sb([128, 32], F32, "sm_" + n) for n in ("zb", "e1", "sp", "aft", "cn", "rs", "cg", "cgb", "attn")}
            vnew = P.sb([8, 256], F32, "vnew")
            osb = P.sb([128, 2, ST], F32, "osb")
            ckv = D["ck"][:, :]; cvv = D["cv"][:, :]
            for b in range(SPC):
                bs = slice(b * ST, (b + 1) * ST)
                P.dma("sp", idx[0:NG, :], D["ptab"][b, :].rearrange("(p o) -> p o", o=1))
                P.cp("dve", idxf[0:NG, :], idx[0:NG, :])
                P.ts("dve", idxf[0:NG, :], idxf[0:NG, :], 8.0, ALU.mult, float(l * NPOOL * 8), ALU.add)
                P.memset("pool", Qblk[:], 0.0)
                for hp in range(2):
                    P.cp("dve", Qblk[0:64, hp, 0:8], QKV[0:64, hp, bs])
                    P.cp("dve", Qblk[64:128, hp, 8:16], QKV[64:128, hp, bs])
                for hp in range(2):
                    P.mm(PB[3][0:ST, hp * 16:(hp + 1) * 16], QKV[:, 2 + hp, bs], Qblk[:, hp, :])
                for h in range(4):
                    P.ts("dve", sm["zb"][0:ST, h * 8:(h + 1) * 8], PB[3][0:ST, h * 8:(h + 1) * 8], PVs[0:ST, 4 + h:5 + h], ALU.add)
                P.act(sm["e1"][0:ST, :], sm["zb"][0:ST, :], AF.Exp)
                P.act(sm["sp"][0:ST, :], sm["e1"][0:ST, :], AF.Ln, bias=1.0)
                P.tt("dve", sm["sp"][0:ST, :], sm["sp"][0:ST, :], C["c_newmask"][:, :], ALU.mult)
                P.mm(PB[4][0:ST, 0:32], C["c_tri"][0:ST, 0:ST], sm["sp"][0:ST, :])
                P.mm(PB[5][:, 0:32], onesf[0:ST, :], sm["sp"][0:ST, :])
                P.cp("dve", sm["cn"][:, :], PB[5][:, 0:32])
                P.tt("dve", sm["aft"][0:ST, :], sm["zb"][0:ST, :], sm["sp"][0:ST, :], ALU.subtract)
                P.tt("dve", sm["aft"][0:ST, :], sm["aft"][0:ST, :], PB[4][0:ST, 0:32], ALU.subtract)
                P.act(sm["attn"][0:ST, :], sm["aft"][0:ST, :], AF.Exp)
                P.tt("dve", sm["attn"][0:ST, :], sm["attn"][0:ST, :], C["c_newmask"][:, :], ALU.mult)
                for si in range(7, -1, -1):
                    kb = kvr.next(); ix = idxr.next()
                    P.ts("dve", ix[0:NG, :], idxf[0:NG, :], float(si), ALU.add)
                    P.op("pool", lambda e, kb=kb, ix=ix: e.indirect_dma_start(
                        out=kb.t[0:NG, :, :].rearrange("p t c -> p (t c)"), out_offset=None,
                        in_=ckv.ap,
                        in_offset=bass.IndirectOffsetOnAxis(ap=ix.t[0:NG, :], axis=0)), [ix, ckv.buf], [kb], dma=True)
                    for tl in range(15, -1, -1):
                        jl = 15 - tl
                        kt = KT.next() if jl % 2 == 0 else kt
                        for hp in range(2):
                            P.tr(PB[0 + (jl % 2)][:, hp * 128:hp * 128 + NG], kb[0:NG, tl, hp * 128:(hp + 1) * 128], ident[0:NG, 0:NG])
                        cast(kt[:, (jl % 2) * 256:(jl % 2) * 256 + 256], PB[0 + (jl % 2)][:, 0:256], pool_ok=False)
                        for hp in range(2):
                            P.mm(PB[2][0:NG, jl * 32 + hp * 16:jl * 32 + (hp + 1) * 16], kt[:, (jl % 2) * 256 + hp * 128:(jl % 2) * 256 + hp * 128 + NG], Qblk[:, hp, :])
                    j0 = (7 - si) * 16
                    P.cp("dve", Zt[0:NG, :, j0:j0 + 16], PB[2][0:NG, :].rearrange("p (j h) -> p h j", h=32))
                g_ = slice(0, NG)
                for h in range(4):
                    hs = slice(h * 8, (h + 1) * 8)
                    P.act(SPt[g_, hs, :], Zt[g_, hs, :], AF.Exp, bias=PVs[g_, 4 + h:5 + h])
                P.act(SPt[g_, :, :], SPt[g_, :, :], AF.Ln, bias=1.0)
                fl = lambda v: v.rearrange("p h j -> p (h j)")
                P.scan(fl(INC[g_, :, :]), fl(m0s[g_, :, :]), fl(SPt[g_, :, :]), 0.0)
                P.cp("dve", sm["rs"][g_, :], INC[g_, :, 127])
                P.mm(PB[4][g_, 0:32], C["c_tri"][g_, g_], sm["rs"][g_, :])
                P.tt("dve", sm["cg"][g_, :], sm["cn"][g_, :], PB[4][g_, 0:32], ALU.add)
                for h in range(4):
                    P.ts("dve", sm["cgb"][g_, h * 8:(h + 1) * 8], sm["cg"][g_, h * 8:(h + 1) * 8], -1.0, ALU.mult, PVs[g_, 4 + h:5 + h], ALU.add)
                P.tt("pool", INC[g_, :, :], Zt[g_, :, :], INC[g_, :, :], ALU.subtract)
                for hq in range(32):
                    P.act(Zt[g_, hq, :], INC[g_, hq, :], AF.Exp, bias=sm["cgb"][g_, hq:hq + 1])
                P.dma("sp", vnew[:, :], o_sbvs[l, bs, :])
                first = True
                for si in range(7, -1, -1):
                    vb = kvr.next(); ix = idxr.next()
                    P.ts("dve", ix[0:NG, :], idxf[0:NG, :], float(si), ALU.add)
                    P.op("pool", lambda e, vb=vb, ix=ix: e.indirect_dma_start(
                        out=vb.t[0:NG, :, :].rearrange("p t c -> p (t c)"), out_offset=None,
                        in_=cvv.ap,
                        in_offset=bass.IndirectOffsetOnAxis(ap=ix.t[0:NG, :], axis=0)), [ix, cvv.buf], [vb], dma=True)
                    for tl in range(15, -1, -1):
                        j = (7 - si) * 16 + (15 - tl)
                        for p in range(2):
                            P.mm(PB[5 + p][:, 0:32], vb[0:NG, tl, p * 128:(p + 1) * 128], Zt[0:NG, :, j], start=first, stop=False)
                        first = False
                for p in range(2):
                    P.mm(PB[5 + p][:, 0:32], vnew[0:ST, p * 128:(p + 1) * 128], sm["attn"][0:ST, :], start=False, stop=True)
                    for par in range(2):
                        h = 2 * p + par
                        P.cp("dve", osb[par * 64:(par + 1) * 64, p, :], PB[5 + p][par * 64:(par + 1) * 64, h * 8:(h + 1) * 8])
                for h in range(4):
                    P.dma("sp", OBS[h][0:64, bs], osb[(h % 2) * 64:(h % 2) * 64 + 64, h // 2, :])
            phase_end(ph)

        DBG = "dbg" in STAGES
        if DBG:
            DBGC[0] = {"br": OUT("o_dbg_br", [8, 128, PIECE], BF16), "mg": OUT("o_dbg_mg", [128, 8 * PIECE], BF16)}
            d_ob = OUT("o_dbg_ob", [4, NPC, 384, PIECE]); d_x0 = OUT("o_dbg_x0", [SEQ, DM]); d_obs = OUT("o_dbg_obs", [4, 384, STOK])
        for l in range(L):
            phase_A(l)
            if "B" in STAGES:
                phase_B(l)
            if "Bs" in STAGES:
                phase_Bs(l)
            if DBG and l == 0:
                for q in range(4):
                    for G in range(NPC):
                        P.dma("sp", d_ob[q, G], OB[q][G][:, :])
                    P.dma("sp", d_obs[q], OBS[q][:, :])
            if "C" in STAGES:
                phase_C(l)
            if DBG and l == 0:
                for G in range(NPC):
                    P.dma("sp", d_x0[G * PIECE:(G + 1) * PIECE, :], XD[G][:, :])
        for G in range(NPC):
            P.dma("sp" if G % 2 else "act", o_y[G * PIECE:(G + 1) * PIECE, :], (XD[G][:, :] if "C" in STAGES else xp[G * PIECE:(G + 1) * PIECE, :]))
        P.dma("sp", o_ys[:, :], XS[:, :])
        counts = P.emit()
    return nc, counts


_CACHE = {}
ALL_STAGES = ("B", "Bs", "C", "sb", "rwkv", "s5")


def kernel(**inp):
    f32 = np.float32
    SEQ = inp["x_prompt"].shape[1]
    NPAGE = inp["page_table"].shape[1]
    NPOOL = inp["cache_sb_k"].shape[1]
    stages = tuple(os.environ.get("MK_STAGES", ",".join(ALL_STAGES)).split(","))
    key = (SEQ, NPAGE, NPOOL, stages)
    if key not in _CACHE:
        _CACHE[key] = build_program(SEQ, NPAGE, NPOOL, stages)
    nc, counts = _CACHE[key]
    A = lambda n: np.ascontiguousarray(np.asarray(inp[n], f32))
    shared = {"xp": A("x_prompt").reshape(SEQ, DM), "mem_prompt": A("mem_prompt").reshape(256, DM)}
    shared.update(host_consts())
    for n, s in WEIGHT_SHAPES.items():
        shared[n] = A(n).reshape(s)
    shared["ck"] = A("cache_sb_k").reshape(L * NPOOL * 8, 4096)
    shared["cv"] = A("cache_sb_v").reshape(L * NPOOL * 8, 4096)
    xs = A("x_sample").reshape(32 * ST, DM)
    in_maps = []
    for c in range(NCORE):
        b0, b1 = c * SPC, (c + 1) * SPC
        m = dict(shared)
        m["xs"] = np.ascontiguousarray(xs[c * STOK:(c + 1) * STOK])
        m["st_pool"] = np.ascontiguousarray(A("state_pool")[:, b0:b1])
        m["st_shift"] = np.ascontiguousarray(A("state_rwkv_shift")[:, b0:b1])
        m["st_wkv"] = np.ascontiguousarray(A("state_rwkv_wkv")[:, b0:b1])
        m["st_s5r"] = np.ascontiguousarray(A("state_s5_re")[:, b0:b1].reshape(L, SPC, 1024))
        m["st_s5i"] = np.ascontiguousarray(A("state_s5_im")[:, b0:b1].reshape(L, SPC, 1024))
        m["cmk"] = np.ascontiguousarray(A("cache_mem_k")[:, b0:b1].reshape(L, SPC, 256, 256))
        m["cmv"] = np.ascontiguousarray(A("cache_mem_v")[:, b0:b1].reshape(L, SPC, 256, 256))
        m["ptab"] = np.ascontiguousarray(np.asarray(inp["page_table"], np.int32)[b0:b1])
        in_maps.append(m)
    _r = run_bass_kernel_spmd(nc, in_maps, core_ids=list(range(NCORE)))
    if os.environ.get("MK_VERBOSE"):
        print("counts", counts, "exec_time_ns", _r.exec_time_ns, flush=True)
    res = _r.results
    cat = lambda k, ax: np.concatenate([r[k] for r in res], axis=ax)
    r0 = res[0]
    global _DBG
    _DBG = {k: [r[k] for r in res] for k in r0 if k.startswith("o_dbg")}
    return (r0["o_y"].reshape(1, SEQ, DM), cat("o_ys", 0).reshape(32, ST, DM),
            r0["o_sbk"].reshape(L, 1, SEQ, 4, 64), r0["o_sbv"].reshape(L, 1, SEQ, 4, 64),
            r0["o_mk"].reshape(L, 1, 256, 4, 64), r0["o_mv"].reshape(L, 1, 256, 4, 64),
            r0["o_pool"].reshape(L, 1, 15, 256), r0["o_shift"].reshape(L, 1, 1024),
            r0["o_wkv"].reshape(L, 1, 4, 64, 64), r0["o_s5r"].reshape(L, 1, 16, 64), r0["o_s5i"].reshape(L, 1, 16, 64),
            cat("o_sbks", 1).reshape(L, 32, ST, 4, 64), cat("o_sbvs", 1).reshape(L, 32, ST, 4, 64),
            cat("o_pools", 1), cat("o_shifts", 1), cat("o_wkvs", 1),
            cat("o_s5rs", 1).reshape(L, 32, 16, 64), cat("o_s5is", 1).reshape(L, 32, 16, 64))
```

```python
import os
import numpy as np
from contextlib import ExitStack
import concourse.bass as bass
import concourse.mybir as mybir
from concourse.bass_utils import run_bass_kernel_spmd

F32 = mybir.dt.float32
BF16 = mybir.dt.bfloat16
I32 = mybir.dt.int32
AF = mybir.ActivationFunctionType
ALU = mybir.AluOpType
AX = mybir.AxisListType


class View:
    __slots__ = ("buf", "ap")

    def __init__(self, buf, ap):
        self.buf = buf
        self.ap = ap

    def __getitem__(self, k):
        return View(self.buf, self.ap[k])

    def rearrange(self, s, **kw):
        return View(self.buf, self.ap.rearrange(s, **kw))

    def pbc(self, n):
        return View(self.buf, self.ap.partition_broadcast(n))

    def bc(self, shape):
        return View(self.buf, self.ap.to_broadcast(shape))


class Buf:
    __slots__ = ("name", "t", "lw", "rd")

    def __init__(self, name, t=None):
        self.name = name
        self.t = t
        self.lw = None
        self.rd = []

    def __getitem__(self, k):
        return View(self, self.t[k])


class Op:
    __slots__ = ("eng", "fn", "deps", "dma", "sem", "val", "need", "waits")


def _ap(x):
    return x.ap if isinstance(x, View) else x


class Prog:
    ENG = ("pe", "act", "dve", "pool", "sp")
    R = 8

    def __init__(self, nc, es):
        self.nc = nc
        self.es = es
        self.ges = es
        self.ops = {e: [] for e in self.ENG}
        self.ndma = {e: 0 for e in self.ENG}
        self.dma_last = {}
        self.csem = {e: es.enter_context(nc.semaphore("c_" + e)) for e in ("pe", "act", "dve", "pool")}
        self.dsem = {e: [es.enter_context(nc.semaphore("d_%s%d" % (e, i))) for i in range(self.R)]
                     for e in ("sp", "act", "pool")}
        self.nbuf = 0
        self.last = {e: None for e in self.ENG}

    def sb(self, shape, dt=F32, name=None):
        self.nbuf += 1
        name = (name or "sb") + "_%d" % self.nbuf
        t = self.es.enter_context(self.nc.sbuf_tensor(name, list(shape), dt))
        return Buf(name, t)

    def ps(self, shape, dt=F32, name=None):
        self.nbuf += 1
        name = (name or "ps") + "_%d" % self.nbuf
        t = self.es.enter_context(self.nc.psum_tensor(name, list(shape), dt))
        return Buf(name, t)

    def dram(self, name, shape, dt=F32, kind="Internal"):
        t = self.nc.dram_tensor(name, list(shape), dt, kind=kind)
        return Buf(name, t.ap())

    def op(self, eng, fn, reads=(), writes=(), dma=False, extra=()):
        o = Op()
        o.eng, o.fn, o.dma = eng, fn, dma
        o.need = dma
        o.sem = None
        o.val = 0
        deps = [(d, True) for d in extra]
        for b in reads:
            if b.lw is not None:
                deps.append((b.lw, True))
        for b in writes:
            if b.lw is not None:
                deps.append((b.lw, False))
            for r in b.rd:
                deps.append((r, False))
        if dma:
            n = self.ndma[eng]
            self.ndma[eng] = n + 1
            key = (eng, n % self.R)
            prev = self.dma_last.get(key)
            if prev is not None:
                deps.append((prev, True))
            self.dma_last[key] = o
            o.sem = self.dsem[eng][n % self.R]
            o.val = 16 * (n // self.R + 1)
        o.deps = []
        for (d, raw) in deps:
            if d is o:
                continue
            if (not d.dma) and d.eng == eng:
                if eng == "pe" or not raw:
                    continue
            d.need = True
            o.deps.append(d)
        for b in reads:
            b.rd.append(o)
        for b in writes:
            b.lw = o
            b.rd = []
        self.ops[eng].append(o)
        if not dma:
            self.last[eng] = o
        return o

    def _rw(self, outs, ins):
        return [x.buf for x in ins if isinstance(x, View)], [x.buf for x in outs if isinstance(x, View)]

    def dma(self, eng, out, in_, **kw):
        kw.setdefault("allow_slow_non_contiguous", True)
        r, w = self._rw([out], [in_])
        return self.op(eng, lambda e: e.dma_start(out=out.ap, in_=in_.ap, **kw), r, w, dma=True)

    def mm(self, out, lhsT, rhs, start=True, stop=True):
        r, w = self._rw([out], [lhsT, rhs])
        return self.op("pe", lambda e: e.matmul(out.ap, lhsT=lhsT.ap, rhs=rhs.ap, start=start, stop=stop), r, w)

    def tr(self, out, in_, ident):
        r, w = self._rw([out], [in_, ident])
        return self.op("pe", lambda e: e.transpose(out.ap, in_.ap, ident.ap), r, w)

    def act(self, out, in_, func, bias=0.0, scale=1.0, accum=None):
        r, w = self._rw([out, accum], [in_, bias, scale])
        kw = {}
        if accum is not None:
            kw["accum_out"] = accum.ap
        return self.op("act", lambda e: e.activation(out=out.ap, in_=in_.ap, func=func, bias=_ap(bias), scale=_ap(scale), **kw), r, w)

    def tt(self, eng, out, a, b, op):
        r, w = self._rw([out], [a, b])
        return self.op(eng, lambda e: e.tensor_tensor(out=out.ap, in0=a.ap, in1=b.ap, op=op), r, w)

    def ts(self, eng, out, a, s1, op0, s2=None, op1=None):
        r, w = self._rw([out], [a, s1, s2])
        if op1 is None:
            return self.op(eng, lambda e: e.tensor_scalar(out=out.ap, in0=a.ap, scalar1=_ap(s1), scalar2=None, op0=op0), r, w)
        return self.op(eng, lambda e: e.tensor_scalar(out=out.ap, in0=a.ap, scalar1=_ap(s1), scalar2=_ap(s2), op0=op0, op1=op1), r, w)

    def stt(self, eng, out, a, s, b, op0, op1):
        r, w = self._rw([out], [a, s, b])
        return self.op(eng, lambda e: e.scalar_tensor_tensor(out=out.ap, in0=a.ap, scalar=_ap(s), in1=b.ap, op0=op0, op1=op1), r, w)

    def cp(self, eng, out, in_):
        r, w = self._rw([out], [in_])
        if eng == "act":
            return self.op("act", lambda e: e.copy(out=out.ap, in_=in_.ap), r, w)
        return self.op(eng, lambda e: e.tensor_copy(out=out.ap, in_=in_.ap), r, w)

    def memset(self, eng, out, val):
        r, w = self._rw([out], [])
        return self.op(eng, lambda e: e.memset(out.ap, val), r, w)

    def recip(self, out, in_):
        r, w = self._rw([out], [in_])
        return self.op("dve", lambda e: e.reciprocal(out=out.ap, in_=in_.ap), r, w)

    def scan(self, out, d0, d1, init):
        r, w = self._rw([out], [d0, d1, init])
        return self.op("dve", lambda e: e.tensor_tensor_scan(out=out.ap, data0=d0.ap, data1=d1.ap, initial=_ap(init), op0=ALU.mult, op1=ALU.add), r, w)

    def barrier(self, tiles):
        prev = [o for o in self.last.values() if o is not None] + list(self.dma_last.values())
        a = []
        a.append(self.op("pool", lambda e: e.memset(tiles[0].t[0:1, 0:1], 0.0), [], [tiles[0]], extra=prev))
        for i, eng in enumerate(("dve", "act", "pe", "sp")):
            b = tiles[i + 1]
            if eng == "dve":
                o = self.op(eng, lambda e, b=b: e.memset(b.t[0:1, 0:1], 0.0), [tiles[0]], [b])
            elif eng == "act":
                o = self.op(eng, lambda e, b=b: e.copy(out=b.t[0:1, 0:1], in_=tiles[0].t[0:1, 0:1]), [tiles[0]], [b])
            elif eng == "pe":
                o = self.op(eng, lambda e, b=b: e.matmul(tiles[5].t[0:1, 0:1], lhsT=tiles[0].t[0:1, 0:1], rhs=tiles[0].t[0:1, 0:1], start=True, stop=True), [tiles[0]], [tiles[5]])
            else:
                o = self.op(eng, lambda e, b=b: e.dma_start(out=b.t[0:1, 0:1], in_=tiles[0].t[0:1, 0:1]), [tiles[0]], [b], dma=True)
            a.append(o)

    def emit(self):
        nc = self.nc
        for e in ("pe", "act", "dve", "pool"):
            c = 0
            for o in self.ops[e]:
                if not o.dma and o.need:
                    c += 1
                    o.sem = self.csem[e]
                    o.val = c
            assert c < 2 ** 30, (e, c)
        finals = [(o.sem, o.val) for o in self.dma_last.values()]
        for e in self.ENG:
            seen = {}
            for o in self.ops[e]:
                w = {}
                for d in o.deps:
                    k = d.sem
                    if seen.get(k.num, 0) >= d.val:
                        continue
                    if k.num not in w or w[k.num][1] < d.val:
                        w[k.num] = (k, d.val)
                for kn, (k, v) in w.items():
                    seen[kn] = v
                o.waits = list(w.values())
        with nc.Block() as block:
            def body(ename):
                def f(eng):
                    for o in self.ops[ename]:
                        for (s, v) in o.waits:
                            eng.wait_ge(s, v)
                        ins = o.fn(eng)
                        if o.dma:
                            ins.then_inc(o.sem, 16)
                        elif o.need:
                            ins.then_inc(o.sem, 1)
                    if ename == "sp":
                        for (s, v) in finals:
                            eng.wait_ge(s, v)
                return f
            block.tensor(body("pe"))
            block.scalar(body("act"))
            block.vector(body("dve"))
            block.gpsimd(body("pool"))
            block.sync(body("sp"))
        return {e: len(self.ops[e]) for e in self.ENG}

L = 2
DM = 1024
NCORE = 8
SPC = 4
ST = 8
STOK = SPC * ST
N_IN = 6400
OFF_POOL, OFF_RWKV, OFF_SB, OFF_S5, OFF_GATE = 0, 256, 1280, 2048, 2304
DFF = 2816
EPS = 1e-6
PIECE = 512
SUB = 256
CH = 64
WINS = (2, 4, 8, 16)
DECAY_C = -0.6065306597126334
TWO_PI = 6.283185307179586


class Ring:
    def __init__(self, P, shape, dt, n, name):
        self.b = [P.sb(shape, dt, "%s%d" % (name, i)) for i in range(n)]
        self.i = 0

    def next(self):
        b = self.b[self.i % len(self.b)]
        self.i += 1
        return b


def host_consts():
    f = np.float32
    c = {}
    c["c_ident"] = np.eye(128, dtype=f)
    o64 = np.zeros((128, 128), f); o64[:64, :64] = 1; o64[64:, 64:] = 1
    c["c_ones64"] = o64
    i = np.arange(128)
    c["c_tri"] = (i[:, None] > i[None, :]).astype(f)
    j = np.arange(64)
    su = (j[:, None] < j[None, :]).astype(f); iu = (j[:, None] <= j[None, :]).astype(f)
    c["c_msu"] = np.ascontiguousarray(np.broadcast_to(su[:, None, :], (64, 8, 64)))
    c["c_miu"] = np.ascontiguousarray(np.broadcast_to(iu[:, None, :], (64, 8, 64)))
    c["c_msl"] = np.ascontiguousarray(np.broadcast_to(su.T[:, None, :], (64, 8, 64)))
    c["c_id8"] = np.ascontiguousarray(np.broadcast_to(np.eye(64, dtype=f)[:, None, :], (64, 8, 64)))
    p = np.arange(128)[:, None, None]; d = np.arange(4)[None, :, None]; q = np.arange(512)[None, None, :]
    c["c_sbmask"] = ((128 * d + p) < q).astype(f)
    c["c_iota"] = np.ascontiguousarray(np.broadcast_to(np.arange(1, SUB + 1, dtype=f)[None, :], (128, SUB)))
    row = np.arange(64)
    selw = np.zeros((64, 4), f); rc = np.zeros((64, 4, 16), f)
    for g, w in enumerate(WINS):
        sel = (row // 16 == g).astype(f)
        selw[:, g] = sel / w
        rc[:, g, :] = sel[:, None] / np.minimum(np.arange(16) + 1, w)[None, :]
    c["c_selw"] = selw; c["c_rc16"] = rc
    s = np.arange(8)[:, None]; hq = np.arange(32)[None, :]
    c["c_newmask"] = (s < (hq % 8)).astype(f)
    return c

CONST_SHAPES = {"c_ident": [128, 128], "c_ones64": [128, 128], "c_tri": [128, 128], "c_msu": [64, 8, 64],
                "c_miu": [64, 8, 64], "c_msl": [64, 8, 64], "c_id8": [64, 8, 64], "c_sbmask": [128, 4, 512],
                "c_iota": [128, SUB], "c_selw": [64, 4], "c_rc16": [64, 4, 16], "c_newmask": [8, 32]}

WEIGHT_SHAPES = {
    "norm_mix": [L, DM], "norm_cross": [L, DM], "norm_mem": [L, DM], "norm_ffn": [L, DM],
    "w_in": [L, DM, N_IN], "pool_w": [L, 4, 64, 64], "pool_scale": [L, 256], "rwkv_mu": [L, 1024],
    "rwkv_w0": [L, 256], "rwkv_w_up": [L, 64, 256], "rwkv_a0": [L, 256], "rwkv_a_up": [L, 64, 256],
    "rwkv_g_up": [L, 128, 256], "rwkv_k_k": [L, 256], "rwkv_k_a": [L, 256], "rwkv_r_k": [L, 256],
    "rwkv_ln_w": [L, 256], "rwkv_ln_b": [L, 256], "sb_q_norm": [L, 64], "sb_k_norm": [L, 64], "sb_bias": [L, 4],
    "s5_a_re": [L, 1024], "s5_a_im": [L, 1024], "s5_log_dt": [L, 16], "s5_b_re": [L, 1024, 16], "s5_b_im": [L, 1024, 16],
    "s5_c_re": [L, 16, 16, 64], "s5_c_im": [L, 16, 16, 64], "s5_d": [L, 256], "s5_w_glu": [L, 256, 512],
    "w_branch": [L, 4, 256, DM], "w_out": [L, DM, DM], "xa_w_q": [L, DM, 256], "xa_w_k": [L, DM, 256],
    "xa_w_v": [L, DM, 256], "xa_q_norm": [L, 64], "xa_k_norm": [L, 64], "xa_w_o": [L, 256, DM],
    "ffn_w_gate": [L, DM, DFF], "ffn_w_up": [L, DM, DFF], "ffn_w_down": [L, DFF, DM],
}


def build_program(SEQ, NPAGE, NPOOL, STAGES):
    NPC = SEQ // PIECE
    nc = bass.Bass("TRN2", target_bir_lowering=False)
    ges = ExitStack()
    with ges:
        P = Prog(nc, ges)
        D = {}
        DBGC = [None, False]

        def IN(n, s, dt=F32):
            D[n] = P.dram(n, s, dt, kind="ExternalInput")
            return D[n]

        def OUT(n, s, dt=F32):
            D[n] = P.dram(n, s, dt, kind="ExternalOutput")
            return D[n]

        xp = IN("xp", [SEQ, DM]); xs = IN("xs", [STOK, DM])
        for n, s in CONST_SHAPES.items():
            IN(n, s)
        for n, s in WEIGHT_SHAPES.items():
            IN(n, s)
        IN("mem_prompt", [256, DM])
        IN("st_pool", [L, SPC, 15, 256]); IN("st_shift", [L, SPC, 1024]); IN("st_wkv", [L, SPC, 4, 64, 64])
        IN("st_s5r", [L, SPC, 1024]); IN("st_s5i", [L, SPC, 1024])
        IN("cmk", [L, SPC, 256, 256]); IN("cmv", [L, SPC, 256, 256])
        IN("ptab", [SPC, NPAGE], I32)
        IN("ck", [L * NPOOL * 8, 4096]); IN("cv", [L * NPOOL * 8, 4096])
        o_y = OUT("o_y", [SEQ, DM]); o_ys = OUT("o_ys", [STOK, DM])
        o_sbk = OUT("o_sbk", [L, SEQ, 256]); o_sbv = OUT("o_sbv", [L, SEQ, 256])
        o_mk = OUT("o_mk", [L, 256, 256]); o_mv = OUT("o_mv", [L, 256, 256])
        o_pool = OUT("o_pool", [L, 15, 256]); o_shift = OUT("o_shift", [L, 1, 1024])
        o_wkv = OUT("o_wkv", [L, 4, 64, 64]); o_s5r = OUT("o_s5r", [L, 1024]); o_s5i = OUT("o_s5i", [L, 1024])
        o_sbks = OUT("o_sbks", [L, STOK, 256]); o_sbvs = OUT("o_sbvs", [L, STOK, 256])
        o_pools = OUT("o_pools", [L, SPC, 15, 256]); o_shifts = OUT("o_shifts", [L, SPC, 1024])
        o_wkvs = OUT("o_wkvs", [L, SPC, 4, 64, 64]); o_s5rs = OUT("o_s5rs", [L, SPC, 1024]); o_s5is = OUT("o_s5is", [L, SPC, 1024])

        xd_t = nc.dram_tensor("XD", [SEQ, DM], F32, kind="Internal").ap()
        XD = [Buf("XD%d" % p, xd_t[p * PIECE:(p + 1) * PIECE, :]) for p in range(NPC)]
        hd_t = nc.dram_tensor("HD", [NPC, 128, 8 * PIECE], BF16, kind="Internal").ap()
        HD = [Buf("HD%d" % p, hd_t[p]) for p in range(NPC)]
        ob_t = nc.dram_tensor("OB", [4, NPC, 384, PIECE], F32, kind="Internal").ap()
        OB = [[Buf("OB%d_%d" % (q, p), ob_t[q, p]) for p in range(NPC)] for q in range(4)]
        obs_t = nc.dram_tensor("OBS", [4, 384, STOK], F32, kind="Internal").ap()
        OBS = [Buf("OBS%d" % q, obs_t[q]) for q in range(4)]
        W16 = {}
        for n in ("w_in", "w_branch", "w_out", "xa_w_q", "xa_w_k", "xa_w_v", "xa_w_o", "ffn_w_down", "s5_w_glu"):
            W16[n] = P.dram(n + "16", WEIGHT_SHAPES[n], BF16)
        W16["gate_t"] = P.dram("gate_t16", [L, 32, 128, 1024], BF16)
        W16["ffg_t"] = P.dram("ffg_t16", [L, 22, 128, 1024], BF16)
        W16["ffu_t"] = P.dram("ffu_t16", [L, 22, 128, 1024], BF16)

        PB = [P.ps([128, 512], F32, "PB%d" % i) for i in range(8)]
        bar = [P.sb([1, 8], F32, "bar%d" % i) for i in range(5)] + [PB[7]]
        C = {}
        for n, s in CONST_SHAPES.items():
            C[n] = P.sb(s, F32, n)
            P.dma("sp", C[n][:], D[n][:])
        ident = C["c_ident"]
        trib = P.sb([128, 128], BF16, "trib"); onesb = P.sb([128, 128], BF16, "onesb")
        P.cp("dve", trib[:], C["c_tri"][:])
        P.memset("pool", onesb[:], 1.0)
        m0 = P.sb([64, 512], F32, "m0")
        P.memset("pool", m0[:], 1.0)
        P.memset("pool", m0[:].rearrange("p (c j) -> p c j", j=64)[:, :, 0:1], 0.0)
        XS = P.sb([STOK, DM], F32, "XS")
        P.dma("sp", XS[:], xs[:])
        hTs = P.sb([128, 8, STOK], BF16, "hTs")
        cast_i = [0]

        def cast(out, in_, pool_ok=True):
            e = ("dve", "act", "pool")[cast_i[0] % 3] if pool_ok else ("dve", "act")[cast_i[0] % 2]
            cast_i[0] += 1
            P.cp(e, out, in_)

        def col(dst, vec):
            P.dma("sp", dst, vec.rearrange("(p o) -> p o", o=1))

        def phase_begin():
            ph = ExitStack()
            P.es = ph
            return ph

        def phase_end(ph):
            P.barrier(bar)
            ph.close()
            P.es = ges

        ph = phase_begin()
        stg = Ring(P, [128, 2048], F32, 3, "stg"); stg16 = Ring(P, [128, 2048], BF16, 3, "stg16")

        def precast(dst, src, R, Cc):
            for r0 in range(0, R, 128):
                rr = min(128, R - r0)
                for c0 in range(0, Cc, 2048):
                    cw = min(2048, Cc - c0)
                    a = stg.next(); b = stg16.next()
                    P.dma("sp", a[0:rr, 0:cw], src[r0:r0 + rr, c0:c0 + cw])
                    cast(b[0:rr, 0:cw], a[0:rr, 0:cw])
                    P.dma("act", dst[r0:r0 + rr, c0:c0 + cw], b[0:rr, 0:cw])

        def precast_tiled(dst_l, src_l, c_base, ntile):
            for t in range(ntile):
                a = stg.next(); b = stg16.next()
                P.dma("sp", a[:, 0:1024].rearrange("p (k c) -> p k c", k=8),
                      src_l[:, c_base + t * 128:c_base + (t + 1) * 128].rearrange("(k p) c -> p k c", p=128))
                cast(b[:, 0:1024], a[:, 0:1024])
                P.dma("act", dst_l[t], b[:, 0:1024])

        for l in range(L):
            precast(W16["w_in"][l], D["w_in"][l], DM, OFF_GATE)
            precast_tiled(W16["gate_t"][l], D["w_in"][l], OFF_GATE, 32)
            precast_tiled(W16["ffg_t"][l], D["ffn_w_gate"][l], 0, 22)
            precast_tiled(W16["ffu_t"][l], D["ffn_w_up"][l], 0, 22)
            precast(W16["ffn_w_down"][l], D["ffn_w_down"][l], DFF, DM)
            for n_ in range(4):
                precast(W16["w_branch"][l, n_], D["w_branch"][l, n_], 256, DM)
            precast(W16["w_out"][l], D["w_out"][l], DM, DM)
            precast(W16["xa_w_q"][l], D["xa_w_q"][l], DM, 256)
            precast(W16["xa_w_k"][l], D["xa_w_k"][l], DM, 256)
            precast(W16["xa_w_v"][l], D["xa_w_v"][l], DM, 256)
            precast(W16["xa_w_o"][l], D["xa_w_o"][l], 256, DM)
            precast(W16["s5_w_glu"][l], D["s5_w_glu"][l], 256, 512)
        phase_end(ph)

        def rmsnorm_T(R, xview, nrows, gb, dst, c0):
            j = R["sqj"].next(); s = R["stat"].next(); h = R["hrow"].next()
            P.act(j[0:nrows, :], xview, AF.Square, accum=s[0:nrows, 0:1])
            P.ts("dve", s[0:nrows, 1:2], s[0:nrows, 0:1], 1.0 / DM, ALU.mult, EPS, ALU.add)
            P.act(s[0:nrows, 3:4], s[0:nrows, 1:2], AF.Sqrt)
            P.recip(s[0:nrows, 2:3], s[0:nrows, 3:4])
            P.stt("dve", h[0:nrows, :], xview, s[0:nrows, 2:3], gb[0:nrows, :], ALU.mult, ALU.mult)
            for half in range(2):
                pt = PB[6 + half]
                for q4 in range(4):
                    kc = half * 4 + q4
                    P.tr(pt[:, q4 * 128:q4 * 128 + nrows], h[0:nrows, kc * 128:(kc + 1) * 128], ident[0:nrows, 0:nrows])
                cast(dst[:, half * 4:half * 4 + 4, c0:c0 + nrows], pt[:].rearrange("p (q t) -> p q t", q=4)[:, :, 0:nrows], pool_ok=False)

        def head_norm_rows(R, dst, src, nrows, gain_bc):
            s = R["stat"].next(); j = R["sqj"].next()
            for hh in range(4):
                P.act(j[0:nrows, hh * 64:(hh + 1) * 64], src[0:nrows, hh * 64:(hh + 1) * 64], AF.Square, accum=s[0:nrows, hh:hh + 1])
            P.ts("dve", s[0:nrows, 0:4], s[0:nrows, 0:4], 1.0 / 64, ALU.mult, EPS, ALU.add)
            P.act(s[0:nrows, 4:8], s[0:nrows, 0:4], AF.Sqrt)
            P.recip(s[0:nrows, 0:4], s[0:nrows, 4:8])
            for hh in range(4):
                P.stt("dve", dst[0:nrows, hh * 64:(hh + 1) * 64], src[0:nrows, hh * 64:(hh + 1) * 64], s[0:nrows, hh:hh + 1], gain_bc[0:nrows, :], ALU.mult, ALU.mult)

        def std_rings():
            return {"sqj": Ring(P, [128, DM], F32, 2, "sqj"), "hrow": Ring(P, [128, DM], F32, 2, "hrow"),
                    "stat": Ring(P, [128, 8], F32, 4, "stat"), "ev": Ring(P, [128, 512], F32, 4, "ev")}

        def xsrc(l, p):
            return (xp[p * PIECE:(p + 1) * PIECE, :] if l == 0 else XD[p][:, :])

        def phase_A(l):
            ph = phase_begin(); R = std_rings()
            gbc = P.sb([128, DM], F32, "gbc"); P.dma("sp", gbc[:], D["norm_mix"][l:l + 1, :].pbc(128))
            knbc = P.sb([128, 64], F32, "knbc"); P.dma("sp", knbc[:], D["sb_k_norm"][l:l + 1, :].pbc(128))
            w3 = W16["w_in"][l].rearrange("(k p) c -> p k c", p=128)
            Wkv = P.sb([128, 8, 512], BF16, "Wkv"); P.dma("sp", Wkv[:], w3[:, :, 1536:2048])
            Wx = Ring(P, [128, 8, 512], BF16, 2, "Wx")
            xr = Ring(P, [128, DM], F32, 6, "xt")
            hT = Ring(P, [128, 8, PIECE], BF16, 2, "hTA")
            bk = [0]

            def kv_out(hTt, c0, nrows, ok, ov):
                pp = PB[bk[0] % 4]; bk[0] += 1
                for kc in range(8):
                    P.mm(pp[0:nrows, :], hTt[:, kc, c0:c0 + nrows], Wkv[:, kc, :], start=(kc == 0), stop=(kc == 7))
                o = R["ev"].next()
                head_norm_rows(R, o, pp, nrows, knbc)
                P.cp("act", o[0:nrows, 256:512], pp[0:nrows, 256:512])
                P.dma("sp", ok, o[0:nrows, 0:256]); P.dma("act", ov, o[0:nrows, 256:512])

            def extra_cols(hTt, c0, nrows, sink):
                for cg in range(3):
                    w = Wx.next(); P.dma("sp", w[:], w3[:, :, cg * 512:(cg + 1) * 512])
                    pp = PB[4 + cg % 2]
                    for kc in range(8):
                        P.mm(pp[0:nrows, :], hTt[:, kc, c0:c0 + nrows], w[:, kc, :], start=(kc == 0), stop=(kc == 7))
                    o = R["ev"].next(); P.cp("dve", o[0:nrows, :], pp[0:nrows, :])
                    sink(cg, o)

            def sink_p(cg, o):
                if cg == 0:
                    P.dma("sp", o_pool[l, :, :], o[113:128, 0:256])
                    P.dma("sp", o_shift[l, :, 0:256], o[127:128, 256:512])
                elif cg == 1:
                    P.dma("sp", o_shift[l, :, 256:768], o[127:128, :])
                else:
                    P.dma("sp", o_shift[l, :, 768:1024], o[127:128, 0:256])

            def sink_s(cg, o):
                for b in range(SPC):
                    r = b * 8 + 7
                    if cg == 0:
                        P.dma("sp", o_pools[l, b, 7:15, :], o[b * 8:(b + 1) * 8, 0:256])
                        P.dma("act", o_pools[l, b, 0:7, :], D["st_pool"][l, b, 8:15, :])
                        P.dma("sp", o_shifts[l, b:b + 1, 0:256], o[r:r + 1, 256:512])
                    elif cg == 1:
                        P.dma("sp", o_shifts[l, b:b + 1, 256:768], o[r:r + 1, :])
                    else:
                        P.dma("sp", o_shifts[l, b:b + 1, 768:1024], o[r:r + 1, 0:256])

            for p in range(NPC):
                h = hT.next()
                for t in range(4):
                    x = xr.next()
                    P.dma("sp" if t % 2 == 0 else "act", x[:], xsrc(l, p)[t * 128:(t + 1) * 128, :])
                    rmsnorm_T(R, x[:, :], 128, gbc, h, t * 128)
                P.dma("sp", HD[p][:, :], h[:].rearrange("p k t -> p (k t)"))
                for t in range(4):
                    r0 = p * PIECE + t * 128
                    kv_out(h, t * 128, 128, o_sbk[l, r0:r0 + 128, :], o_sbv[l, r0:r0 + 128, :])
                if p == NPC - 1:
                    extra_cols(h, 384, 128, sink_p)
            rmsnorm_T(R, XS[:, :], STOK, gbc, hTs, 0)
            kv_out(hTs, 0, STOK, o_sbks[l, :, :], o_sbvs[l, :, :])
            extra_cols(hTs, 0, STOK, sink_s)
            P.dma("sp", gbc[:], D["norm_mem"][l:l + 1, :].pbc(128))
            P.dma("sp", knbc[:], D["xa_k_norm"][l:l + 1, :].pbc(128))
            P.dma("sp", Wkv[:, :, 0:256], W16["xa_w_k"][l].rearrange("(k p) c -> p k c", p=128))
            P.dma("sp", Wkv[:, :, 256:512], W16["xa_w_v"][l].rearrange("(k p) c -> p k c", p=128))
            hm = hT.next()
            for i in range(2):
                x = xr.next(); P.dma("sp", x[:], D["mem_prompt"][i * 128:(i + 1) * 128, :])
                rmsnorm_T(R, x[:, :], 128, gbc, hm, i * 128)
                kv_out(hm, i * 128, 128, o_mk[l, i * 128:(i + 1) * 128, :], o_mv[l, i * 128:(i + 1) * 128, :])
            phase_end(ph)

        ktd_t = nc.dram_tensor("KTD", [NPC, 64, PIECE], BF16, kind="Internal").ap()
        KTD = [Buf("KTD%d" % p, ktd_t[p]) for p in range(NPC)]
        vd_t = nc.dram_tensor("VD", [NPC, 128, 4 * 64], BF16, kind="Internal").ap()
        VD = [Buf("VD%d" % p, vd_t[p]) for p in range(NPC)]
        MU_OFF = {3: None, 4: None, 5: None, 6: 768, 7: 832, 8: 896, 9: 960}

        def wcols(q):
            r = [(OFF_SB + 64 * q, 64), (OFF_SB + 256 + 64 * q, 64), (OFF_SB + 512 + 64 * q, 64),
                 (OFF_RWKV + 64 * q, 64), (OFF_RWKV + 256 + 64 * q, 64), (OFF_RWKV + 512 + 64 * q, 64),
                 (OFF_RWKV + 768, 64), (OFF_RWKV + 832, 64), (OFF_RWKV + 896, 64), (OFF_RWKV + 960, 64),
                 (OFF_S5 + 64 * q, 64)]
            return r

        def phase_B(l):
            ph = phase_begin()
            Cb = {}
            for n in ("c_msu", "c_miu", "c_msl", "c_id8", "c_sbmask", "c_iota", "c_selw", "c_rc16", "c_newmask"):
                Cb[n] = C[n]
            w3 = W16["w_in"][l].rearrange("(k p) c -> p k c", p=128)
            WB = P.sb([128, 8, 768], BF16, "WB")
            hTp = Ring(P, [128, 8, PIECE], BF16, 2, "hTp")
            PT = P.sb([64, 12, PIECE + 1], F32, "PT")
            PTs = P.sb([64, 12, SPC, ST + 1], F32, "PTs")
            OBst = P.sb([64, 6, PIECE], F32, "OBst")
            OBsts = P.sb([64, 6, STOK], F32, "OBsts")
            PV = P.sb([128, 48], F32, "PV")
            tmp = Ring(P, [128, SUB + 16], F32, 10, "tmp")
            big = Ring(P, [128, PIECE], F32, 3, "big")
            bigb = Ring(P, [128, PIECE], BF16, 1, "bigb")
            kring = Ring(P, [64, PIECE], BF16, 3, "kring"); vring = Ring(P, [128, 4, 64], BF16, 4, "vring")
            RW = {n: P.sb([64, SUB], F32, "rw_" + n) for n in
                  ("r", "k", "v", "a", "kk", "kp", "logd", "L", "gam", "At", "Bt", "Kt", "Rt", "BtT", "KtT", "VT",
                   "Tba", "Tbr", "Tka", "Tkr", "P0", "P1", "Q0", "Q1", "N0", "N1", "M0", "M1", "WT", "UT")}
            Sst = P.sb([64, 64], F32, "Sst")
            wup = P.sb([64, 64], F32, "wup"); aup = P.sb([64, 64], F32, "aup"); gup = P.sb([64, 2, 64], F32, "gup")
            S5 = {n: P.sb([128, 2, SUB], F32, "s5_" + n) for n in ("cos", "sin", "rho")}
            S5w = {n: P.sb([128, 2, 64], F32, "s5w_" + n) for n in ("ccre", "ccimn", "bpre", "bpim")}
            BBT = {n: P.sb([64, 2, 128], F32, "bbt_" + n) for n in ("re", "im")}
            s5st = P.sb([128, 2, 2], F32, "s5st")
            s5c = P.sb([128, 2, 24], F32, "s5c")
            s5b = P.sb([128, 2, 2, 16], F32, "s5b")
            poolx = P.sb([64, 15 + PIECE], F32, "poolx")
            ones64 = C["c_ones64"]

            MAGIC = 12582912.0

            def sin_of(out, x, shift, t, u):
                P.ts("dve", t, x, float(shift), ALU.add)
                P.ts("dve", u, t, 1.0 / TWO_PI, ALU.mult)
                P.ts("dve", u, u, MAGIC, ALU.add)
                P.ts("dve", u, u, -MAGIC, ALU.add)
                P.stt("dve", t, u, -TWO_PI, t, ALU.mult, ALU.add)
                P.ts("dve", t, t, -3.1415925, ALU.max)
                P.ts("dve", t, t, 3.1415925, ALU.min)
                P.act(out, t, AF.Sin)

            def setup_slice(q):
                for ct, (c0, w) in enumerate(wcols(q)):
                    P.dma("sp" if ct % 2 == 0 else "act", WB[:, :, ct * 64:(ct + 1) * 64], w3[:, :, c0:c0 + w])
                for g in range(4):
                    P.dma("sp", WB[:, :, 704 + 16 * g:704 + 16 * (g + 1)], w3[:, :, OFF_POOL + 64 * g + 16 * q:OFF_POOL + 64 * g + 16 * q + 16])
                mu = D["rwkv_mu"][l]
                for i, off in enumerate((64 * q, 256 + 64 * q, 512 + 64 * q, 768, 832, 896, 960)):
                    col(PV[0:64, i:i + 1], mu[off:off + 64])
                col(PV[0:64, 7:8], D["rwkv_w0"][l, 64 * q:64 * q + 64]); col(PV[0:64, 8:9], D["rwkv_a0"][l, 64 * q:64 * q + 64])
                P.ts("dve", PV[0:64, 7:9], PV[0:64, 7:9], -1.0, ALU.mult)
                col(PV[0:64, 9:10], D["rwkv_k_k"][l, 64 * q:64 * q + 64]); col(PV[0:64, 10:11], D["rwkv_k_a"][l, 64 * q:64 * q + 64])
                col(PV[0:64, 11:12], D["rwkv_r_k"][l, 64 * q:64 * q + 64])
                col(PV[0:64, 12:13], D["sb_q_norm"][l, :]); col(PV[0:64, 13:14], D["sb_k_norm"][l, :])
                P.ts("dve", PV[0:64, 12:13], PV[0:64, 12:13], 0.125, ALU.mult)
                P.dma("sp", PV[:, 14:15], D["sb_bias"][l:l + 1, q:q + 1].pbc(128))
                col(PV[0:64, 15:16], D["s5_d"][l, 64 * q:64 * q + 64])
                P.dma("sp", wup[:], D["rwkv_w_up"][l, :, 64 * q:64 * q + 64])
                P.dma("sp", aup[:], D["rwkv_a_up"][l, :, 64 * q:64 * q + 64])
                P.dma("sp", gup[:], D["rwkv_g_up"][l, :, 64 * q:64 * q + 64].rearrange("(k p) c -> p k c", p=64))
                for t in range(2):
                    base = (4 * q + 2 * t) * 64
                    c_ = s5c[:, t, :]
                    col(c_[:, 0:1], D["s5_a_re"][l, base:base + 128]); col(c_[:, 1:2], D["s5_a_im"][l, base:base + 128])
                    for gg in range(2):
                        g = 4 * q + 2 * t + gg
                        P.dma("sp", c_[gg * 64:(gg + 1) * 64, 2:3], D["s5_log_dt"][l:l + 1, g:g + 1].pbc(64))
                    P.act(c_[:, 3:4], c_[:, 2:3], AF.Exp)
                    P.tt("dve", c_[:, 4:5], c_[:, 3:4], c_[:, 0:1], ALU.mult)
                    P.act(c_[:, 5:6], c_[:, 4:5], AF.Exp)
                    P.tt("dve", c_[:, 6:7], c_[:, 3:4], c_[:, 1:2], ALU.mult)
                    sin_of(c_[:, 9:10], c_[:, 6:7], 0.0, c_[:, 7:8], c_[:, 8:9])
                    sin_of(c_[:, 10:11], c_[:, 6:7], 0.5 * np.pi, c_[:, 7:8], c_[:, 8:9])
                    P.tt("dve", c_[:, 11:12], c_[:, 5:6], c_[:, 10:11], ALU.mult)
                    P.tt("dve", c_[:, 12:13], c_[:, 5:6], c_[:, 9:10], ALU.mult)
                    P.tt("dve", c_[:, 13:14], c_[:, 0:1], c_[:, 0:1], ALU.mult)
                    P.stt("dve", c_[:, 14:15], c_[:, 1:2], c_[:, 1:2], c_[:, 13:14], ALU.mult, ALU.add)
                    P.recip(c_[:, 15:16], c_[:, 14:15])
                    P.ts("dve", c_[:, 16:17], c_[:, 11:12], -1.0, ALU.add)
                    P.tt("dve", c_[:, 17:18], c_[:, 16:17], c_[:, 0:1], ALU.mult)
                    P.stt("dve", c_[:, 18:19], c_[:, 12:13], c_[:, 1:2], c_[:, 17:18], ALU.mult, ALU.add)
                    P.tt("dve", c_[:, 19:20], c_[:, 18:19], c_[:, 15:16], ALU.mult)
                    P.tt("dve", c_[:, 17:18], c_[:, 16:17], c_[:, 1:2], ALU.mult)
                    P.stt("dve", c_[:, 18:19], c_[:, 12:13], c_[:, 0:1], c_[:, 17:18], ALU.mult, ALU.subtract)
                    P.tt("dve", c_[:, 20:21], c_[:, 18:19], c_[:, 15:16], ALU.mult)
                    P.dma("sp", s5b[:, t, 0, :], D["s5_b_re"][l, base:base + 128, :])
                    P.dma("sp", s5b[:, t, 1, :], D["s5_b_im"][l, base:base + 128, :])
                    P.memset("pool", S5w["bpre"][:, t, :], 0.0); P.memset("pool", S5w["bpim"][:, t, :], 0.0)
                    P.memset("pool", S5w["ccre"][:, t, :], 0.0); P.memset("pool", S5w["ccimn"][:, t, :], 0.0)
                    tb = tmp.next()
                    for gg in range(2):
                        rs = slice(gg * 64, (gg + 1) * 64); cs = slice(32 * t + 16 * gg, 32 * t + 16 * gg + 16)
                        g = 4 * q + 2 * t + gg
                        P.ts("dve", tb[rs, 0:16], s5b[rs, t, 1, :], c_[rs, 20:21], ALU.mult)
                        P.stt("dve", S5w["bpre"][rs, t, cs], s5b[rs, t, 0, :], c_[rs, 19:20], tb[rs, 0:16], ALU.mult, ALU.subtract)
                        P.ts("dve", tb[rs, 16:32], s5b[rs, t, 0, :], c_[rs, 20:21], ALU.mult)
                        P.stt("dve", S5w["bpim"][rs, t, cs], s5b[rs, t, 1, :], c_[rs, 19:20], tb[rs, 16:32], ALU.mult, ALU.add)
                        P.dma("sp", S5w["ccre"][rs, t, cs], D["s5_c_re"][l, g].rearrange("c p -> p c"), allow_slow_non_contiguous=True)
                        P.dma("sp", S5w["ccimn"][rs, t, cs], D["s5_c_im"][l, g].rearrange("c p -> p c"), allow_slow_non_contiguous=True)
                    P.ts("dve", S5w["ccimn"][:, t, :], S5w["ccimn"][:, t, :], -1.0, ALU.mult)
                    for nm, src in (("re", "bpre"), ("im", "bpim")):
                        P.tr(PB[0][0:64, 0:128], S5w[src][:, t, :], ident[:, :])
                        P.cp("dve", BBT[nm][:, t, :], PB[0][0:64, 0:128])
                    a1 = tmp.next(); a2 = tmp.next()
                    P.ts("dve", a1[:, 0:SUB], Cb["c_iota"][:, :], c_[:, 6:7], ALU.mult)
                    a3 = tmp.next()
                    sin_of(S5["sin"][:, t, :], a1[:, 0:SUB], 0.0, a2[:, 0:SUB], a3[:, 0:SUB])
                    sin_of(S5["cos"][:, t, :], a1[:, 0:SUB], 0.5 * np.pi, a2[:, 0:SUB], a3[:, 0:SUB])
                    P.ts("dve", S5["rho"][:, t, :], Cb["c_iota"][:, :], 0.0, ALU.mult, c_[:, 5:6], ALU.add)

            P.memset("pool", PV[:, 16:17], -float(np.pi))

            def rwkv(cur, prev, ntok, Cn, ob_o, ob_bv, ob_g, S):
                nch = ntok // Cn
                n = slice(0, ntok)
                pm = {}
                for i, (ct, nm) in enumerate(((3, "r"), (4, "k"), (5, "v"))):
                    d = tmp.next()
                    P.tt("pool", d[0:64, n], prev(ct), cur(ct), ALU.subtract)
                    P.stt("dve", RW[nm][:, n], d[0:64, n], PV[0:64, i:i + 1], cur(ct), ALU.mult, ALU.add)
                lo = {}
                for i, ct in ((3, 6), (4, 7), (5, 8), (6, 9)):
                    d = tmp.next(); o = tmp.next()
                    P.tt("pool", d[0:64, n], prev(ct), cur(ct), ALU.subtract)
                    P.stt("dve", o[0:64, n], d[0:64, n], PV[0:64, i:i + 1], cur(ct), ALU.mult, ALU.add)
                    lo[ct] = o
                th = tmp.next()
                P.act(th[0:64, n], lo[6][0:64, n], AF.Tanh)
                P.mm(PB[0][0:64, n], wup[:, :], th[0:64, n])
                e = tmp.next()
                P.act(e[0:64, n], PB[0][0:64, n], AF.Exp, bias=PV[0:64, 7:8], scale=-1.0)
                P.ts("dve", e[0:64, n], e[0:64, n], 1.0, ALU.add)
                P.recip(e[0:64, n], e[0:64, n])
                P.ts("dve", RW["logd"][:, n], e[0:64, n], DECAY_C, ALU.mult)
                P.mm(PB[1][0:64, n], aup[:, :], lo[7][0:64, n])
                e2 = tmp.next()
                P.act(e2[0:64, n], PB[1][0:64, n], AF.Exp, bias=PV[0:64, 8:9], scale=-1.0)
                P.ts("dve", e2[0:64, n], e2[0:64, n], 1.0, ALU.add)
                P.recip(RW["a"][:, n], e2[0:64, n])
                for i, ct in enumerate((8, 9)):
                    sg = lo[ct]
                    P.act(sg[0:64, n], sg[0:64, n], AF.Exp, scale=-1.0)
                    P.ts("dve", sg[0:64, n], sg[0:64, n], 1.0, ALU.add)
                    P.recip(sg[0:64, n], sg[0:64, n])
                    P.mm(PB[0][0:64, n], gup[:, i, :], sg[0:64, n], start=(i == 0), stop=(i == 1))
                P.cp("act", ob_g, PB[0][0:64, n])
                P.ts("dve", RW["kk"][:, n], RW["k"][:, n], PV[0:64, 9:10], ALU.mult)
                sq = tmp.next()
                P.tt("pool", sq[0:64, n], RW["kk"][:, n], RW["kk"][:, n], ALU.mult)
                P.mm(PB[1][0:64, n], ones64[0:64, 0:64], sq[0:64, n])
                rn = tmp.next()
                P.ts("dve", rn[0:64, n], PB[1][0:64, n], 1e-24, ALU.max)
                P.act(rn[0:64, n], rn[0:64, n], AF.Sqrt)
                P.recip(rn[0:64, n], rn[0:64, n])
                P.tt("dve", RW["kk"][:, n], RW["kk"][:, n], rn[0:64, n], ALU.mult)
                t1 = tmp.next()
                P.ts("dve", t1[0:64, n], RW["a"][:, n], -1.0, ALU.add, PV[0:64, 10:11], ALU.mult)
                P.stt("dve", RW["kp"][:, n], t1[0:64, n], 1.0, RW["k"][:, n], ALU.add, ALU.mult)
                pr = tmp.next()
                P.stt("dve", pr[0:64, n], RW["r"][:, n], PV[0:64, 11:12], RW["kp"][:, n], ALU.mult, ALU.mult)
                P.mm(PB[0][0:64, n], ones64[0:64, 0:64], pr[0:64, n])
                P.tt("dve", ob_bv, PB[0][0:64, n], RW["v"][:, n], ALU.mult)
                P.scan(RW["L"][:, n], m0[:, n], RW["logd"][:, n], 0.0)
                gi = tmp.next(); gp = tmp.next(); lp = tmp.next()
                P.act(RW["gam"][:, n], RW["L"][:, n], AF.Exp)
                P.act(gi[0:64, n], RW["L"][:, n], AF.Exp, scale=-1.0)
                P.tt("pool", lp[0:64, n], RW["L"][:, n], RW["logd"][:, n], ALU.subtract)
                P.act(gp[0:64, n], lp[0:64, n], AF.Exp)
                P.stt("dve", RW["At"][:, n], RW["kk"][:, n], -1.0, gp[0:64, n], ALU.mult, ALU.mult)
                t2 = tmp.next()
                P.tt("pool", t2[0:64, n], RW["kk"][:, n], RW["a"][:, n], ALU.mult)
                P.tt("dve", RW["Bt"][:, n], t2[0:64, n], gi[0:64, n], ALU.mult)
                P.tt("dve", RW["Kt"][:, n], RW["kp"][:, n], gi[0:64, n], ALU.mult)
                P.tt("dve", RW["Rt"][:, n], RW["r"][:, n], RW["gam"][:, n], ALU.mult)
                for nm, src, bank in (("BtT", "Bt", 2), ("KtT", "Kt", 3), ("VT", "v", 4)):
                    for c in range(nch):
                        P.tr(PB[bank][0:Cn, c * 64:(c + 1) * 64], RW[src][:, c * Cn:(c + 1) * Cn], ident[0:64, 0:64])
                    P.cp("act" if nm == "KtT" else "dve", RW[nm][0:Cn, 0:nch * 64], PB[bank][0:Cn, 0:nch * 64])
                def v3(t, rows=Cn):
                    return t[0:rows, 0:nch * Cn].rearrange("p (c j) -> p c j", j=Cn)
                prods = (("Tba", "Bt", "At", "c_msu", 2), ("Tbr", "Bt", "Rt", "c_miu", 3), ("Tka", "Kt", "At", "c_msu", 4),
                         ("Tkr", "Kt", "Rt", "c_miu", 5), ("Q0", "At", "Bt", "c_msl", 6))
                for nm, a_, b_, mk, bank in prods:
                    for c in range(nch):
                        cs = slice(c * Cn, (c + 1) * Cn)
                        P.mm(PB[bank][0:Cn, cs], RW[a_][:, cs], RW[b_][:, cs])
                    P.tt("dve", v3(RW[nm]), v3(PB[bank]), Cb[mk][0:Cn, 0:nch, 0:Cn], ALU.mult)
                P.cp("pool", RW["P0"][0:Cn, 0:nch * Cn], RW["Tba"][0:Cn, 0:nch * Cn])
                P.tt("dve", v3(RW["N0"]), v3(RW["Tba"]), Cb["c_id8"][0:Cn, 0:nch, 0:Cn], ALU.add)
                P.tt("dve", v3(RW["M0"]), v3(RW["Q0"]), Cb["c_id8"][0:Cn, 0:nch, 0:Cn], ALU.add)
                cu = 0
                m = 1
                while (1 << m) < Cn:
                    nx = 1 - cu
                    Pc, Qc, Nc, Mc = RW["P%d" % cu], RW["Q%d" % cu], RW["N%d" % cu], RW["M%d" % cu]
                    Pn, Qn, Nn, Mn = RW["P%d" % nx], RW["Q%d" % nx], RW["N%d" % nx], RW["M%d" % nx]
                    for c in range(nch):
                        cs = slice(c * Cn, (c + 1) * Cn)
                        P.mm(PB[2][0:Cn, cs], Qc[0:Cn, cs], Pc[0:Cn, cs])
                        P.mm(PB[3][0:Cn, cs], Pc[0:Cn, cs], Qc[0:Cn, cs])
                    P.cp("dve", Pn[0:Cn, 0:nch * Cn], PB[2][0:Cn, 0:nch * Cn])
                    P.cp("act", Qn[0:Cn, 0:nch * Cn], PB[3][0:Cn, 0:nch * Cn])
                    for c in range(nch):
                        cs = slice(c * Cn, (c + 1) * Cn)
                        P.mm(PB[4][0:Cn, cs], Mc[0:Cn, cs], Pn[0:Cn, cs])
                        P.mm(PB[6][0:Cn, cs], Pn[0:Cn, cs], Mc[0:Cn, cs])
                    P.tt("dve", Nn[0:Cn, 0:nch * Cn], Nc[0:Cn, 0:nch * Cn], PB[4][0:Cn, 0:nch * Cn], ALU.add)
                    P.tt("dve", Mn[0:Cn, 0:nch * Cn], Mc[0:Cn, 0:nch * Cn], PB[6][0:Cn, 0:nch * Cn], ALU.add)
                    cu = nx
                    m += 1
                Nf = RW["N%d" % cu]
                for c in range(nch):
                    cs = slice(c * Cn, (c + 1) * Cn); c64 = slice(c * 64, (c + 1) * 64)
                    P.mm(PB[0][0:Cn, 0:64], RW["At"][:, cs], S[:, :], start=True, stop=False)
                    P.mm(PB[0][0:Cn, 0:64], RW["Tka"][0:Cn, cs], RW["VT"][0:Cn, c64], start=False, stop=True)
                    P.cp("act", RW["WT"][0:Cn, 0:64], PB[0][0:Cn, 0:64])
                    P.mm(PB[1][0:Cn, 0:64], Nf[0:Cn, cs], RW["WT"][0:Cn, 0:64])
                    P.cp("dve", RW["UT"][0:Cn, 0:64], PB[1][0:Cn, 0:64])
                    P.mm(PB[7][0:64, cs], S[:, :], RW["Rt"][:, cs], start=True, stop=False)
                    P.mm(PB[7][0:64, cs], RW["UT"][0:Cn, 0:64], RW["Tbr"][0:Cn, cs], start=False, stop=False)
                    P.mm(PB[7][0:64, cs], RW["VT"][0:Cn, c64], RW["Tkr"][0:Cn, cs], start=False, stop=True)
                    P.mm(PB[3][0:64, 0:64], RW["BtT"][0:Cn, c64], RW["UT"][0:Cn, 0:64], start=True, stop=False)
                    P.mm(PB[3][0:64, 0:64], RW["KtT"][0:Cn, c64], RW["VT"][0:Cn, c64], start=False, stop=True)
                    sn = tmp.next()
                    P.tt("dve", sn[0:64, 0:64], PB[3][0:64, 0:64], S[:, :], ALU.add)
                    P.ts("dve", S[:, :], sn[0:64, 0:64], RW["gam"][:, (c + 1) * Cn - 1:(c + 1) * Cn], ALU.mult)
                P.cp("act", ob_o, PB[7][0:64, n])

            def s5mix(cur, ntok, ob_y, st):
                n = slice(0, ntok)
                u = cur(10)
                for t in range(2):
                    P.mm(PB[0][:, n], BBT["re"][:, t, :], u)
                    P.mm(PB[1][:, n], BBT["im"][:, t, :], u)
                    cs_, sn_ = S5["cos"][:, t, n], S5["sin"][:, t, n]
                    a1 = tmp.next(); a2 = tmp.next(); zr = tmp.next(); zi = tmp.next()
                    P.tt("dve", a1[:, n], PB[0][:, n], cs_, ALU.mult)
                    P.tt("dve", a2[:, n], PB[1][:, n], sn_, ALU.mult)
                    P.tt("pool", a1[:, n], a1[:, n], a2[:, n], ALU.add)
                    a3 = tmp.next(); a4 = tmp.next()
                    P.tt("dve", a3[:, n], PB[1][:, n], cs_, ALU.mult)
                    P.tt("dve", a4[:, n], PB[0][:, n], sn_, ALU.mult)
                    P.tt("pool", a3[:, n], a3[:, n], a4[:, n], ALU.subtract)
                    P.scan(zr[:, n], S5["rho"][:, t, n], a1[:, n], st[:, t, 0:1])
                    P.scan(zi[:, n], S5["rho"][:, t, n], a3[:, n], st[:, t, 1:2])
                    sr = tmp.next(); si = tmp.next()
                    P.tt("dve", sr[:, n], zr[:, n], cs_, ALU.mult)
                    P.tt("pool", a2[:, n], zi[:, n], sn_, ALU.mult)
                    P.tt("dve", sr[:, n], sr[:, n], a2[:, n], ALU.subtract)
                    P.tt("dve", si[:, n], zi[:, n], cs_, ALU.mult)
                    P.tt("pool", a4[:, n], zr[:, n], sn_, ALU.mult)
                    P.tt("dve", si[:, n], si[:, n], a4[:, n], ALU.add)
                    P.cp("act", st[:, t, 0:1], sr[:, ntok - 1:ntok])
                    P.cp("act", st[:, t, 1:2], si[:, ntok - 1:ntok])
                    P.mm(PB[6][0:64, n], S5w["ccre"][:, t, :], sr[:, n], start=(t == 0), stop=False)
                    P.mm(PB[6][0:64, n], S5w["ccimn"][:, t, :], si[:, n], start=False, stop=(t == 1))
                P.stt("dve", ob_y, u, PV[0:64, 15:16], PB[6][0:64, n], ALU.mult, ALU.add)

            def poolmix(ext, ntok, ob_p, first):
                W = 15 + ntok
                s2 = tmp.next() if W <= SUB + 16 else big.next()
                s4 = tmp.next() if W <= SUB + 16 else big.next()
                s8 = tmp.next() if W <= SUB + 16 else big.next()
                s16 = tmp.next() if W <= SUB + 16 else big.next()
                P.tt("dve", s2[0:64, 1:W], ext[:, 1:W], ext[:, 0:W - 1], ALU.add)
                P.tt("dve", s4[0:64, 3:W], s2[0:64, 3:W], s2[0:64, 1:W - 2], ALU.add)
                P.tt("dve", s8[0:64, 7:W], s4[0:64, 7:W], s4[0:64, 3:W - 4], ALU.add)
                P.tt("dve", s16[0:64, 15:W], s8[0:64, 15:W], s8[0:64, 7:W - 8], ALU.add)
                sl = slice(15, W)
                acc = tmp.next() if W <= SUB + 16 else big.next()
                sw = Cb["c_selw"]
                P.ts("dve", acc[0:64, sl], s2[0:64, sl], sw[:, 0:1], ALU.mult)
                P.stt("dve", acc[0:64, sl], s4[0:64, sl], sw[:, 1:2], acc[0:64, sl], ALU.mult, ALU.add)
                P.stt("dve", acc[0:64, sl], s8[0:64, sl], sw[:, 2:3], acc[0:64, sl], ALU.mult, ALU.add)
                P.stt("dve", acc[0:64, sl], s16[0:64, sl], sw[:, 3:4], acc[0:64, sl], ALU.mult, ALU.add)
                if first:
                    f = slice(15, 31); rc = Cb["c_rc16"]
                    t1 = tmp.next()
                    P.tt("dve", acc[0:64, f], s2[0:64, f], rc[:, 0, :], ALU.mult)
                    for gi_, sK in ((1, s4), (2, s8), (3, s16)):
                        P.tt("dve", t1[0:64, 0:16], sK[0:64, f], rc[:, gi_, :], ALU.mult)
                        P.tt("dve", acc[0:64, f], acc[0:64, f], t1[0:64, 0:16], ALU.add)
                P.tt("dve", ob_p, acc[0:64, sl], ext[:, sl], ALU.subtract)

            Qs = P.sb([64, PIECE], BF16, "Qs"); Kn = P.sb([64, PIECE], BF16, "Kn"); vtb = P.sb([128, 4, 64], BF16, "vtb")
            sbmask = Cb["c_sbmask"]
            e1r = Ring(P, [128, PIECE], F32, 3, "e1r")
            t1r = Ring(P, [128, PIECE], F32, 4, "t1r")
            csum = P.sb([128, PIECE], F32, "csum")
            ntrib = P.sb([128, 128], BF16, "ntrib")
            ntf = P.sb([128, 128], F32, "ntf")
            P.tt("dve", ntf[:, :], C["c_tri"][:, :], ident[:, :], ALU.add)
            P.ts("dve", ntrib[:, :], ntf[:, :], -1.0, ALU.mult)
            spbr = Ring(P, [128, PIECE], BF16, 6, "spbr"); attr = Ring(P, [128, PIECE], BF16, 5, "attr")

            def sb_prompt(G):
                for ct, gcol, dst in ((0, 12, Qs), (1, 13, Kn)):
                    x = PT[:, ct, 1:PIECE + 1]
                    sq = big.next(); P.tt("pool", sq[0:64, :], x, x, ALU.mult)
                    P.mm(PB[0][0:64, :], ones64[0:64, 0:64], sq[0:64, :])
                    rs = big.next()
                    P.ts("dve", rs[0:64, :], PB[0][0:64, :], 1.0 / 64, ALU.mult, EPS, ALU.add)
                    P.act(rs[0:64, :], rs[0:64, :], AF.Sqrt)
                    P.recip(rs[0:64, :], rs[0:64, :])
                    P.stt("dve", dst[:, :], x, PV[0:64, gcol:gcol + 1], rs[0:64, :], ALU.mult, ALU.mult)
                P.dma("sp", KTD[G][:, :], Kn[:, :])
                for t in range(4):
                    P.tr(PB[6][:, t * 64:(t + 1) * 64], PT[:, 2, 1 + t * 128:1 + (t + 1) * 128], ident[0:64, 0:64])
                P.cp("dve", vtb[:].rearrange("p a d -> p (a d)"), PB[6][:, 0:256])
                P.dma("act", VD[G][:, :], vtb[:].rearrange("p a d -> p (a d)"))
                blocks = [(kg, b4) for kg in range(G, -1, -1) for b4 in range(3, -1, -1)]
                nblk = len(blocks)
                zbank = (PB[2], PB[3], PB[6], PB[7]); cbank = (PB[4], PB[1])
                kv = {}
                S = {}
                P.memset("pool", csum[:, :], 0.0)

                def ok(j):
                    return 0 <= j < nblk

                for i in range(-3, nblk + 1):
                    j = i + 3
                    if ok(j):
                        kg, b4 = blocks[j]
                        if b4 == 3:
                            kt = kring.next(); P.dma("sp", kt[:, :], KTD[kg][:, :])
                            vt = vring.next(); P.dma("act", vt[:].rearrange("p a d -> p (a d)"), VD[kg][:, :])
                            kv[kg] = (kt, vt)
                        kt, vt = kv[kg]
                        S[j] = {"vt": vt, "b4": b4, "diag": (kg == G)}
                        P.mm(zbank[j % 4][:, :], kt[:, b4 * 128:(b4 + 1) * 128], Qs[:, :], start=True, stop=True)
                    if ok(i):
                        s = S[i]
                        s["att"] = attr.next()
                        P.act(s["att"][:, :], s["t1"][:, :], AF.Exp, bias=PV[:, 14:15])
                        if s["diag"]:
                            P.tt("pool", s["att"][:, :], s["att"][:, :], sbmask[:, s["b4"], :], ALU.mult)
                    if ok(j):
                        s = S[j]
                        e1 = e1r.next(); P.act(e1[:, :], zbank[j % 4][:, :], AF.Exp, bias=PV[:, 14:15])
                        s["spb"] = spbr.next(); P.act(s["spb"][:, :], e1[:, :], AF.Ln, bias=1.0)
                        if s["diag"]:
                            P.tt("pool", s["spb"][:, :], s["spb"][:, :], sbmask[:, s["b4"], :], ALU.mult)
                    j = i + 2
                    if ok(j):
                        s = S[j]
                        P.mm(zbank[j % 4][:, :], ntrib[:, :], s["spb"][:, :], start=False, stop=True)
                        P.mm(cbank[j % 2][:, :], onesb[:, :], s["spb"][:, :])
                    j = i + 1
                    if ok(j):
                        s = S[j]
                        s["t1"] = t1r.next()
                        P.tt("dve", s["t1"][:, :], zbank[j % 4][:, :], csum[:, :], ALU.subtract)
                        P.tt("dve", csum[:, :], csum[:, :], cbank[j % 2][:, :], ALU.add)
                    j = i - 1
                    if ok(j):
                        s = S.pop(j)
                        P.mm(PB[5][0:64, :], s["vt"][:, s["b4"], :], s["att"][:, :], start=(j == 0), stop=(j == nblk - 1))
                P.cp("act", OBst[:, 0, :], PB[5][0:64, :])

            Sst_s = P.sb([64, 64], F32, "Sst_s"); s5st_s = P.sb([128, 2, 2], F32, "s5st_s")
            poolx_s = P.sb([64, 15 + ST], F32, "poolx_s")
            RW_OFF = {3: None, 4: None, 5: None, 6: 768, 7: 832, 8: 896, 9: 960}

            def run_slice(q):
                setup_slice(q)
                P.memset("pool", Sst[:], 0.0); P.memset("pool", s5st[:], 0.0)
                P.memset("pool", PT[:, :, 0:1], 0.0); P.memset("pool", poolx[:, 0:15], 0.0)
                for G in range(NPC):
                    h = hTp.next(); P.dma("sp", h[:].rearrange("p k t -> p (k t)"), HD[G][:, :])
                    if G > 0:
                        P.cp("pool", PT[:, :, 0:1], PT[:, :, PIECE:PIECE + 1])
                    for ct in range(12):
                        pb = PB[ct % 2]
                        for kc in range(8):
                            P.mm(pb[0:64, :], WB[:, kc, ct * 64:(ct + 1) * 64], h[:, kc, :], start=(kc == 0), stop=(kc == 7))
                        P.cp("act" if ct % 2 else "dve", PT[:, ct, 1:PIECE + 1], pb[0:64, :])
                    if "sb" in STAGES:
                        sb_prompt(G)
                    P.cp("pool", poolx[:, 15:15 + PIECE], PT[:, 11, 1:PIECE + 1])
                    for sub in range(PIECE // SUB):
                        o = sub * SUB
                        cur = lambda ct, o=o: PT[:, ct, 1 + o:1 + o + SUB]
                        prev = lambda ct, o=o: PT[:, ct, o:o + SUB]
                        if "rwkv" in STAGES:
                            rwkv(cur, prev, SUB, CH, OBst[:, 1, o:o + SUB], OBst[:, 2, o:o + SUB], OBst[:, 3, o:o + SUB], Sst)
                        if "s5" in STAGES:
                            s5mix(cur, SUB, OBst[:, 4, o:o + SUB], s5st)
                        poolmix(poolx[:, o:o + 15 + SUB], SUB, OBst[:, 5, o:o + SUB], first=(G == 0 and sub == 0))
                    P.cp("pool", poolx[:, 0:15], poolx[:, PIECE:PIECE + 15])
                    P.dma("sp", OB[q][G][:, :].rearrange("(k r) t -> r k t", r=64), OBst[:])
                P.dma("sp", o_wkv[l, q].rearrange("v k -> k v"), Sst[:], allow_slow_non_contiguous=True)
                for t in range(2):
                    base = (4 * q + 2 * t) * 64
                    P.dma("sp", o_s5r[l, base:base + 128].rearrange("(p o) -> p o", o=1), s5st[:, t, 0:1])
                    P.dma("sp", o_s5i[l, base:base + 128].rearrange("(p o) -> p o", o=1), s5st[:, t, 1:2])
                for ct in range(12):
                    pb = PB[ct % 2]
                    for kc in range(8):
                        P.mm(pb[0:64, 0:STOK], WB[:, kc, ct * 64:(ct + 1) * 64], hTs[:, kc, :], start=(kc == 0), stop=(kc == 7))
                    P.cp("act" if ct % 2 else "dve", PTs[:, ct, :, 1:ST + 1], pb[0:64, 0:STOK].rearrange("p (b t) -> p b t", t=ST))
                for ct, off in ((3, 64 * q), (4, 256 + 64 * q), (5, 512 + 64 * q), (6, 768), (7, 832), (8, 896), (9, 960)):
                    P.dma("sp", PTs[:, ct, :, 0:1], D["st_shift"][l, :, off:off + 64].rearrange("b (p o) -> p b o", o=1))
                for b in range(SPC):
                    cur = lambda ct, b=b: PTs[:, ct, b, 1:ST + 1]
                    prev = lambda ct, b=b: PTs[:, ct, b, 0:ST]
                    bs = slice(b * ST, (b + 1) * ST)
                    if "rwkv" in STAGES:
                        P.dma("sp", Sst_s[:], D["st_wkv"][l, b, q].rearrange("v k -> k v"), allow_slow_non_contiguous=True)
                        rwkv(cur, prev, ST, ST, OBsts[:, 1, bs], OBsts[:, 2, bs], OBsts[:, 3, bs], Sst_s)
                        P.dma("sp", o_wkvs[l, b, q].rearrange("v k -> k v"), Sst_s[:], allow_slow_non_contiguous=True)
                    if "s5" in STAGES:
                        for t in range(2):
                            base = (4 * q + 2 * t) * 64
                            col(s5st_s[:, t, 0:1], D["st_s5r"][l, b, base:base + 128])
                            col(s5st_s[:, t, 1:2], D["st_s5i"][l, b, base:base + 128])
                        s5mix(cur, ST, OBsts[:, 4, bs], s5st_s)
                        for t in range(2):
                            base = (4 * q + 2 * t) * 64
                            P.dma("sp", o_s5rs[l, b, base:base + 128].rearrange("(p o) -> p o", o=1), s5st_s[:, t, 0:1])
                            P.dma("sp", o_s5is[l, b, base:base + 128].rearrange("(p o) -> p o", o=1), s5st_s[:, t, 1:2])
                    for g in range(4):
                        c0 = 64 * g + 16 * q
                        P.dma("sp", poolx_s[16 * g:16 * g + 16, 0:15], D["st_pool"][l, b, :, c0:c0 + 16].rearrange("r c -> c r"), allow_slow_non_contiguous=True)
                    P.cp("pool", poolx_s[:, 15:15 + ST], PTs[:, 11, b, 1:ST + 1])
                    poolmix(poolx_s[:, :], ST, OBsts[:, 5, bs], first=False)
                P.dma("sp", OBS[q][:, :].rearrange("(k r) t -> r k t", r=64)[:, 1:6, :], OBsts[:, 1:6, :])

            if os.environ.get("MK_VERBOSE"):
                print("phase B sbuf remaining", nc.sbuf_bytes_remaining, flush=True)
            for q in range(4):
                run_slice(q)
            phase_end(ph)

        def phase_C(l):
            ph = phase_begin(); R = std_rings()
            gb = {}
            for nm in ("norm_cross", "norm_ffn"):
                gb[nm] = P.sb([128, DM], F32, "gb_" + nm); P.dma("sp", gb[nm][:], D[nm][l:l + 1, :].pbc(128))
            PVc = P.sb([128, 16], F32, "PVc")
            for p in range(2):
                col(PVc[:, p:p + 1], D["pool_scale"][l, p * 128:(p + 1) * 128])
                col(PVc[:, 2 + p:3 + p], D["rwkv_ln_w"][l, p * 128:(p + 1) * 128])
                col(PVc[:, 4 + p:5 + p], D["rwkv_ln_b"][l, p * 128:(p + 1) * 128])
                col(PVc[p * 64:(p + 1) * 64, 6:7], D["xa_q_norm"][l, :])
            P.ts("dve", PVc[:, 6:7], PVc[:, 6:7], 0.125, ALU.mult)
            poolW = P.sb([128, 2, 128], F32, "poolW"); P.memset("pool", poolW[:], 0.0)
            for g in range(4):
                P.dma("sp", poolW[(g % 2) * 64:(g % 2) * 64 + 64, g // 2, (g % 2) * 64:(g % 2) * 64 + 64], D["pool_w"][l, g])
            glu = P.sb([128, 2, 512], BF16, "glu"); P.dma("sp", glu[:], W16["s5_w_glu"][l].rearrange("(k p) c -> p k c", p=128))
            Wbr = P.sb([128, 4, 2, DM], BF16, "Wbr")
            for nb in range(4):
                P.dma("act", Wbr[:, nb, :, :], W16["w_branch"][l, nb].rearrange("(k p) c -> p k c", p=128))
            Wq = P.sb([128, 8, 256], BF16, "Wq"); P.dma("sp", Wq[:], W16["xa_w_q"][l].rearrange("(k p) c -> p k c", p=128))
            Wo = P.sb([128, 2, DM], BF16, "Wo"); P.dma("sp", Wo[:], W16["xa_w_o"][l].rearrange("(k p) c -> p k c", p=128))
            onespad = P.sb([128, 2, 128], BF16, "onespad"); P.memset("pool", onespad[:], 0.0)
            P.memset("pool", onespad[:, 0, 0:64], 1.0); P.memset("pool", onespad[:, 1, 64:128], 1.0)
            KmT = P.sb([128, 2, 256], BF16, "KmT"); Vpad = P.sb([128, 2, 4, 128], BF16, "Vpad")
            memf = P.sb([128, 2, 256], F32, "memf")
            xt = [P.sb([128, DM], F32, "xc%d" % i) for i in range(4)]
            hTg = P.sb([128, 8, PIECE], BF16, "hTg"); hT2 = P.sb([128, 8, PIECE], BF16, "hT2")
            inr = Ring(P, [128, PIECE], F32, 5, "inr")
            tf = Ring(P, [128, PIECE], F32, 5, "tf")
            accr = Ring(P, [128, PIECE], F32, 2, "accr")
            BR = [[P.sb([128, PIECE], BF16, "BR%d_%d" % (nb, p)) for p in range(2)] for nb in range(4)]
            merged = P.sb([128, 8, PIECE], BF16, "merged")
            actT = P.sb([128, 22, PIECE], BF16, "actT")
            wr = Ring(P, [128, 1024], BF16, 6, "wr")
            qn = [P.sb([128, PIECE], BF16, "qn%d" % p) for p in range(2)]
            xo = [P.sb([128, PIECE], BF16, "xo%d" % p) for p in range(2)]
            eb = Ring(P, [128, PIECE], BF16, 3, "eb")
            ones64 = C["c_ones64"]
            if os.environ.get("MK_VERBOSE"):
                print("phase C sbuf remaining", nc.sbuf_bytes_remaining, flush=True)

            def load_mem(kview, vview):
                P.memset("pool", Vpad[:], 0.0)
                P.dma("sp", memf[:], kview.rearrange("(t p) c -> p t c", p=128))
                for mt in range(2):
                    for hp in range(2):
                        P.tr(PB[0][:, (mt * 2 + hp) * 128:(mt * 2 + hp + 1) * 128], memf[:, mt, hp * 128:(hp + 1) * 128], ident[:, :])
                P.cp("dve", KmT[:].rearrange("p h (t m) -> p h t m", t=2), PB[0][:, :].rearrange("p (t h m) -> p h t m", t=2, h=2))
                vf = inr.next()
                P.dma("act", vf[:, 0:512].rearrange("p (t c) -> p t c", t=2), vview.rearrange("(t p) c -> p t c", p=128))
                for mt in range(2):
                    for h in range(4):
                        cast(Vpad[:, mt, h, (h % 2) * 64:(h % 2) * 64 + 64], vf[:, mt * 256 + h * 64:mt * 256 + (h + 1) * 64])

            def proc(xts, nrows, ntok, hsrc, obsrc, mem_cols, xdst):
                n = slice(0, ntok)
                def load_in(k, p):
                    t = inr.next()
                    if k == 5:
                        for gg in range(2):
                            g = 2 * p + gg
                            for qq in range(4):
                                P.dma("sp" if qq % 2 else "act", t[gg * 64 + 16 * qq:gg * 64 + 16 * qq + 16, n], obsrc(qq)[5 * 64 + 16 * g:5 * 64 + 16 * g + 16, :])
                    else:
                        for hh in range(2):
                            P.dma("sp" if hh else "act", t[hh * 64:(hh + 1) * 64, n], obsrc(2 * p + hh)[k * 64:(k + 1) * 64, :])
                    return t
                for p in range(2):
                    ip = load_in(5, p)
                    P.mm(PB[0][:, n], poolW[:, p, :], ip[:, n])
                    P.ts("dve", BR[0][p][:, n], PB[0][:, n], PVc[:, p:p + 1], ALU.mult)
                    isb = load_in(0, p)
                    cast(BR[2][p][:, n], isb[:, n])
                    io = load_in(1, p); ibv = load_in(2, p); ig = load_in(3, p)
                    P.mm(PB[1][:, n], ones64[:, :], io[:, n])
                    cen = tf.next()
                    P.stt("dve", cen[:, n], PB[1][:, n], -1.0 / 64, io[:, n], ALU.mult, ALU.add)
                    sq = tf.next(); P.tt("pool", sq[:, n], cen[:, n], cen[:, n], ALU.mult)
                    P.mm(PB[2][:, n], ones64[:, :], sq[:, n])
                    rs = tf.next()
                    P.ts("dve", rs[:, n], PB[2][:, n], 1.0 / 64, ALU.mult, 64e-5, ALU.add)
                    P.act(rs[:, n], rs[:, n], AF.Sqrt)
                    P.recip(rs[:, n], rs[:, n])
                    P.tt("dve", cen[:, n], cen[:, n], rs[:, n], ALU.mult)
                    P.ts("dve", cen[:, n], cen[:, n], PVc[:, 2 + p:3 + p], ALU.mult, PVc[:, 4 + p:5 + p], ALU.add)
                    P.tt("pool", cen[:, n], cen[:, n], ibv[:, n], ALU.add)
                    P.tt("dve", BR[1][p][:, n], cen[:, n], ig[:, n], ALU.mult)
                ge = []
                for p in range(2):
                    iy = load_in(4, p)
                    x2 = tf.next(); P.tt("pool", x2[:, n], iy[:, n], iy[:, n], ALU.mult)
                    P.ts("dve", x2[:, n], x2[:, n], 0.044715, ALU.mult, 1.0, ALU.add)
                    P.stt("dve", x2[:, n], x2[:, n], 0.7978845608028654, iy[:, n], ALU.mult, ALU.mult)
                    P.act(x2[:, n], x2[:, n], AF.Tanh)
                    g_ = eb.next()
                    P.stt("dve", x2[:, n], x2[:, n], 1.0, iy[:, n], ALU.add, ALU.mult)
                    P.ts("dve", g_[:, n], x2[:, n], 0.5, ALU.mult)
                    ge.append(g_)
                for p in range(2):
                    for kc in range(2):
                        P.mm(PB[3][:, n], glu[:, kc, (2 + p) * 128:(3 + p) * 128], ge[kc][:, n], start=(kc == 0), stop=(kc == 1))
                    for kc in range(2):
                        P.mm(PB[4][:, n], glu[:, kc, p * 128:(p + 1) * 128], ge[kc][:, n], start=(kc == 0), stop=(kc == 1))
                    sg = tf.next(); P.act(sg[:, n], PB[3][:, n], AF.Sigmoid)
                    P.stt("dve", BR[3][p][:, n], PB[4][:, n], 1.0, sg[:, n], ALU.mult, ALU.mult)
                hsrc()
                for dt in range(8):
                    acc = accr.next()
                    for nb in range(4):
                        w = wr.next(); P.dma("sp" if nb % 2 else "act", w[:, :], W16["gate_t"][l, nb * 8 + dt])
                        for kc in range(8):
                            P.mm(PB[5][:, n], w[:, kc * 128:(kc + 1) * 128], hTg[:, kc, n], start=(kc == 0), stop=(kc == 7))
                        sg = tf.next(); P.act(sg[:, n], PB[5][:, n], AF.Sigmoid)
                        for kc in range(2):
                            P.mm(PB[6][:, n], Wbr[:, nb, kc, dt * 128:(dt + 1) * 128], BR[nb][kc][:, n], start=(kc == 0), stop=(kc == 1))
                        if nb == 0:
                            P.tt("dve", acc[:, n], sg[:, n], PB[6][:, n], ALU.mult)
                        else:
                            pr = tf.next(); P.tt("dve", pr[:, n], sg[:, n], PB[6][:, n], ALU.mult)
                            if nb < 3:
                                P.tt("pool", acc[:, n], acc[:, n], pr[:, n], ALU.add)
                            else:
                                P.tt("pool", merged[:, dt, n], acc[:, n], pr[:, n], ALU.add)

                if DBGC[0] is not None and ntok == PIECE and not DBGC[1]:
                    DBGC[1] = True
                    for nb in range(4):
                        for p in range(2):
                            P.dma("sp", DBGC[0]["br"][nb * 2 + p], BR[nb][p][:, :])
                    P.dma("sp", DBGC[0]["mg"][:, :], merged[:].rearrange("p k t -> p (k t)"))

                def tok_major_add(lhs_fn, w_fn, nk):
                    for kc in range(nk):
                        w = w_fn(kc)
                        c = 0
                        for ti, nr in enumerate(nrows):
                            for ch in range(2):
                                P.mm(PB[ti * 2 + ch][0:nr, :], lhs_fn(kc)[:, c:c + nr], w[:, ch * 512:(ch + 1) * 512], start=(kc == 0), stop=(kc == nk - 1))
                            c += nr
                    for ti, nr in enumerate(nrows):
                        for ch in range(2):
                            P.tt("dve", xts[ti][0:nr, ch * 512:(ch + 1) * 512], xts[ti][0:nr, ch * 512:(ch + 1) * 512], PB[ti * 2 + ch][0:nr, :], ALU.add)

                def w_stream(name, kc):
                    w = wr.next(); P.dma("sp" if kc % 2 else "act", w[:, :], W16[name][l, kc * 128:(kc + 1) * 128, :])
                    return w
                tok_major_add(lambda kc: merged[:, kc, :], lambda kc: w_stream("w_out", kc), 8)
                c = 0
                for ti, nr in enumerate(nrows):
                    rmsnorm_T(R, xts[ti][0:nr, :], nr, gb["norm_cross"], hT2, c); c += nr
                for p in range(2):
                    for kc in range(8):
                        P.mm(PB[0][:, n], Wq[:, kc, p * 128:(p + 1) * 128], hT2[:, kc, n], start=(kc == 0), stop=(kc == 7))
                    sq = tf.next(); P.act(sq[:, n], PB[0][:, n], AF.Square)
                    P.mm(PB[1][:, n], ones64[:, :], sq[:, n])
                    rs = tf.next()
                    P.ts("dve", rs[:, n], PB[1][:, n], 1.0 / 64, ALU.mult, EPS, ALU.add)
                    P.act(rs[:, n], rs[:, n], AF.Sqrt)
                    P.recip(rs[:, n], rs[:, n])
                    P.stt("dve", qn[p][:, n], PB[0][:, n], PVc[:, 6:7], rs[:, n], ALU.mult, ALU.mult)
                for (mem_loader, cs) in mem_cols:
                    if mem_loader is not None:
                        mem_loader()
                    for p in range(2):
                        for par in range(2):
                            h = 2 * p + par
                            ps_ = slice(par * 64, (par + 1) * 64)
                            for mt in range(2):
                                P.mm(PB[2][:, cs], KmT[ps_, p, mt * 128:(mt + 1) * 128], qn[p][ps_, cs])
                                e = eb.next(); P.act(e[:, cs], PB[2][:, cs], AF.Exp)
                                fst = (par == 0 and mt == 0); lst = (par == 1 and mt == 1)
                                P.mm(PB[3 + p][:, cs], Vpad[:, mt, h, :], e[:, cs], start=fst, stop=lst)
                                P.mm(PB[5 + p][:, cs], onespad[:, par, :], e[:, cs], start=fst, stop=lst)
                for p in range(2):
                    rc = tf.next(); P.recip(rc[:, n], PB[5 + p][:, n])
                    P.tt("dve", xo[p][:, n], PB[3 + p][:, n], rc[:, n], ALU.mult)
                tok_major_add(lambda kc: xo[kc], lambda kc: Wo[:, kc, :], 2)
                c = 0
                for ti, nr in enumerate(nrows):
                    rmsnorm_T(R, xts[ti][0:nr, :], nr, gb["norm_ffn"], hT2, c); c += nr
                for ft in range(22):
                    wg = wr.next(); wu = wr.next()
                    P.dma("sp", wg[:, :], W16["ffg_t"][l, ft]); P.dma("act", wu[:, :], W16["ffu_t"][l, ft])
                    pg = PB[(ft % 2) * 2]; pu = PB[(ft % 2) * 2 + 1]
                    for kc in range(8):
                        P.mm(pg[:, n], wg[:, kc * 128:(kc + 1) * 128], hT2[:, kc, n], start=(kc == 0), stop=(kc == 7))
                    for kc in range(8):
                        P.mm(pu[:, n], wu[:, kc * 128:(kc + 1) * 128], hT2[:, kc, n], start=(kc == 0), stop=(kc == 7))
                    sg = tf.next(); P.act(sg[:, n], pg[:, n], AF.Silu)
                    P.tt("dve", actT[:, ft, n], sg[:, n], pu[:, n], ALU.mult)
                tok_major_add(lambda kc: actT[:, kc, :], lambda kc: w_stream("ffn_w_down", kc), 22)
                xdst()

            for G in range(NPC):
                for t in range(4):
                    P.dma("sp" if t % 2 else "act", xt[t][:], xsrc(l, G)[t * 128:(t + 1) * 128, :])

                def hsrc(G=G):
                    P.dma("sp", hTg[:].rearrange("p k t -> p (k t)"), HD[G][:, :])

                def xdst(G=G):
                    for t in range(4):
                        P.dma("sp" if t % 2 else "act", XD[G][t * 128:(t + 1) * 128, :], xt[t][:])
                mem = [((lambda: load_mem(o_mk[l], o_mv[l])) if G == 0 else None, slice(0, PIECE))]
                proc([x[:, :] for x in xt], [128] * 4, PIECE, hsrc, lambda qq, G=G: OB[qq][G], mem, xdst)

            def hsrc_s():
                P.cp("pool", hTg[:, :, 0:STOK], hTs[:, :, :])
            mem_s = [((lambda b=b: load_mem(D["cmk"][l, b], D["cmv"][l, b])), slice(b * ST, (b + 1) * ST)) for b in range(SPC)]
            proc([XS[:, :]], [STOK], STOK, hsrc_s, lambda qq: OBS[qq], mem_s, lambda: None)
            phase_end(ph)

        def phase_Bs(l):
            ph = phase_begin()
            NG = NPAGE
            w3 = W16["w_in"][l].rearrange("(k p) c -> p k c", p=128)
            Wsb = P.sb([128, 8, 768], BF16, "Wsb"); P.dma("sp", Wsb[:], w3[:, :, OFF_SB:OFF_SB + 768])
            onesf = P.sb([128, 128], F32, "onesf"); P.memset("pool", onesf[:], 1.0)
            ones64 = C["c_ones64"]
            PVs = P.sb([128, 16], F32, "PVs")
            for par in range(2):
                col(PVs[par * 64:(par + 1) * 64, 0:1], D["sb_q_norm"][l, :]); col(PVs[par * 64:(par + 1) * 64, 1:2], D["sb_k_norm"][l, :])
            P.ts("dve", PVs[:, 0:1], PVs[:, 0:1], 0.125, ALU.mult)
            P.dma("sp", PVs[:, 4:8], D["sb_bias"][l:l + 1, :].pbc(128))
            QKV = P.sb([128, 6, STOK], F32, "QKV")
            for i in range(6):
                for kc in range(8):
                    P.mm(PB[i % 2][:, 0:STOK], Wsb[:, kc, i * 128:(i + 1) * 128], hTs[:, kc, :], start=(kc == 0), stop=(kc == 7))
                if i < 4:
                    sq = P.sb([128, STOK], F32, "sqs%d" % i); rs = P.sb([128, STOK], F32, "rss%d" % i)
                    P.act(sq[:, :], PB[i % 2][:, 0:STOK], AF.Square)
                    P.mm(PB[2][:, 0:STOK], ones64[:, :], sq[:, :])
                    P.ts("dve", rs[:, :], PB[2][:, 0:STOK], 1.0 / 64, ALU.mult, EPS, ALU.add)
                    P.act(rs[:, :], rs[:, :], AF.Sqrt)
                    P.recip(rs[:, :], rs[:, :])
                    P.stt("dve", QKV[:, i, :], PB[i % 2][:, 0:STOK], PVs[:, (0 if i < 2 else 1):(1 if i < 2 else 2)], rs[:, :], ALU.mult, ALU.mult)
                else:
                    P.cp("dve", QKV[:, i, :], PB[i % 2][:, 0:STOK])
            Zt = P.sb([128, 32, 128], F32, "Zt"); SPt = P.sb([128, 32, 128], F32, "SPt")
            INC = P.sb([128, 32, 128], F32, "INC")
            m0s = P.sb([128, 32, 128], F32, "m0s"); P.memset("pool", m0s[:], 1.0); P.memset("pool", m0s[:, :, 0:1], 0.0)
            kvr = Ring(P, [128, 16, 256], F32, 2, "kvr")
            KT = Ring(P, [128, 512], F32, 3, "KTs")
            idx = P.sb([128, 1], I32, "idx"); idxf = P.sb([128, 1], F32, "idxf")
            idxr = Ring(P, [128, 1], I32, 4, "idxr")
            Qblk = P.sb([128, 2, 16], F32, "Qblk")
            sm = {n: P.sb([128, 32], F32, "sm_" + n) for n in ("zb", "e1", "sp", "aft", "cn", "rs", "cg", "cgb", "attn")}
            vnew = P.sb([8, 256], F32, "vnew")
            osb = P.sb([128, 2, ST], F32, "osb")
            ckv = D["ck"][:, :]; cvv = D["cv"][:, :]
            for b in range(SPC):
                bs = slice(b * ST, (b + 1) * ST)
                P.dma("sp", idx[0:NG, :], D["ptab"][b, :].rearrange("(p o) -> p o", o=1))
                P.cp("dve", idxf[0:NG, :], idx[0:NG, :])
                P.ts("dve", idxf[0:NG, :], idxf[0:NG, :], 8.0, ALU.mult, float(l * NPOOL * 8), ALU.add)
                P.memset("pool", Qblk[:], 0.0)
                for hp in range(2):
                    P.cp("dve", Qblk[0:64, hp, 0:8], QKV[0:64, hp, bs])
                    P.cp("dve", Qblk[64:128, hp, 8:16], QKV[64:128, hp, bs])
                for hp in range(2):
                    P.mm(PB[3][0:ST, hp * 16:(hp + 1) * 16], QKV[:, 2 + hp, bs], Qblk[:, hp, :])
                for h in range(4):
                    P.ts("dve", sm["zb"][0:ST, h * 8:(h + 1) * 8], PB[3][0:ST, h * 8:(h + 1) * 8], PVs[0:ST, 4 + h:5 + h], ALU.add)
                P.act(sm["e1"][0:ST, :], sm["zb"][0:ST, :], AF.Exp)
                P.act(sm["sp"][0:ST, :], sm["e1"][0:ST, :], AF.Ln, bias=1.0)
                P.tt("dve", sm["sp"][0:ST, :], sm["sp"][0:ST, :], C["c_newmask"][:, :], ALU.mult)
                P.mm(PB[4][0:ST, 0:32], C["c_tri"][0:ST, 0:ST], sm["sp"][0:ST, :])
                P.mm(PB[5][:, 0:32], onesf[0:ST, :], sm["sp"][0:ST, :])
                P.cp("dve", sm["cn"][:, :], PB[5][:, 0:32])
                P.tt("dve", sm["aft"][0:ST, :], sm["zb"][0:ST, :], sm["sp"][0:ST, :], ALU.subtract)
                P.tt("dve", sm["aft"][0:ST, :], sm["aft"][0:ST, :], PB[4][0:ST, 0:32], ALU.subtract)
                P.act(sm["attn"][0:ST, :], sm["aft"][0:ST, :], AF.Exp)
                P.tt("dve", sm["attn"][0:ST, :], sm["attn"][0:ST, :], C["c_newmask"][:, :], ALU.mult)
                for si in range(7, -1, -1):
                    kb = kvr.next(); ix = idxr.next()
                    P.ts("dve", ix[0:NG, :], idxf[0:NG, :], float(si), ALU.add)
                    P.op("pool", lambda e, kb=kb, ix=ix: e.indirect_dma_start(
                        out=kb.t[0:NG, :, :].rearrange("p t c -> p (t c)"), out_offset=None,
                        in_=ckv.ap,
                        in_offset=bass.IndirectOffsetOnAxis(ap=ix.t[0:NG, :], axis=0)), [ix, ckv.buf], [kb], dma=True)
                    for tl in range(15, -1, -1):
                        jl = 15 - tl
                        kt = KT.next() if jl % 2 == 0 else kt
                        for hp in range(2):
                            P.tr(PB[0 + (jl % 2)][:, hp * 128:hp * 128 + NG], kb[0:NG, tl, hp * 128:(hp + 1) * 128], ident[0:NG, 0:NG])
                        cast(kt[:, (jl % 2) * 256:(jl % 2) * 256 + 256], PB[0 + (jl % 2)][:, 0:256], pool_ok=False)
                        for hp in range(2):
                            P.mm(PB[2][0:NG, jl * 32 + hp * 16:jl * 32 + (hp + 1) * 16], kt[:, (jl % 2) * 256 + hp * 128:(jl % 2) * 256 + hp * 128 + NG], Qblk[:, hp, :])
                    j0 = (7 - si) * 16
                    P.cp("dve", Zt[0:NG, :, j0:j0 + 16], PB[2][0:NG, :].rearrange("p (j h) -> p h j", h=32))
                g_ = slice(0, NG)
                for h in range(4):
                    hs = slice(h * 8, (h + 1) * 8)
                    P.act(SPt[g_, hs, :], Zt[g_, hs, :], AF.Exp, bias=PVs[g_, 4 + h:5 + h])
                P.act(SPt[g_, :, :], SPt[g_, :, :], AF.Ln, bias=1.0)
                fl = lambda v: v.rearrange("p h j -> p (h j)")
                P.scan(fl(INC[g_, :, :]), fl(m0s[g_, :, :]), fl(SPt[g_, :, :]), 0.0)
                P.cp("dve", sm["rs"][g_, :], INC[g_, :, 127])
                P.mm(PB[4][g_, 0:32], C["c_tri"][g_, g_], sm["rs"][g_, :])
                P.tt("dve", sm["cg"][g_, :], sm["cn"][g_, :], PB[4][g_, 0:32], ALU.add)
                for h in range(4):
                    P.ts("dve", sm["cgb"][g_, h * 8:(h + 1) * 8], sm["cg"][g_, h * 8:(h + 1) * 8], -1.0, ALU.mult, PVs[g_, 4 + h:5 + h], ALU.add)
                P.tt("pool", INC[g_, :, :], Zt[g_, :, :], INC[g_, :, :], ALU.subtract)
                for hq in range(32):
                    P.act(Zt[g_, hq, :], INC[g_, hq, :], AF.Exp, bias=sm["cgb"][g_, hq:hq + 1])
                P.dma("sp", vnew[:, :], o_sbvs[l, bs, :])
                first = True
                for si in range(7, -1, -1):
                    vb = kvr.next(); ix = idxr.next()
                    P.ts("dve", ix[0:NG, :], idxf[0:NG, :], float(si), ALU.add)
                    P.op("pool", lambda e, vb=vb, ix=ix: e.indirect_dma_start(
                        out=vb.t[0:NG, :, :].rearrange("p t c -> p (t c)"), out_offset=None,
                        in_=cvv.ap,
                        in_offset=bass.IndirectOffsetOnAxis(ap=ix.t[0:NG, :], axis=0)), [ix, cvv.buf], [vb], dma=True)
                    for tl in range(15, -1, -1):
                        j = (7 - si) * 16 + (15 - tl)
                        for p in range(2):
                            P.mm(PB[5 + p][:, 0:32], vb[0:NG, tl, p * 128:(p + 1) * 128], Zt[0:NG, :, j], start=first, stop=False)
                        first = False
                for p in range(2):
                    P.mm(PB[5 + p][:, 0:32], vnew[0:ST, p * 128:(p + 1) * 128], sm["attn"][0:ST, :], start=False, stop=True)
                    for par in range(2):
                        h = 2 * p + par
                        P.cp("dve", osb[par * 64:(par + 1) * 64, p, :], PB[5 + p][par * 64:(par + 1) * 64, h * 8:(h + 1) * 8])
                for h in range(4):
                    P.dma("sp", OBS[h][0:64, bs], osb[(h % 2) * 64:(h % 2) * 64 + 64, h // 2, :])
            phase_end(ph)

        DBG = "dbg" in STAGES
        if DBG:
            DBGC[0] = {"br": OUT("o_dbg_br", [8, 128, PIECE], BF16), "mg": OUT("o_dbg_mg", [128, 8 * PIECE], BF16)}
            d_ob = OUT("o_dbg_ob", [4, NPC, 384, PIECE]); d_x0 = OUT("o_dbg_x0", [SEQ, DM]); d_obs = OUT("o_dbg_obs", [4, 384, STOK])
        for l in range(L):
            phase_A(l)
            if "B" in STAGES:
                phase_B(l)
            if "Bs" in STAGES:
                phase_Bs(l)
            if DBG and l == 0:
                for q in range(4):
                    for G in range(NPC):
                        P.dma("sp", d_ob[q, G], OB[q][G][:, :])
                    P.dma("sp", d_obs[q], OBS[q][:, :])
            if "C" in STAGES:
                phase_C(l)
            if DBG and l == 0:
                for G in range(NPC):
                    P.dma("sp", d_x0[G * PIECE:(G + 1) * PIECE, :], XD[G][:, :])
        for G in range(NPC):
            P.dma("sp" if G % 2 else "act", o_y[G * PIECE:(G + 1) * PIECE, :], (XD[G][:, :] if "C" in STAGES else xp[G * PIECE:(G + 1) * PIECE, :]))
        P.dma("sp", o_ys[:, :], XS[:, :])
        counts = P.emit()
    return nc, counts


_CACHE = {}
ALL_STAGES = ("B", "Bs", "C", "sb", "rwkv", "s5")


def kernel(**inp):
    f32 = np.float32
    SEQ = inp["x_prompt"].shape[1]
    NPAGE = inp["page_table"].shape[1]
    NPOOL = inp["cache_sb_k"].shape[1]
    stages = tuple(os.environ.get("MK_STAGES", ",".join(ALL_STAGES)).split(","))
    key = (SEQ, NPAGE, NPOOL, stages)
    if key not in _CACHE:
        _CACHE[key] = build_program(SEQ, NPAGE, NPOOL, stages)
    nc, counts = _CACHE[key]
    A = lambda n: np.ascontiguousarray(np.asarray(inp[n], f32))
    shared = {"xp": A("x_prompt").reshape(SEQ, DM), "mem_prompt": A("mem_prompt").reshape(256, DM)}
    shared.update(host_consts())
    for n, s in WEIGHT_SHAPES.items():
        shared[n] = A(n).reshape(s)
    shared["ck"] = A("cache_sb_k").reshape(L * NPOOL * 8, 4096)
    shared["cv"] = A("cache_sb_v").reshape(L * NPOOL * 8, 4096)
    xs = A("x_sample").reshape(32 * ST, DM)
    in_maps = []
    for c in range(NCORE):
        b0, b1 = c * SPC, (c + 1) * SPC
        m = dict(shared)
        m["xs"] = np.ascontiguousarray(xs[c * STOK:(c + 1) * STOK])
        m["st_pool"] = np.ascontiguousarray(A("state_pool")[:, b0:b1])
        m["st_shift"] = np.ascontiguousarray(A("state_rwkv_shift")[:, b0:b1])
        m["st_wkv"] = np.ascontiguousarray(A("state_rwkv_wkv")[:, b0:b1])
        m["st_s5r"] = np.ascontiguousarray(A("state_s5_re")[:, b0:b1].reshape(L, SPC, 1024))
        m["st_s5i"] = np.ascontiguousarray(A("state_s5_im")[:, b0:b1].reshape(L, SPC, 1024))
        m["cmk"] = np.ascontiguousarray(A("cache_mem_k")[:, b0:b1].reshape(L, SPC, 256, 256))
        m["cmv"] = np.ascontiguousarray(A("cache_mem_v")[:, b0:b1].reshape(L, SPC, 256, 256))
        m["ptab"] = np.ascontiguousarray(np.asarray(inp["page_table"], np.int32)[b0:b1])
        in_maps.append(m)
    _r = run_bass_kernel_spmd(nc, in_maps, core_ids=list(range(NCORE)))
    if os.environ.get("MK_VERBOSE"):
        print("counts", counts, "exec_time_ns", _r.exec_time_ns, flush=True)
    res = _r.results
    cat = lambda k, ax: np.concatenate([r[k] for r in res], axis=ax)
    r0 = res[0]
    global _DBG
    _DBG = {k: [r[k] for r in res] for k in r0 if k.startswith("o_dbg")}
    return (r0["o_y"].reshape(1, SEQ, DM), cat("o_ys", 0).reshape(32, ST, DM),
            r0["o_sbk"].reshape(L, 1, SEQ, 4, 64), r0["o_sbv"].reshape(L, 1, SEQ, 4, 64),
            r0["o_mk"].reshape(L, 1, 256, 4, 64), r0["o_mv"].reshape(L, 1, 256, 4, 64),
            r0["o_pool"].reshape(L, 1, 15, 256), r0["o_shift"].reshape(L, 1, 1024),
            r0["o_wkv"].reshape(L, 1, 4, 64, 64), r0["o_s5r"].reshape(L, 1, 16, 64), r0["o_s5i"].reshape(L, 1, 16, 64),
            cat("o_sbks", 1).reshape(L, 32, ST, 4, 64), cat("o_sbvs", 1).reshape(L, 32, ST, 4, 64),
            cat("o_pools", 1), cat("o_shifts", 1), cat("o_wkvs", 1),
            cat("o_s5rs", 1).reshape(L, 32, 16, 64), cat("o_s5is", 1).reshape(L, 32, 16, 64))
```
